# Optimizing a Trainium2 kernel written in Bass

```python
import jax, jax.numpy as jnp
from jax import lax
import numpy as np

D_MODEL = 2048
BATCH = 4
SEQ = 8192
DEPTH = 1
DEC_BATCH = 8
DEC_SEQ = 32
PAST_LEN = 1024

CHUNK = 64
Q_BLOCK = 128
EPS = 1e-6
ROPE_THETA = 10000.0
N_HEADS_A = 16
N_KV_A = 4
HEAD_DIM_A = 128
KV_GROUP = N_HEADS_A // N_KV_A
N_HEADS_IDX = 16
HEAD_DIM_IDX = 64
TOPK_MAX = 256
N_HEADS_R = 8
KEY_DIM_R = 128
VAL_DIM_R = 256
WIDTH_A = N_HEADS_A * HEAD_DIM_A
WIDTH_B = N_HEADS_R * VAL_DIM_R
D_FF = ((8 * D_MODEL + 3 * 256 - 1) // (3 * 256)) * 256
PROJ_WIDTHS = (WIDTH_A, N_KV_A * HEAD_DIM_A, N_KV_A * HEAD_DIM_A,
               N_HEADS_IDX * HEAD_DIM_IDX, HEAD_DIM_IDX, N_HEADS_IDX,
               N_HEADS_R * KEY_DIM_R, N_HEADS_R * KEY_DIM_R, WIDTH_B, WIDTH_B,
               D_MODEL, D_MODEL)

kernel_name = 'hybrid_dsa_retention_stream_step'


def rms_norm(x, g):
    xf = x.astype(jnp.float32)
    y = xf * lax.rsqrt(jnp.mean(xf * xf, axis=-1, keepdims=True) + EPS)
    return (y * g.astype(jnp.float32)).astype(x.dtype)


def rope(x, pos):
    half = x.shape[-1] // 2
    inv_freq = ROPE_THETA ** (-jnp.arange(half, dtype=jnp.float32) / half)
    ang = pos.astype(jnp.float32)[:, None] * inv_freq[None, :]
    cos = jnp.cos(ang)[None, :, None, :]
    sin = jnp.sin(ang)[None, :, None, :]
    xf = x.astype(jnp.float32)
    x1, x2 = xf[..., :half], xf[..., half:]
    return jnp.concatenate([x1 * cos - x2 * sin, x2 * cos + x1 * sin], axis=-1).astype(x.dtype)


def retention_log_decay():
    return jnp.log1p(-(2.0 ** (-5.0 - jnp.arange(N_HEADS_R, dtype=jnp.float32))))


def split_projection(h, w_in, pos):
    B, T, _ = h.shape
    points = [int(p) for p in np.cumsum(PROJ_WIDTHS)[:-1]]
    qa, ka, va, qi, ki, wi, qr, kr, vr, gr, ga, gb = jnp.split(h @ w_in, points, axis=-1)
    qa = rope(qa.reshape(B, T, N_HEADS_A, HEAD_DIM_A), pos)
    ka = rope(ka.reshape(B, T, N_KV_A, HEAD_DIM_A), pos)
    va = va.reshape(B, T, N_KV_A, HEAD_DIM_A)
    qi = rope(qi.reshape(B, T, N_HEADS_IDX, HEAD_DIM_IDX), pos)
    ki = rope(ki[:, :, None, :], pos)[:, :, 0, :]
    wi = wi * (N_HEADS_IDX ** -0.5)
    qr = rope(qr.reshape(B, T, N_HEADS_R, KEY_DIM_R), pos)
    kr = rope(kr.reshape(B, T, N_HEADS_R, KEY_DIM_R), pos) * (KEY_DIM_R ** -0.5)
    vr = vr.reshape(B, T, N_HEADS_R, VAL_DIM_R)
    return qa, ka, va, qi, ki, wi, qr, kr, vr, gr, ga, gb


def dsa_block(qa, qi, wi, q_pos, ka, va, ki, k_pos, top_k):
    B, Tq = qa.shape[:2]
    logits = jnp.einsum('bqjd,bsd->bqjs', qi.astype(jnp.float32), ki.astype(jnp.float32)) * (HEAD_DIM_IDX ** -0.5)
    score = jnp.einsum('bqjs,bqj->bqs', jax.nn.relu(logits), wi.astype(jnp.float32))
    admissible = (k_pos[None, :] // CHUNK) <= (q_pos[:, None] // CHUNK)
    score = jnp.where(admissible[None], score, -jnp.inf)
    top_val, top_idx = lax.top_k(score, top_k)
    valid = jnp.isfinite(top_val)
    k_sel = jax.vmap(lambda kb, ib: kb[ib])(ka, top_idx)
    v_sel = jax.vmap(lambda vb, ib: vb[ib])(va, top_idx)
    qg = qa.reshape(B, Tq, N_KV_A, KV_GROUP, HEAD_DIM_A).astype(jnp.float32)
    s = jnp.einsum('bqhgd,bqnhd->bqhgn', qg, k_sel.astype(jnp.float32)) * (HEAD_DIM_A ** -0.5)
    s = jnp.where(valid[:, :, None, None, :], s, -jnp.inf)
    p = jax.nn.softmax(s, axis=-1)
    o = jnp.einsum('bqhgn,bqnhd->bqhgd', p, v_sel.astype(jnp.float32))
    return o.reshape(B, Tq, WIDTH_A).astype(qa.dtype)


def dsa_prompt(qa, qi, wi, ka, va, ki, pos, top_k):
    B, T = qa.shape[:2]
    nb = T // Q_BLOCK

    def blocks(a):
        return a.reshape(B, nb, Q_BLOCK, *a.shape[2:]).swapaxes(0, 1)

    def one_block(args):
        qa_b, qi_b, wi_b, pos_b = args
        return dsa_block(qa_b, qi_b, wi_b, pos_b, ka, va, ki, pos, top_k)

    o = lax.map(one_block, (blocks(qa), blocks(qi), blocks(wi), pos.reshape(nb, Q_BLOCK)))
    return o.swapaxes(0, 1).reshape(B, T, WIDTH_A)


def retention_chunk(state, q, k, v, log_gamma):
    C = q.shape[1]
    idx = jnp.arange(C, dtype=jnp.float32)
    diff = idx[:, None] - idx[None, :]
    decay = jnp.where(diff[None] >= 0,
                      jnp.exp(jnp.maximum(diff, 0.0)[None] * log_gamma[:, None, None]), 0.0)
    inner = jnp.einsum('bihd,bjhd->bhij', q, k) * decay[None]
    o = jnp.einsum('bhij,bjhe->bihe', inner, v)
    cross_decay = jnp.exp((idx + 1.0)[:, None] * log_gamma[None, :])
    o = o + jnp.einsum('bihd,bhde->bihe', q, state) * cross_decay[None, :, :, None]
    k_decay = jnp.exp((C - 1.0 - idx)[:, None] * log_gamma[None, :])
    new_state = (state * jnp.exp(C * log_gamma)[None, :, None, None]
                 + jnp.einsum('bjhd,bjhe->bhde', k * k_decay[None, :, :, None], v))
    return o, new_state


def retention_prompt(q, k, v):
    B, T = q.shape[:2]
    nc = T // CHUNK
    log_gamma = retention_log_decay()

    def chunks(a):
        return a.astype(jnp.float32).reshape(B, nc, CHUNK, *a.shape[2:]).swapaxes(0, 1)

    def step(state, qkv):
        o, s = retention_chunk(state, qkv[0], qkv[1], qkv[2], log_gamma)
        return s, o

    state0 = jnp.zeros((B, N_HEADS_R, KEY_DIM_R, VAL_DIM_R), jnp.float32)
    final, o = lax.scan(step, state0, (chunks(q), chunks(k), chunks(v)))
    return o.swapaxes(0, 1).reshape(B, T, N_HEADS_R, VAL_DIM_R), final


def merge_and_ffn(x, o_a, o_b, gr, ga, gb, w_pa, w_pb, w_o, n2, wg, wu, wd):
    B, T, _ = x.shape
    o_b = o_b * lax.rsqrt(jnp.mean(o_b * o_b, axis=-1, keepdims=True) + EPS)
    o_b = o_b.reshape(B, T, WIDTH_B).astype(x.dtype) * jax.nn.silu(gr)
    merged = jax.nn.sigmoid(ga) * (o_a @ w_pa) + jax.nn.sigmoid(gb) * (o_b @ w_pb)
    x = x + merged @ w_o
    h2 = rms_norm(x, n2)
    return x + (jax.nn.silu(h2 @ wg) * (h2 @ wu)) @ wd


def setup_inputs(seed: int = 0) -> dict:
    key = jax.random.key(seed)
    ks = jax.random.split(key, 18)
    f32 = jnp.float32
    p_total = sum(PROJ_WIDTHS)

    def nrm(k, shape, scale):
        return jax.random.normal(k, shape, f32) * scale

    return {
        'x_prompt': nrm(ks[0], (BATCH, SEQ, D_MODEL), 1.0),
        'x_sample': nrm(ks[1], (DEC_BATCH, DEC_SEQ, D_MODEL), 1.0),
        'cache_k': nrm(ks[2], (DEPTH, DEC_BATCH, PAST_LEN, N_KV_A, HEAD_DIM_A), 1.0),
        'cache_v': nrm(ks[3], (DEPTH, DEC_BATCH, PAST_LEN, N_KV_A, HEAD_DIM_A), 1.0),
        'cache_idx_k': nrm(ks[4], (DEPTH, DEC_BATCH, PAST_LEN, HEAD_DIM_IDX), 1.0),
        'state_ret': nrm(ks[5], (DEPTH, DEC_BATCH, N_HEADS_R, KEY_DIM_R, VAL_DIM_R), 0.5),
        'norm1_g': 1.0 + nrm(ks[6], (DEPTH, D_MODEL), 0.01),
        'w_in': nrm(ks[7], (DEPTH, D_MODEL, p_total), D_MODEL ** -0.5),
        'w_pa': nrm(ks[8], (DEPTH, WIDTH_A, D_MODEL), WIDTH_A ** -0.5),
        'w_pb': nrm(ks[9], (DEPTH, WIDTH_B, D_MODEL), WIDTH_B ** -0.5),
        'w_o': nrm(ks[10], (DEPTH, D_MODEL, D_MODEL), D_MODEL ** -0.5),
        'norm2_g': 1.0 + nrm(ks[11], (DEPTH, D_MODEL), 0.01),
        'w_ffn_gate': nrm(ks[12], (DEPTH, D_MODEL, D_FF), D_MODEL ** -0.5),
        'w_ffn_up': nrm(ks[13], (DEPTH, D_MODEL, D_FF), D_MODEL ** -0.5),
        'w_ffn_down': nrm(ks[14], (DEPTH, D_FF, D_MODEL), D_FF ** -0.5),
        'norm_f_g': 1.0 + nrm(ks[15], (D_MODEL,), 0.01),
    }


def reference(x_prompt, x_sample, cache_k, cache_v, cache_idx_k, state_ret, norm1_g, w_in, w_pa, w_pb,
              w_o, norm2_g, w_ffn_gate, w_ffn_up, w_ffn_down, norm_f_g):
    t_p = x_prompt.shape[1]
    t_s = x_sample.shape[1]
    past = cache_k.shape[2]
    pos_p = jnp.arange(t_p, dtype=jnp.int32)
    pos_s = past + jnp.arange(t_s, dtype=jnp.int32)
    key_pos_s = jnp.arange(past + t_s, dtype=jnp.int32)
    top_k_p = min(TOPK_MAX, t_p // 4)
    top_k_s = min(TOPK_MAX, (past + t_s) // 4)
    log_gamma = retention_log_decay()

    xp, xs = x_prompt, x_sample
    kp, vp, ip, sp, ksl, vsl, isl, ssl = [], [], [], [], [], [], [], []
    for l in range(DEPTH):
        qa, ka, va, qi, ki, wi, qr, kr, vr, gr, ga, gb = split_projection(rms_norm(xp, norm1_g[l]), w_in[l], pos_p)
        o_a = dsa_prompt(qa, qi, wi, ka, va, ki, pos_p, top_k_p)
        o_b, st_p = retention_prompt(qr, kr, vr)
        xp = merge_and_ffn(xp, o_a, o_b, gr, ga, gb, w_pa[l], w_pb[l], w_o[l], norm2_g[l],
                           w_ffn_gate[l], w_ffn_up[l], w_ffn_down[l])
        kp.append(ka)
        vp.append(va)
        ip.append(ki)
        sp.append(st_p.astype(x_prompt.dtype))

        qa, ka, va, qi, ki, wi, qr, kr, vr, gr, ga, gb = split_projection(rms_norm(xs, norm1_g[l]), w_in[l], pos_s)
        k_all = jnp.concatenate([cache_k[l].astype(ka.dtype), ka], axis=1)
        v_all = jnp.concatenate([cache_v[l].astype(va.dtype), va], axis=1)
        i_all = jnp.concatenate([cache_idx_k[l].astype(ki.dtype), ki], axis=1)
        o_a = dsa_block(qa, qi, wi, pos_s, k_all, v_all, i_all, key_pos_s, top_k_s)
        o_b, st_s = retention_chunk(state_ret[l].astype(jnp.float32), qr.astype(jnp.float32),
                                    kr.astype(jnp.float32), vr.astype(jnp.float32), log_gamma)
        xs = merge_and_ffn(xs, o_a, o_b, gr, ga, gb, w_pa[l], w_pb[l], w_o[l], norm2_g[l],
                           w_ffn_gate[l], w_ffn_up[l], w_ffn_down[l])
        ksl.append(ka)
        vsl.append(va)
        isl.append(ki)
        ssl.append(st_s.astype(state_ret.dtype))

    y_prompt = rms_norm(xp, norm_f_g)
    y_sample = rms_norm(xs, norm_f_g)
    return (y_prompt, y_sample, jnp.stack(kp), jnp.stack(vp), jnp.stack(ip), jnp.stack(sp),
            jnp.stack(ksl), jnp.stack(vsl), jnp.stack(isl), jnp.stack(ssl))
```

```python
import contextlib
import numpy as np
import concourse.bass as bass
import concourse.mybir as mybir
from concourse.bass_utils import run_bass_kernel_spmd

F32 = mybir.dt.float32
BF16 = mybir.dt.bfloat16
ALU = mybir.AluOpType
AF = mybir.ActivationFunctionType
AX = mybir.AxisListType

D = 2048
FF = 5632
C_QA, C_KA, C_VA, C_QI, C_KI, C_WI, C_QR, C_KR, C_VR, C_GR, C_GA, C_GB = (
    0, 2048, 2560, 3072, 4096, 4160, 4176, 5200, 6224, 8272, 10320, 12368)
PTOT = 14416
NBIS = 16
NEG = -1.0e30
_STAGE = 99
_STOPVB = None
_STAGEVB = 0
_DEBUG = None


class Sched:
    ENG = ('pe', 'act', 'dve', 'pool', 'sp')

    def __init__(self):
        self.ins = {e: [] for e in self.ENG}
        self.lastw = {}
        self.readers = {}
        self.dma_cnt = {}
        self.seen = {e: {} for e in self.ENG}
        self.fence_toks = []
        self.fence_id = 0
        self.eng_fence = {e: 0 for e in self.ENG}

    def fence(self):
        toks = []
        for e in self.ENG:
            for i in range(len(self.ins[e]) - 1, -1, -1):
                if self.ins[e][i]['dma'] is None:
                    toks.append(('eng', e, i))
                    break
        for k, v in self.dma_cnt.items():
            toks.append(('dma', k, v))
        self.fence_toks = toks
        self.fence_id += 1

    def op(self, eng, fn, reads=(), writes=(), dma=None, nofence=False):
        idx = len(self.ins[eng])
        deps = []
        for r in reads:
            t = self.lastw.get(r)
            if t is not None:
                deps.append((t, 'raw'))
            if r.startswith(('rot', 'acc', 'trp')):
                for t in self.readers.get(r, ()):
                    if t[1] != eng:
                        deps.append((t, 'raw'))
        for w in writes:
            t = self.lastw.get(w)
            if t is not None:
                deps.append((t, 'waw'))
            for t in self.readers.get(w, ()):
                deps.append((t, 'war'))
        if not nofence and self.eng_fence[eng] < self.fence_id:
            self.eng_fence[eng] = self.fence_id
            for t in self.fence_toks:
                if not (t[0] == 'eng' and t[1] == eng):
                    deps.append((t, 'raw'))
        if dma is not None:
            self.dma_cnt[dma] = self.dma_cnt.get(dma, 0) + 16
            tok = ('dma', dma, self.dma_cnt[dma])
        else:
            tok = ('eng', eng, idx)
        waits = {}
        for t, kind in deps:
            if t[0] == 'eng':
                if t[1] == eng and dma is None:
                    if eng == 'pe':
                        continue
                    if kind == 'war' or idx - t[2] > 8:
                        continue
                key = ('eng', t[1])
            else:
                key = ('dma', t[1])
            val = t[2]
            if self.seen[eng].get(key, -1) >= val:
                continue
            if waits.get(key, -1) < val:
                waits[key] = val
        for k, v in waits.items():
            self.seen[eng][k] = v
        self.ins[eng].append(dict(fn=fn, waits=waits, dma=dma))
        for r in reads:
            self.readers.setdefault(r, []).append(tok)
        for w in writes:
            self.lastw[w] = tok
            self.readers[w] = []
        return tok

    def emit(self, nc, finals):
        miles = {e: set() for e in self.ENG}
        for e in self.ENG:
            for ins in self.ins[e]:
                for k, v in ins['waits'].items():
                    if k[0] == 'eng':
                        miles[k[1]].add(v)
        rank = {e: {s: i + 1 for i, s in enumerate(sorted(miles[e]))} for e in self.ENG}
        dkeys = sorted(self.dma_cnt.keys(), key=str)
        with contextlib.ExitStack() as st:
            psem = {e: st.enter_context(nc.semaphore("p_" + e)) for e in self.ENG}
            dsem = {k: st.enter_context(nc.semaphore("d_%d" % i)) for i, k in enumerate(dkeys)}
            block = st.enter_context(nc.Block())

            def run(e, eng):
                for i, ins in enumerate(self.ins[e]):
                    for k, v in ins['waits'].items():
                        if k[0] == 'eng':
                            eng.wait_ge(psem[k[1]], rank[k[1]][v])
                        else:
                            eng.wait_ge(dsem[k[1]], v)
                    bi = ins['fn'](eng)
                    if ins['dma'] is not None:
                        bi.then_inc(dsem[ins['dma']], 16)
                    elif i in rank[e]:
                        bi.then_inc(psem[e], 1)
                if e == 'sp':
                    fin = {}
                    for t in finals:
                        fin[t[1]] = max(fin.get(t[1], 0), t[2])
                    for k, v in fin.items():
                        eng.wait_ge(dsem[k], v)

            @block.tensor
            def _(eng):
                run('pe', eng)

            @block.scalar
            def _(eng):
                run('act', eng)

            @block.vector
            def _(eng):
                run('dve', eng)

            @block.gpsimd
            def _(eng):
                run('pool', eng)

            @block.sync
            def _(eng):
                run('sp', eng)


def build(NBR):
    NV = NBR + 1
    NOWN = NBR // 2
    NTIL = NV * 4 + 1
    SOFF = NV * 512
    NTOK = SOFF + 1152
    nc = bass.Bass("TRN2", target_bir_lowering=False)
    S = Sched()
    finals = []

    def din(name, shape):
        return nc.dram_tensor(name, shape, F32, kind="ExternalInput").ap()

    def dout(name, shape):
        return nc.dram_tensor(name, shape, F32, kind="ExternalOutput").ap()

    def dscr(name, shape, dt=BF16):
        return nc.dram_tensor(name, shape, dt, kind="Internal").ap()

    xv = din("xv", [NV * 512, D]); xs = din("xs", [128, D])
    ck = din("ck", [1024, 512]); cv = din("cv", [1024, 512]); ci = din("ci", [1024, 64])
    st0 = din("st0", [128, 8, 256])
    w_in = din("w_in", [D, PTOT]); w_pa = din("w_pa", [D, D]); w_pb = din("w_pb", [D, D]); w_o = din("w_o", [D, D])
    w_g = din("w_g", [D, FF]); w_u = din("w_u", [D, FF]); w_d = din("w_d", [FF, D])
    n1t = din("n1t", [128, 16]); n2t = din("n2t", [128, 16]); nfr = din("nfr", [128, D])
    rt = din("rt", [NV * 512 + 128, 320])
    kdec_d = din("kdec", [128, NTIL * 8]); sdec_d = din("sdec", [128, NTIL * 8])
    kq_d = din("kq", [128, 16])
    idf_d = din("idf", [128, 128]); tri_d = din("tri", [128, 128])
    maskD_d = din("maskD", [128, 256]); mask0_d = din("mask0", [128, 512]); pow2_d = din("pow2", [128, NBIS])

    y_o = dout("y_o", [NOWN * 512, D]); k_o = dout("k_o", [NOWN * 512, 512]); v_o = dout("v_o", [NOWN * 512, 512])
    ik_o = dout("ik_o", [NOWN * 512, 64]); st_o = dout("st_o", [128, 8, 256])
    ys_o = dout("ys_o", [128, D]); ks_o = dout("ks_o", [128, 512]); vs_o = dout("vs_o", [128, 512])
    iks_o = dout("iks_o", [128, 64]); sts_o = dout("sts_o", [128, 8, 256])

    if _DEBUG is not None:
        dbg_s = dout("dbg_s", [128, 1024]); dbg_m = dout("dbg_m", [128, 1024]); dbg_oa = dout("dbg_oa", [128, 2048]); dbg_ob = dout("dbg_ob", [128, 2048])
    kTs = dscr("kTs", [4, 128, NTOK]); Vs = dscr("Vs", [NTOK, 4, 136]); kiTs = dscr("kiTs", [2, 128, NTOK])
    hTs = dscr("hTs", [128, 16, 512]); obTs = dscr("obTs", [128, 16, 512])

    ucount = [0]

    def mkgroups(wap, K, c0, ncols_total, gw=512):
        groups = []
        kch = K // 128
        for g0 in range(0, ncols_total, gw):
            ncols = min(gw, ncols_total - g0)
            units = []
            for k0 in range(0, kch, 16):
                kc = min(16, kch - k0)
                uid = ucount[0]; ucount[0] += 1
                scr = dscr("wu%d" % uid, [128, kc, ncols])
                src = wap[k0 * 128:(k0 + kc) * 128, c0 + g0:c0 + g0 + ncols].rearrange("(c p) n -> p c n", p=128)
                units.append(dict(uid=uid, scr=scr, src=src, kc=kc, k0=k0, ncols=ncols, res="wu%d" % uid))
            groups.append(units)
        return groups

    G_ka = mkgroups(w_in, D, C_KA, 512); G_va = mkgroups(w_in, D, C_VA, 512); G_ki = mkgroups(w_in, D, C_KI, 80)
    G_kr = mkgroups(w_in, D, C_KR, 1024); G_vr = mkgroups(w_in, D, C_VR, 2048)
    G_qr = mkgroups(w_in, D, C_QR, 1024); G_gr = mkgroups(w_in, D, C_GR, 2048)
    G_qa = mkgroups(w_in, D, C_QA, 2048); G_qi = mkgroups(w_in, D, C_QI, 1024)
    G_ga = mkgroups(w_in, D, C_GA, 2048); G_pa = mkgroups(w_pa, D, 0, 2048)
    G_gb = mkgroups(w_in, D, C_GB, 2048); G_pb = mkgroups(w_pb, D, 0, 2048)
    G_o = mkgroups(w_o, D, 0, 2048)
    G_g = mkgroups(w_g, D, 0, FF); G_u = mkgroups(w_u, D, 0, FF); G_d = mkgroups(w_d, FF, 0, 2048)
    allgroups = [G_ka, G_va, G_ki, G_kr, G_vr, G_qr, G_gr, G_qa, G_qi, G_ga, G_pa, G_gb, G_pb, G_o]
    ffn_order = []
    for i in range(len(G_g)):
        ffn_order += [G_g[i], G_u[i]]

    with contextlib.ExitStack() as st:
        def sb(name, shape, dt):
            return st.enter_context(nc.sbuf_tensor("s_" + name, shape, dt))

        def pst(name, shape, dt):
            return st.enter_context(nc.psum_tensor("p_" + name, shape, dt))

        wslot = [sb("ws0", [128, 16, 512], BF16), sb("ws1", [128, 16, 512], BF16)]
        Sst = sb("Sst", [128, 8, 256], F32); Sb = sb("Sb", [128, 8, 256], BF16)
        oaT = sb("oaT", [128, 16, 512], BF16)
        identf = sb("identf", [128, 128], F32); identb = sb("identb", [128, 128], BF16)
        trif = sb("trif", [128, 128], F32); trib = sb("trib", [128, 128], BF16)
        i4big = sb("i4big", [128, 4, 128], BF16)
        kdec = sb("kdec", [128, NTIL * 8], F32); sdec = sb("sdec", [128, NTIL * 8], F32)
        kq = sb("kq", [128, 16], F32)
        n1s = sb("n1s", [128, 16], F32); n2s = sb("n2s", [128, 16], F32)
        pow2 = sb("pow2", [128, NBIS], F32)
        maskD = sb("maskD", [128, 256], F32); mask0 = sb("mask0", [128, 512], F32)
        rts = [sb("rt0", [128, 320], F32), sb("rt1", [128, 320], F32)]
        rt1 = sb("rtmp1", [128, 256], F32); rt2 = sb("rtmp2", [128, 256], F32)
        stat = sb("stat", [128, 64], F32)
        wab = sb("wab", [128, 4, 16], F32); wsg = sb("wsg", [128, 4, 16], F32)
        UB = 114688
        U = sb("U", [128, UB // 4], F32)

        def uv(off, nbytes, dt, pat=None, **kw):
            a = U[:, off // 4:(off + nbytes) // 4]
            if dt == BF16:
                a = a.bitcast(BF16)
            if pat:
                a = a.rearrange(pat, **kw)
            return a

        K1 = 1024
        qaT = uv(0, 16 * K1, BF16, "p (c t) -> p c t", c=16)
        qiT = uv(16 * K1, 8 * K1, BF16, "p (c t) -> p c t", c=8)
        hT = uv(24 * K1, 16 * K1, BF16, "p (c t) -> p c t", c=16)
        xst = uv(40 * K1, 8 * K1, F32)
        kd = uv(48 * K1, 8 * K1, BF16, "p (t c) -> p t c", t=4)
        kpT = uv(56 * K1, 8 * K1, BF16, "p (c t) -> p c t", c=8)
        qrT = uv(64 * K1, 8 * K1, BF16, "p (c t) -> p c t", c=8)
        vr = uv(72 * K1, 16 * K1, BF16, "p (t c) -> p t c", t=4)
        grs = uv(88 * K1, 16 * K1, BF16, "p (t c) -> p t c", t=4)
        obb = uv(104 * K1, 4 * K1, BF16)
        obst = uv(40 * K1, 4 * K1, BF16, "p (c t) -> p c t", c=16)
        innT = uv(108 * K1, 2 * K1, BF16, "p (h t) -> p h t", h=8)
        score = uv(24 * K1, 32 * K1, F32)
        mmask = uv(56 * K1, 16 * K1, BF16)
        kvk = [uv(72 * K1 + i * 8704, 4096, BF16, "p (g t) -> p g t", g=4) for i in range(2)]
        kvv = [uv(72 * K1 + i * 8704 + 4096, 4352, BF16, "p (c g e) -> p c g e", c=4, g=4) for i in range(2)]
        o3 = 72 * K1 + 2 * 8704
        kis = [uv(o3 + i * 2 * K1, 2 * K1, BF16, "p (v t) -> p v t", v=2) for i in range(2)]
        rbuf = [uv(o3 + 4 * K1 + i * 2 * K1, 2 * K1, F32) for i in range(2)]
        PTb = [uv(o3 + 8 * K1 + i * K1, K1, BF16) for i in range(2)]
        oacc = uv(o3 + 10 * K1, 8256, F32, "p (h e) -> p h e", h=16)
        oab = uv(o3 + 10 * K1 + 8256, 4 * K1, BF16)
        hTb = uv(0, 16 * K1, BF16, "p (c t) -> p c t", c=16)
        obT = uv(16 * K1, 16 * K1, BF16, "p (c t) -> p c t", c=16)
        sA = uv(32 * K1, 16 * K1, BF16, "p (c t) -> p c t", c=16)
        sB = uv(48 * K1, 16 * K1, BF16, "p (c t) -> p c t", c=16)
        mgT = uv(64 * K1, 16 * K1, BF16, "p (c t) -> p c t", c=16)
        x2 = uv(80 * K1, 32 * K1, F32, "p (t c) -> p t c", t=4)
        xsb = uv(32 * K1, 4 * K1, BF16)
        h2T = uv(0, 16 * K1, BF16, "p (c t) -> p c t", c=16)
        aT = uv(16 * K1, 44 * K1, BF16, "p (c t) -> p c t", c=44)
        sgt = uv(60 * K1, 4 * K1, BF16, "p (c t) -> p c t", c=4)
        nfrep = uv(64 * K1, 8 * K1, F32)
        sgj = uv(60 * K1, 4 * K1, BF16)
        kfst = sb("kfst", [128, 512], F32); vfst = sb("vfst", [128, 512], F32); kifst = sb("kifst", [128, 64], F32)
        kbst = sb("kbst", [128, 512], BF16); kTst = sb("kTst", [128, 4, 128], BF16)
        vbst = sb("vbst", [128, 4, 136], BF16); kibst = sb("kibst", [128, 64], BF16); kiTst = sb("kiTst", [128, 2, 128], BF16)
        qst = sb("qst", [128, 512], BF16)

        rot = [pst("rot%d" % i, [128, 512], F32) for i in range(4)]
        trp = [pst("trp%d" % i, [128, 1024], BF16) for i in range(2)]
        acc = [pst("acc%d" % i, [128, 512], F32) for i in range(2)]
        trc = [0]
        accl = [acc[0], acc[1], rot[2], rot[3]]
        accr = ['acc0', 'acc1', 'rot2', 'rot3']
        agc = [0]

        def MM(out, lhsT, rhs, start, stop, R, W, sgc=False):
            if sgc:
                S.op('pe', lambda e: e.matmul(out, lhsT, rhs, start=start, stop=stop, skip_group_check=True), R, W)
            else:
                S.op('pe', lambda e: e.matmul(out, lhsT, rhs, start=start, stop=stop), R, W)

        def TR(out, in_, R, W):
            S.op('pe', lambda e: e.transpose(out=out, in_=in_, identity=identb[:]), list(R) + ['identb'], W)

        def ACT(out, in_, func, R, W, scale=None, bias=None, accum=None):
            kw = {}
            if scale is not None:
                kw['scale'] = scale
            if bias is not None:
                kw['bias'] = bias
            if accum is not None:
                kw['accum_out'] = accum
            S.op('act', lambda e: e.activation(out=out, in_=in_, func=func, **kw), R, W)

        def TS(eng, out, in0, s1, s2, op0, op1, R, W, accum=None):
            kw = {}
            if op1 is not None:
                kw['op1'] = op1
            if accum is not None:
                kw['accum_out'] = accum
            S.op(eng, lambda e: e.tensor_scalar(out=out, in0=in0, scalar1=s1, scalar2=s2, op0=op0, **kw), R, W)

        def TT(eng, out, in0, in1, op, R, W):
            S.op(eng, lambda e: e.tensor_tensor(out=out, in0=in0, in1=in1, op=op), R, W)

        def STT(eng, out, in0, scalar, in1, op0, op1, R, W):
            S.op(eng, lambda e: e.scalar_tensor_tensor(out=out, in0=in0, scalar=scalar, in1=in1, op0=op0, op1=op1), R, W)

        def DMA(eng, out, in_, R, W, key, nofence=False):
            return S.op(eng, lambda e: e.dma_start(out=out, in_=in_), R, W, dma=key, nofence=nofence)

        for dst, src, nm in ((identf[:], idf_d, 'identf'), (trif[:], tri_d, 'trif'), (kdec[:], kdec_d, 'kdec'),
                             (sdec[:], sdec_d, 'sdec'), (kq[:], kq_d, 'kq'), (n1s[:], n1t, 'n1s'), (n2s[:], n2t, 'n2s'),
                             (pow2[:], pow2_d, 'pow2'), (maskD[:], maskD_d, 'maskD'), (mask0[:], mask0_d, 'mask0')):
            DMA('sp', dst, src[:, :], [], [nm], 'const', nofence=True)
        ACT(identb[:], identf[:], AF.Copy, ['identf'], ['identb'])
        ACT(trib[:], trif[:], AF.Copy, ['trif'], ['trib'])
        for j in range(4):
            ACT(i4big[:, j, :], identf[:], AF.Copy, ['identf'], ['i4big'], scale=30000.0)
        S.op('dve', lambda e: e.memset(stat[:, 60:61], 1e-6), [], ['eps'])
        S.op('dve', lambda e: e.memset(vbst[:], 1.0), [], ['vbst'])
        S.op('dve', lambda e: e.memset(kiTst[:], 0.0), [], ['kiTst'])

        cast_order = []
        for G in allgroups:
            for units in G:
                cast_order += units
        for units in ffn_order:
            cast_order += units
        for units in G_d:
            cast_order += units
        for i, u in enumerate(cast_order):
            DMA('pool', u['scr'][:, :, :], u['src'], [], [u['res']], ('wc', i % 8), nofence=True)

        if _STAGE <= 1:
            S.emit(nc, finals); return nc
        wcnt = [0]

        def wload(u):
            s = wcnt[0] % 2
            wcnt[0] += 1
            DMA('sp', wslot[s][:, 0:u['kc'], 0:u['ncols']], u['scr'][:, :, :], [u['res']], ['ws%d' % s], ('wl', s), nofence=True)
            return s

        def stream(ulist, body):
            slots = {}
            if ulist:
                slots[0] = wload(ulist[0])
            for i, u in enumerate(ulist):
                if i + 1 < len(ulist):
                    slots[i + 1] = wload(ulist[i + 1])
                s = slots[i]
                body(i, u, wslot[s], 'ws%d' % s)

        defq = []

        def run_deferred():
            while defq:
                defq.pop(0)()

        def gemm_tok(groups, actT, actres, ntile, evac):
            flat = []
            for gi, units in enumerate(groups):
                for ui, u in enumerate(units):
                    flat.append((gi, ui, len(units), u))

            def body(i, u, slot, sres):
                gi, ui, nu, _ = flat[i]
                for t in range(ntile):
                    b = rot[t]
                    for c in range(u['kc']):
                        MM(b[:, 0:u['ncols']], actT[:, u['k0'] + c, t * 128:(t + 1) * 128], slot[:, c, 0:u['ncols']],
                           (ui == 0 and c == 0), (ui == nu - 1 and c == u['kc'] - 1), [actres, sres], ['rot%d' % t])
                    run_deferred()
                    if ui == nu - 1:
                        evac(gi, t, b, 'rot%d' % t, u['ncols'])
            stream([f[3] for f in flat], body)
            run_deferred()

        def gemm_feat(groups, actT, actres, NT, evac):
            flat = [units[0] for units in groups]

            def body(i, u, slot, sres):
                for nn in range(u['ncols'] // 128):
                    b = rot[nn]
                    for c in range(16):
                        MM(b[:, 0:NT], slot[:, c, nn * 128:(nn + 1) * 128], actT[:, c, 0:NT], c == 0, c == 15,
                           [actres, sres], ['rot%d' % nn])
                    evac(i, nn, b, 'rot%d' % nn)
            stream(flat, body)

        def transposes(src, dstfn, nchunk, R, W, evac_eng='act', scale_ap=None):
            for c0 in range(0, nchunk, 8):
                n = min(8, nchunk - c0)
                tb = trp[trc[0] % 2]; tres = 'trp%d' % (trc[0] % 2); trc[0] += 1
                for j in range(n):
                    TR(tb[:, j * 128:(j + 1) * 128], src[:, (c0 + j) * 128:(c0 + j + 1) * 128], R, [tres])
                dst = dstfn(c0, n)
                srcv = tb[:, 0:n * 128].rearrange("p (c t) -> p c t", c=n)
                if scale_ap is not None:
                    TT('dve', dst, srcv, scale_ap[:, c0:c0 + n].unsqueeze(2).to_broadcast([128, n, 128]), ALU.mult,
                       [tres] + list(R), W)
                elif evac_eng == 'act':
                    ACT(dst, srcv, AF.Copy, [tres], W)
                else:
                    S.op('dve', lambda e: e.tensor_copy(out=dst, in_=srcv), [tres], W)

        def rope(ps, psres, nh, hd, cos, sin, d, W, rsl):
            half = hd // 2
            x = ps.rearrange("p (h t d) -> p h t d", h=nh, t=2)
            dv = d.rearrange("p (h t d) -> p h t d", h=nh, t=2)
            t1 = rt1[:, 0:nh * half].rearrange("p (h d) -> p h d", h=nh)
            t2 = rt2[:, 0:nh * half].rearrange("p (h d) -> p h d", h=nh)
            cb = cos.unsqueeze(1).to_broadcast([128, nh, half])
            sbb = sin.unsqueeze(1).to_broadcast([128, nh, half])
            TT('dve', t1, x[:, :, 0, :], cb, ALU.mult, [psres, rsl], ['rt1'])
            TT('dve', t2, x[:, :, 1, :], sbb, ALU.mult, [psres, rsl], ['rt2'])
            TT('dve', dv[:, :, 0, :], t1, t2, ALU.subtract, ['rt1', 'rt2'], W)
            TT('dve', t1, x[:, :, 1, :], cb, ALU.mult, [psres, rsl], ['rt1'])
            TT('dve', t2, x[:, :, 0, :], sbb, ALU.mult, [psres, rsl], ['rt2'])
            TT('dve', dv[:, :, 1, :], t1, t2, ALU.add, ['rt1', 'rt2'], W)

        def rstd_of(ss_col, out_col, n, R, W):
            ACT(stat[:, 62:63], ss_col, AF.Sqrt, list(R) + ['eps'], ['sq'], scale=1.0 / n, bias=stat[:, 60:61])
            S.op('dve', lambda e: e.reciprocal(out=out_col, in_=stat[:, 62:63]), ['sq'], W)

        def norm_to_T(xt, xres, gts, dstT, dres, t, tmpb, tmpres):
            ACT(tmpb, xt, AF.Square, [xres], [tmpres, 'ss'], accum=stat[:, 0:1])
            rstd_of(stat[:, 0:1], stat[:, 1:2], D, ['ss'], ['rstd'])
            ACT(tmpb, xt, AF.Copy, [xres, 'rstd'], [tmpres], scale=stat[:, 1:2])
            transposes(tmpb, lambda c0, n: dstT[:, c0:c0 + n, t * 128:(t + 1) * 128], 16, [tmpres], [dres], scale_ap=gts)

        rtc = [0]

        def ropeload_g(gt):
            sl = rtc[0] % 2
            rtc[0] += 1
            rr = gt * 128
            DMA('sp', rts[sl][:], rt[rr:rr + 128, :], [], ['rts%d' % sl], ('rt', sl))
            return rts[sl], 'rts%d' % sl

        def kside(ntile, xsrc_fn, tok0, tile0, rrow0, own_out, spar):
            NT = ntile * 128
            scr_key = ('scw', spar)
            sres = 'scr%d' % spar
            for t in range(ntile):
                DMA('sp', xst, xsrc_fn(t), [], ['xst'], 'xst')
                norm_to_T(xst, 'xst', n1s, hT, 'hT', t, obb, 'obb')

            def ropeload(t):
                return ropeload_g(tile0 + t)

            def ev_ka(gi, t, b, bres, ncols):
                r, rres = ropeload(t)
                rope(b[:, 0:512], bres, 4, 128, r[:, 0:64], r[:, 64:128], kfst[:], ['kfst'], rres)
                if own_out:
                    finals.append(DMA('pool', own_out['k'](t), kfst[:], ['kfst'], [], 'ko'))
                ACT(kbst[:], kfst[:], AF.Copy, ['kfst'], ['kbst'])

                def later():
                    transposes(kbst, lambda c0, n: kTst[:, c0:c0 + n, :], 4, ['kbst'], ['kTst'])
                    tk = tok0 + t * 128
                    DMA('pool', kTs[:, :, tk:tk + 128].rearrange("g p t -> p g t"), kTst[:], ['kTst'], [sres], scr_key)
                defq.append(later)
            gemm_tok(G_ka, hT, 'hT', ntile, ev_ka)

            def ev_va(gi, t, b, bres, ncols):
                ACT(vfst[:], b[:, 0:512], AF.Copy, [bres], ['vfst'])
                if own_out:
                    finals.append(DMA('pool', own_out['v'](t), vfst[:], ['vfst'], [], 'vo'))
                ACT(vbst[:, :, 0:128], b[:, 0:512].rearrange("p (g e) -> p g e", g=4), AF.Copy, [bres], ['vbst'])
                tk = tok0 + t * 128
                DMA('pool', Vs[tk:tk + 128, :, :], vbst[:], ['vbst'], [sres], scr_key)
            gemm_tok(G_va, hT, 'hT', ntile, ev_va)

            def ev_ki(gi, t, b, bres, ncols):
                r, rres = ropeload(t)
                rope(b[:, 0:64], bres, 1, 64, r[:, 256:288], r[:, 288:320], kifst[:], ['kifst'], rres)
                if own_out:
                    finals.append(DMA('pool', own_out['ik'](t), kifst[:], ['kifst'], [], 'iko'))
                    ACT(wab[:, t, :], b[:, 64:80], AF.Abs, [bres], ['wab'], scale=1.0 / 32.0)
                    ACT(wsg[:, t, :], b[:, 64:80], AF.Sign, [bres], ['wsg'])
                ACT(kibst[:], kifst[:], AF.Copy, ['kifst'], ['kibst'])

                def later():
                    tb = trp[trc[0] % 2]; tres = 'trp%d' % (trc[0] % 2); trc[0] += 1
                    TR(tb[0:64, 0:128], kibst[:], ['kibst'], [tres])
                    ACT(kiTst[0:64, 0, :], tb[0:64, 0:128], AF.Copy, [tres], ['kiTst'])
                    ACT(kiTst[64:128, 1, :], tb[0:64, 0:128], AF.Copy, [tres], ['kiTst'])
                    tk = tok0 + t * 128
                    DMA('pool', kiTs[:, :, tk:tk + 128].rearrange("v p t -> p v t"), kiTst[:], ['kiTst'], [sres], scr_key)
                defq.append(later)
            gemm_tok(G_ki, hT, 'hT', ntile, ev_ki)

            def ev_kr(gi, t, b, bres, ncols):
                r, rres = ropeload(t)
                rope(b[:, 0:512], bres, 4, 128, r[:, 128:192], r[:, 192:256], kfst[:], ['kfst'], rres)
                krf = kfst[:].rearrange("p (h d) -> p h d", h=4)
                kdv = kdec[:, (tile0 + t) * 8 + gi * 4:(tile0 + t) * 8 + gi * 4 + 4].unsqueeze(2).to_broadcast([128, 4, 128])
                TT('dve', kd[:, t, gi * 512:(gi + 1) * 512].rearrange("p (h d) -> p h d", h=4), krf, kdv, ALU.mult,
                   ['kfst', 'kdec'], ['kd'])
                if own_out:
                    kiv = kq[:, gi * 4:gi * 4 + 4].unsqueeze(2).to_broadcast([128, 4, 128])
                    TT('dve', qst[:].rearrange("p (h d) -> p h d", h=4), krf, kiv, ALU.mult, ['kfst', 'kq'], ['qst'])
                    defq.append(lambda: transposes(qst, lambda c0, n: kpT[:, gi * 4 + c0:gi * 4 + c0 + n, t * 128:(t + 1) * 128], 4,
                                                   ['qst'], ['kpT']))
            gemm_tok(G_kr, hT, 'hT', ntile, ev_kr)

            def ev_vr(gi, t, b, bres, ncols):
                ACT(vr[:, t, gi * 512:(gi + 1) * 512], b[:, 0:512], AF.Copy, [bres], ['vr'])
            gemm_tok(G_vr, hT, 'hT', ntile, ev_vr)

        def state_update(t, gt):
            for hp in range(4):
                b = rot[hp]; bres = 'rot%d' % hp
                for hh in range(2):
                    h = hp * 2 + hh
                    MM(b[:, hh * 256:(hh + 1) * 256], kd[:, t, h * 128:(h + 1) * 128], vr[:, t, h * 256:(h + 1) * 256],
                       True, True, ['kd', 'vr'], [bres])
                for hh in range(2):
                    h = hp * 2 + hh
                    STT('dve', Sst[:, h, :], Sst[:, h, :], sdec[:, gt * 8 + h:gt * 8 + h + 1], b[:, hh * 256:(hh + 1) * 256],
                        ALU.mult, ALU.add, ['Sst', bres, 'sdec'], ['Sst'])
            ACT(Sb[:], Sst[:], AF.Copy, ['Sst'], ['Sb'])

        def qside_ret(ntile, tile0):
            def ev_qr(gi, t, b, bres, ncols):
                r, rres = ropeload_g(tile0 + t)
                rope(b[:, 0:512], bres, 4, 128, r[:, 0:64], r[:, 64:128], qst[:], ['qst'], rres)
                defq.append(lambda: transposes(qst, lambda c0, n: qrT[:, gi * 4 + c0:gi * 4 + c0 + n, t * 128:(t + 1) * 128], 4, ['qst'], ['qrT']))

            gemm_tok(G_qr, hT, 'hT', ntile, ev_qr)

            def ev_gr(gi, t, b, bres, ncols):
                ACT(grs[:, t, gi * 512:(gi + 1) * 512], b[:, 0:512], AF.Silu, [bres], ['grs'])
            gemm_tok(G_gr, hT, 'hT', ntile, ev_gr)
            for t in range(ntile):
                tsl = slice(t * 128, (t + 1) * 128)
                for hq in range(2):
                    b = rot[hq]; bres = 'rot%d' % hq
                    for hh in range(4):
                        h = hq * 4 + hh
                        MM(b[:, hh * 128:(hh + 1) * 128], kpT[:, h, tsl], qrT[:, h, tsl], True, True, ['kpT', 'qrT'], [bres])
                    TT('dve', innT[:, hq * 4:hq * 4 + 4, :], b[:, 0:512].rearrange("p (h t) -> p h t", h=4),
                       trib[:].unsqueeze(1).to_broadcast([128, 4, 128]), ALU.mult, [bres, 'trib'], ['innT'])
                for hp in range(4):
                    b = rot[hp]; bres = 'rot%d' % hp
                    for hh in range(2):
                        h = hp * 2 + hh
                        MM(b[:, hh * 256:(hh + 1) * 256], innT[:, h, :], vr[:, t, h * 256:(h + 1) * 256], True, False,
                           ['innT', 'vr'], [bres])
                        MM(b[:, hh * 256:(hh + 1) * 256], qrT[:, h, tsl], Sb[:, h, :], False, True, ['qrT', 'Sb'], [bres])
                    for hh in range(2):
                        h = hp * 2 + hh
                        ACT(rt1[:, 0:256], b[:, hh * 256:(hh + 1) * 256], AF.Square, [bres, 'kq'], ['rt1', 'gss'],
                            scale=kq[:, 8 + h:9 + h], accum=stat[:, 8 + h:9 + h])
                ACT(stat[:, 16:24], stat[:, 8:16], AF.Sqrt, ['gss', 'eps'], ['gsq'], scale=1.0 / 256, bias=stat[:, 60:61])
                S.op('dve', lambda e: e.reciprocal(out=stat[:, 24:32], in_=stat[:, 16:24]), ['gsq'], ['grc'])
                TT('dve', stat[:, 32:40], stat[:, 24:32], kq[:, 8:16], ALU.mult, ['grc', 'kq'], ['gc'])
                for hp in range(4):
                    b = rot[hp]; bres = 'rot%d' % hp
                    for hh in range(2):
                        h = hp * 2 + hh
                        STT('dve', obb[:, h * 256:(h + 1) * 256], b[:, hh * 256:(hh + 1) * 256], stat[:, 32 + h:33 + h],
                            grs[:, t, h * 256:(h + 1) * 256], ALU.mult, ALU.mult, [bres, 'gc', 'grs'], ['obb'])
                if _DEBUG is not None and _DEBUG == (tile0 // 4, t) and ntile == 4:
                    finals.append(DMA('pool', dbg_ob[:, :], obb, ['obb'], [], 'dbg'))
                transposes(obb, lambda c0, n: obst[:, c0:c0 + n, :], 16, ['obb'], ['obst'])
                DMA('pool', obTs[:, :, t * 128:(t + 1) * 128], obst[:], ['obst'], ['obTs'], 'obw')
                state_update(t, tile0 + t)

        def qside_proj(ntile, tile0):
            def ev_qa(gi, t, b, bres, ncols):
                r, rres = ropeload_g(tile0 + t)
                rope(b[:, 0:512], bres, 4, 128, r[:, 0:64], r[:, 64:128], qst[:], ['qst'], rres)
                defq.append(lambda: transposes(qst, lambda c0, n: qaT[:, gi * 4 + c0:gi * 4 + c0 + n, t * 128:(t + 1) * 128], 4, ['qst'], ['qaT']))
            gemm_tok(G_qa, hT, 'hT', ntile, ev_qa)

            def ev_qi(gi, t, b, bres, ncols):
                r, rres = ropeload_g(tile0 + t)
                rope(b[:, 0:512], bres, 8, 64, r[:, 256:288], r[:, 288:320], qst[:], ['qst'], rres)
                defq.append(lambda: transposes(qst, lambda c0, n: qiT[:, gi * 4 + c0:gi * 4 + c0 + n, t * 128:(t + 1) * 128], 4, ['qst'], ['qiT']))
            gemm_tok(G_qi, hT, 'hT', ntile, ev_qi)
            NT = ntile * 128
            DMA('pool', hTs[:, :, 0:NT], hT[:, :, 0:NT], ['hT'], ['hTs'], 'hTw')

        def attention_tile(t, key_segs, mtype, use_mask0, dbg=False):
            sgs = []
            pos = 0
            for (o, n) in key_segs:
                for a in range(0, n, 512):
                    w = min(512, n - a)
                    sgs.append((o + a, w, pos)); pos += w
            L = pos
            tsl = slice(t * 128, (t + 1) * 128)
            for si, (so, w, p0) in enumerate(sgs):
                sl = si % 2
                DMA('sp', kis[sl][:, :, 0:w], kiTs[:, :, so:so + w].rearrange("v p t -> p v t"), ['scr0', 'scr1'], ['kis%d' % sl], ('kis', sl))
                for j in range(16):
                    hf = j % 2
                    b = rot[j % 4]; bres = 'rot%d' % (j % 4)
                    MM(b[:, 0:w], qiT[:, j // 2, tsl], kis[sl][:, hf, 0:w], True, True, ['qiT', 'kis%d' % sl], [bres])
                    rb = rbuf[j % 2]; rres = 'rb%d' % (j % 2)
                    ACT(rb[:, 0:w], b[:, 0:w], AF.Relu, [bres, 'wab'], [rres], scale=wab[:, t, j:j + 1])
                    if j == 0:
                        TS('dve', score[:, p0:p0 + w], rb[:, 0:w], wsg[:, t, 0:1], None, ALU.mult, None, [rres, 'wsg'], ['score'])
                    else:
                        STT('dve', score[:, p0:p0 + w], rb[:, 0:w], wsg[:, t, j:j + 1], score[:, p0:p0 + w], ALU.mult, ALU.add,
                            [rres, 'wsg', 'score'], ['score'])
            if _STAGE == 5.1:
                return
            lo, hi, rng, mid, cnt, tmp = (stat[:, 40:41], stat[:, 41:42], stat[:, 42:43], stat[:, 43:44], stat[:, 44:45], stat[:, 45:46])
            S.op('dve', lambda e: e.tensor_reduce(out=lo, in_=score[:, 0:L], axis=AX.X, op=ALU.min), ['score'], ['lo'])
            S.op('dve', lambda e: e.tensor_reduce(out=hi, in_=score[:, 0:L], axis=AX.X, op=ALU.max), ['score'], ['hi'])
            TT('dve', rng, hi, lo, ALU.subtract, ['lo', 'hi'], ['rng'])
            TS('dve', stp[:], pow2[:], rng, None, ALU.mult, None, ['rng', 'pow2'], ['stp'])
            TS('dve', stp2[:], pow2[:], rng, 2.0, ALU.mult, ALU.mult, ['rng', 'pow2'], ['stp'])
            TS('dve', stp2[:, 0:1], stp[:, NBIS - 1:NBIS], 1.125, None, ALU.mult, None, ['stp'], ['stp'])
            TT('dve', score[:, L - 128:L], score[:, L - 128:L], maskD[:, mtype * 128:(mtype + 1) * 128], ALU.add,
               ['score', 'maskD'], ['score'])
            if use_mask0:
                TT('dve', score[:, 0:512], score[:, 0:512], mask0[:], ALU.add, ['score', 'mask0'], ['score'])
            TT('dve', mid, lo, stp[:, 0:1], ALU.add, ['lo', 'stp'], ['mid'])
            for k in range(NBIS):
                TS('dve', mmask[:, 0:L], score[:, 0:L], mid, None, ALU.is_ge, ALU.add, ['score', 'mid'], ['mmask', 'cnt'], accum=cnt)
                if k + 1 < NBIS:
                    STT('dve', tmp, cnt, 256.0, stp2[:, k + 1:k + 2], ALU.is_ge, ALU.mult, ['cnt', 'stp'], ['tmp'])
                    STT('dve', mid, tmp, stp[:, k + 1:k + 2], mid, ALU.subtract, ALU.add, ['tmp', 'stp', 'mid'], ['mid'])
                else:
                    STT('dve', tmp, cnt, 256.0, stp2[:, 0:1], ALU.is_ge, ALU.mult, ['cnt', 'stp'], ['tmp'])
                    STT('dve', lo, tmp, stp2[:, 0:1], mid, ALU.subtract, ALU.add, ['tmp', 'stp', 'mid'], ['lo'])
            TS('dve', mmask[:, 0:L], score[:, 0:L], lo, -1.0, ALU.is_ge, ALU.add, ['score', 'lo'], ['mmask'])
            if _STAGE == 5.2:
                return
            for si, (so, w, p0) in enumerate(sgs):
                sl = si % 2
                nch = w // 128
                DMA('sp', kvk[sl][:, :, 0:w], kTs[:, :, so:so + w].rearrange("g p t -> p g t"), ['scr0', 'scr1'],
                    ['kvk%d' % sl], ('kvk', sl))
                DMA('sp', kvv[sl][:, 0:nch, :, :], Vs[so:so + w, :, :].rearrange("(c p) g e -> p c g e", p=128),
                    ['scr0', 'scr1'], ['kvv%d' % sl], ('kvv', sl))
                items = [(g, c) for g in range(4) for c in range(nch)]

                def emit_S(ix):
                    g, c = items[ix]
                    b = rot[ix % 2]; bres = 'rot%d' % (ix % 2)
                    MM(b[:, 0:512], kvk[sl][:, g, c * 128:(c + 1) * 128], qaT[:, 4 * g:4 * g + 4, tsl], True, False,
                       ['kvk%d' % sl, 'qaT'], [bres])
                    MM(b[:, 0:512], mmask[:, p0 + c * 128:p0 + (c + 1) * 128], i4big[:], False, True, ['mmask', 'i4big'], [bres])
                emit_S(0)
                for ix, (g, c) in enumerate(items):
                    b = rot[ix % 2]; bres = 'rot%d' % (ix % 2)
                    pb = PTb[ix % 2]; pres = 'PT%d' % (ix % 2)
                    ACT(pb, b[:, 0:512], AF.Exp, [bres], [pres], scale=float(128 ** -0.5))
                    if ix + 1 < len(items):
                        emit_S(ix + 1)
                    aset = (agc[0] % 2) * 2
                    for hh in range(4):
                        a = accl[aset + hh // 2]; ares = accr[aset + hh // 2]
                        MM(a[:, (hh % 2) * 256:(hh % 2) * 256 + 129], pb[:, hh * 128:(hh + 1) * 128], kvv[sl][:, c, g, 0:129],
                           (c == 0 and hh % 2 == 0), c == nch - 1, [pres, 'kvv%d' % sl], [ares], sgc=True)
                    if c == nch - 1:
                        agc[0] += 1
                        for hp in range(2):
                            a = accl[aset + hp]; ares = accr[aset + hp]
                            dst = oacc[:, 4 * g + 2 * hp:4 * g + 2 * hp + 2, :]
                            srcv = a[:, 0:512].rearrange("p (h e) -> p h e", h=2)[:, :, 0:129]
                            if si == 0:
                                S.op('dve', lambda e, dst=dst, srcv=srcv: e.tensor_copy(out=dst, in_=srcv), [ares], ['oacc'])
                            else:
                                TT('dve', dst, srcv, dst, ALU.add, [ares, 'oacc'], ['oacc'])
            S.op('dve', lambda e: e.reciprocal(out=rcp[:], in_=oacc[:, :, 128]), ['oacc'], ['rcp'])
            TT('dve', oab.rearrange("p (h d) -> p h d", h=16), oacc[:, :, 0:128], rcp[:].unsqueeze(2).to_broadcast([128, 16, 128]),
               ALU.mult, ['oacc', 'rcp'], ['oab'])
            if dbg:
                finals.append(DMA('pool', dbg_s[:, 0:min(L, 1024)], score[:, 0:min(L, 1024)], ['score'], [], 'dbg'))
                finals.append(DMA('pool', dbg_m[:, 0:min(L, 1024)], mmask[:, 0:min(L, 1024)], ['mmask'], [], 'dbg'))
                finals.append(DMA('pool', dbg_oa[:, :], oab, ['oab'], [], 'dbg'))
            transposes(oab, lambda c0, n: oaT[:, c0:c0 + n, tsl], 16, ['oab'], ['oaT'])

        def merge_ffn(ntile, xsrc_fn, yout_fn):
            NT = ntile * 128
            S.fence()
            DMA('sp', hTb[:, :, 0:NT], hTs[:, :, 0:NT], ['hTs'], ['hTb'], 'hTr')
            DMA('sp', obT[:, :, 0:NT], obTs[:, :, 0:NT], ['obTs'], ['obT'], 'obr')

            def ev_ga(gi, nn, b, bres):
                ACT(sA[:, gi * 4 + nn, 0:NT], b[:, 0:NT], AF.Sigmoid, [bres], ['sA'])
            gemm_feat(G_ga, hTb, 'hTb', NT, ev_ga)

            def ev_pa(gi, nn, b, bres):
                TT('dve', sA[:, gi * 4 + nn, 0:NT], b[:, 0:NT], sA[:, gi * 4 + nn, 0:NT], ALU.mult, [bres, 'sA'], ['sA'])
            gemm_feat(G_pa, oaT, 'oaT', NT, ev_pa)

            def ev_gb(gi, nn, b, bres):
                ACT(sB[:, gi * 4 + nn, 0:NT], b[:, 0:NT], AF.Sigmoid, [bres], ['sB'])
            gemm_feat(G_gb, hTb, 'hTb', NT, ev_gb)

            def ev_pb(gi, nn, b, bres):
                TT('dve', sB[:, gi * 4 + nn, 0:NT], b[:, 0:NT], sB[:, gi * 4 + nn, 0:NT], ALU.mult, [bres, 'sB'], ['sB'])
                TT('dve', mgT[:, gi * 4 + nn, 0:NT], sA[:, gi * 4 + nn, 0:NT], sB[:, gi * 4 + nn, 0:NT], ALU.add, ['sA', 'sB'], ['mgT'])
            gemm_feat(G_pb, obT, 'obT', NT, ev_pb)
            for t in range(ntile):
                DMA('sp', x2[:, t, :], xsrc_fn(t), [], ['x2_%d' % t], ('x2l', t))

            def ev_o(gi, t, b, bres, ncols):
                TT('dve', x2[:, t, gi * 512:(gi + 1) * 512], b[:, 0:512], x2[:, t, gi * 512:(gi + 1) * 512], ALU.add,
                   [bres, 'x2_%d' % t], ['x2_%d' % t])
            gemm_tok(G_o, mgT, 'mgT', ntile, ev_o)
            if _STAGE == 6.5:
                return
            S.fence()
            for t in range(ntile):
                norm_to_T(x2[:, t, :], 'x2_%d' % t, n2s, h2T, 'h2T', t, xsb, 'xsb')
            S.fence()
            DMA('sp', nfrep, nfr[:, :], [], ['nfrep'], 'nfl')

            def ev_gu(i, nn, b, bres):
                G = i // 2
                if i % 2 == 0:
                    ACT(sgt[:, nn, 0:NT], b[:, 0:NT], AF.Silu, [bres], ['sgt%d' % nn])
                else:
                    TT('dve', aT[:, G * 4 + nn, 0:NT], b[:, 0:NT], sgt[:, nn, 0:NT], ALU.mult, [bres, 'sgt%d' % nn], ['aT'])
            if _STAGE == 6.6:
                return
            gemm_feat(ffn_order, h2T, 'h2T', NT, ev_gu)
            if _STAGE == 6.7:
                return

            def ev_d(gi, t, b, bres, ncols):
                TT('dve', x2[:, t, gi * 512:(gi + 1) * 512], b[:, 0:512], x2[:, t, gi * 512:(gi + 1) * 512], ALU.add,
                   [bres, 'x2_%d' % t], ['x2_%d' % t])
                if gi == 3 and _STAGE != 6.8:
                    ACT(sgj, x2[:, t, :], AF.Square, ['x2_%d' % t], ['sgj', 'ss'], accum=stat[:, 0:1])
                    rstd_of(stat[:, 0:1], stat[:, 1:2], D, ['ss'], ['rstd'])
                    STT('dve', x2[:, t, :], x2[:, t, :], stat[:, 1:2], nfrep, ALU.mult, ALU.mult, ['x2_%d' % t, 'rstd', 'nfrep'], ['x2_%d' % t])
                    finals.append(DMA('pool', yout_fn(t), x2[:, t, :], ['x2_%d' % t], [], ('yo', t)))
            gemm_tok(G_d, aT, 'aT', ntile, ev_d)
            S.fence()

        stp = sb("stp", [128, NBIS], F32)
        stp2 = sb("stp2", [128, NBIS], F32)
        rcp = sb("rcp", [128, 16], F32)

        S.op('dve', lambda e: e.memset(Sst[:], 0.0), [], ['Sst'])
        S.op('dve', lambda e: e.memset(Sb[:], 0.0), [], ['Sb'])
        for vb in range(NV):
            own = (vb % 2 == 1)
            oi = vb // 2
            oo = None
            if own:
                oo = dict(k=lambda t, oi=oi: k_o[oi * 512 + t * 128:oi * 512 + (t + 1) * 128, :],
                          v=lambda t, oi=oi: v_o[oi * 512 + t * 128:oi * 512 + (t + 1) * 128, :],
                          ik=lambda t, oi=oi: ik_o[oi * 512 + t * 128:oi * 512 + (t + 1) * 128, :])
            xf = lambda t, vb=vb: xv[vb * 512 + t * 128:vb * 512 + (t + 1) * 128, :]
            kside(4, xf, vb * 512, vb * 4, vb * 512, oo, vb % 2)
            if (_STAGE <= 2 and vb >= _STAGEVB):
                S.emit(nc, finals); return nc
            if not own:
                for t in range(4):
                    state_update(t, vb * 4 + t)
                if (_STAGE <= 3 and vb >= _STAGEVB) or _STOPVB == vb:
                    S.emit(nc, finals); return nc
                continue
            qside_ret(4, vb * 4)
            if (_STAGE <= 4 and vb >= _STAGEVB):
                S.emit(nc, finals); return nc
            qside_proj(4, vb * 4)
            if (_STAGE <= 5 and vb >= _STAGEVB):
                S.emit(nc, finals); return nc
            S.fence()
            for t in range(4):
                attention_tile(t, [(0, vb * 512 + (t + 1) * 128)], 0, True, dbg=(_DEBUG == (vb, t)))
            if (_STAGE <= 6 and vb >= _STAGEVB):
                S.emit(nc, finals); return nc
            merge_ffn(4, xf, lambda t, oi=oi: y_o[oi * 512 + t * 128:oi * 512 + (t + 1) * 128, :])
            if (_STAGE <= 7 and vb >= _STAGEVB) or _STOPVB == vb:
                S.emit(nc, finals); return nc
        finals.append(DMA('pool', st_o[:, :, :], Sst[:], ['Sst'], [], 'sto'))
        S.fence()
        DMA('sp', Sst[:], st0[:, :, :], [], ['Sst'], 'stl')
        ACT(Sb[:], Sst[:], AF.Copy, ['Sst'], ['Sb'])
        for ct in range(8):
            tk = SOFF + ct * 128
            DMA('sp', kfst[:], ck[ct * 128:(ct + 1) * 128, :], [], ['kfst'], 'ckl')
            ACT(kbst[:], kfst[:], AF.Copy, ['kfst'], ['kbst'])
            transposes(kbst, lambda c0, n: kTst[:, c0:c0 + n, :], 4, ['kbst'], ['kTst'])
            DMA('pool', kTs[:, :, tk:tk + 128].rearrange("g p t -> p g t"), kTst[:], ['kTst'], ['scr0'], ('scw', 0))
            DMA('sp', vfst[:], cv[ct * 128:(ct + 1) * 128, :], [], ['vfst'], 'cvl')
            ACT(vbst[:, :, 0:128], vfst[:].rearrange("p (g e) -> p g e", g=4), AF.Copy, ['vfst'], ['vbst'])
            DMA('pool', Vs[tk:tk + 128, :, :], vbst[:], ['vbst'], ['scr0'], ('scw', 0))
            DMA('sp', kifst[:], ci[ct * 128:(ct + 1) * 128, :], [], ['kifst'], 'cil')
            ACT(kibst[:], kifst[:], AF.Copy, ['kifst'], ['kibst'])
            tb = trp[trc[0] % 2]; tres = 'trp%d' % (trc[0] % 2); trc[0] += 1
            TR(tb[0:64, 0:128], kibst[:], ['kibst'], [tres])
            ACT(kiTst[0:64, 0, :], tb[0:64, 0:128], AF.Copy, [tres], ['kiTst'])
            ACT(kiTst[64:128, 1, :], tb[0:64, 0:128], AF.Copy, [tres], ['kiTst'])
            DMA('pool', kiTs[:, :, tk:tk + 128].rearrange("v p t -> p v t"), kiTst[:], ['kiTst'], ['scr0'], ('scw', 0))
        soo = dict(k=lambda t: ks_o[:, :], v=lambda t: vs_o[:, :], ik=lambda t: iks_o[:, :])
        xsf = lambda t: xs[:, :]
        kside(1, xsf, SOFF + 1024, NV * 4, NV * 512, soo, 0)
        qside_ret(1, NV * 4)
        qside_proj(1, NV * 4)
        S.fence()
        attention_tile(0, [(SOFF, 1152)], 1, False)
        merge_ffn(1, xsf, lambda t: ys_o[:, :])
        finals.append(DMA('pool', sts_o[:, :, :], Sst[:], ['Sst'], [], 'sto'))
        S.emit(nc, finals)
    return nc


def _tables(NBR, h):
    NV = NBR + 1
    NTIL = NV * 4 + 1
    pos = np.zeros(NV * 512 + 128, np.float64)
    dummy = NBR if h == 1 else 0
    for v in range(NV):
        r = v if h == 1 else v - 1
        if v == dummy:
            r = 0
        pos[v * 512:(v + 1) * 512] = r * 512 + np.arange(512)
    pos[NV * 512:] = 1024 + np.arange(128)
    i64 = np.arange(64, dtype=np.float32) / 64
    i32 = np.arange(32, dtype=np.float32) / 32
    fA = (np.float32(10000.0) ** (-i64)).astype(np.float32)
    fI = (np.float32(10000.0) ** (-i32)).astype(np.float32)
    angA = pos.astype(np.float32)[:, None] * fA[None, :]
    angI = pos.astype(np.float32)[:, None] * fI[None, :]
    sc = np.float32(128 ** -0.5)
    rt = np.concatenate([np.cos(angA), np.sin(angA), np.cos(angA) * sc, np.sin(angA) * sc, np.cos(angI), np.sin(angI)], axis=1)
    lg = np.log1p(-(2.0 ** (-5.0 - np.arange(8, dtype=np.float64))))
    j = np.arange(128, dtype=np.float64)[:, None]
    kdec = np.zeros((128, NTIL, 8), np.float64)
    sdec = np.zeros((128, NTIL, 8), np.float64)
    for tl in range(NTIL):
        if tl == NTIL - 1:
            kdec[:, tl, :] = np.exp((31.0 - j) * lg[None, :]); sdec[:, tl, :] = np.exp(32.0 * lg)[None, :]
        elif tl // 4 == dummy:
            kdec[:, tl, :] = 1.0; sdec[:, tl, :] = 1.0
        else:
            kdec[:, tl, :] = np.exp((127.0 - j) * lg[None, :]); sdec[:, tl, :] = np.exp(128.0 * lg)[None, :]
    kq = np.concatenate([np.exp(-(j + 1.0) * lg[None, :]), np.exp((j + 1.0) * lg[None, :])], axis=1)
    tri = (np.arange(128)[:, None] <= np.arange(128)[None, :]).astype(np.float32)
    q = np.arange(128)[:, None]; s = np.arange(128)[None, :]
    mp = np.where((s // 64) <= (q // 64), 0.0, NEG)
    ms = np.where(s < 32, 0.0, NEG) + 0.0 * q
    maskD = np.concatenate([mp, ms], axis=1)
    mask0 = np.full((128, 512), NEG if h == 0 else 0.0)
    pow2 = np.tile((2.0 ** -(np.arange(NBIS) + 1.0))[None, :], (128, 1))
    f = np.float32
    return dict(rt=rt.astype(f), kdec=kdec.reshape(128, -1).astype(f), sdec=sdec.reshape(128, -1).astype(f), kq=kq.astype(f),
                idf=np.eye(128, dtype=f), tri=tri, maskD=maskD.astype(f), mask0=mask0.astype(f), pow2=pow2.astype(f))


_NC_CACHE = {}


def kernel(x_prompt, x_sample, cache_k, cache_v, cache_idx_k, state_ret, norm1_g, w_in, w_pa, w_pb, w_o, norm2_g,
           w_ffn_gate, w_ffn_up, w_ffn_down, norm_f_g):
    f = np.float32
    x_prompt = np.asarray(x_prompt, f); x_sample = np.asarray(x_sample, f)
    B, T, _ = x_prompt.shape
    DB, DS, _ = x_sample.shape
    NBR = T // 512
    NV = NBR + 1
    NOWN = NBR // 2
    ncore = 2 * B
    assert DB == ncore and DS == 32
    if NBR not in _NC_CACHE:
        _NC_CACHE[NBR] = build(NBR)
    nc = _NC_CACHE[NBR]
    shared = dict(
        w_in=np.ascontiguousarray(np.asarray(w_in, f)[0]), w_pa=np.ascontiguousarray(np.asarray(w_pa, f)[0]),
        w_pb=np.ascontiguousarray(np.asarray(w_pb, f)[0]), w_o=np.ascontiguousarray(np.asarray(w_o, f)[0]),
        w_g=np.ascontiguousarray(np.asarray(w_ffn_gate, f)[0]), w_u=np.ascontiguousarray(np.asarray(w_ffn_up, f)[0]),
        w_d=np.ascontiguousarray(np.asarray(w_ffn_down, f)[0]),
        n1t=np.ascontiguousarray(np.asarray(norm1_g, f)[0].reshape(16, 128).T),
        n2t=np.ascontiguousarray(np.asarray(norm2_g, f)[0].reshape(16, 128).T),
        nfr=np.ascontiguousarray(np.broadcast_to(np.asarray(norm_f_g, f)[None, :], (128, D))),
    )
    tabs = [_tables(NBR, 0), _tables(NBR, 1)]
    in_maps = []
    for c in range(ncore):
        b, h = c // 2, c % 2
        xvv = np.zeros((NV * 512, D), f)
        if h == 1:
            xvv[0:NBR * 512] = x_prompt[b]
        else:
            xvv[512:] = x_prompt[b]
        xsv = np.zeros((128, D), f); xsv[0:32] = x_sample[c]
        m = dict(shared)
        m.update(tabs[h])
        m.update(xv=xvv, xs=xsv,
                 ck=np.ascontiguousarray(np.asarray(cache_k, f)[0, c].reshape(1024, 512)),
                 cv=np.ascontiguousarray(np.asarray(cache_v, f)[0, c].reshape(1024, 512)),
                 ci=np.ascontiguousarray(np.asarray(cache_idx_k, f)[0, c]),
                 st0=np.ascontiguousarray(np.asarray(state_ret, f)[0, c].transpose(1, 0, 2)))
        in_maps.append(m)
    res = run_bass_kernel_spmd(nc, in_maps, core_ids=list(range(ncore)))
    R = res.results
    if _DEBUG is not None:
        kernel.dbg = [dict(s=r['dbg_s'], m=r['dbg_m'], oa=r['dbg_oa'], ob=r['dbg_ob']) for r in R]
    y_p = np.zeros((B, T, D), f); k_p = np.zeros((1, B, T, 4, 128), f); v_p = np.zeros((1, B, T, 4, 128), f)
    i_p = np.zeros((1, B, T, 64), f); s_p = np.zeros((1, B, 8, 128, 256), f)
    y_s = np.zeros((DB, DS, D), f); k_s = np.zeros((1, DB, DS, 4, 128), f); v_s = np.zeros((1, DB, DS, 4, 128), f)
    i_s = np.zeros((1, DB, DS, 64), f); s_s = np.zeros((1, DB, 8, 128, 256), f)
    for c in range(ncore):
        b, h = c // 2, c % 2
        r = R[c]
        for i in range(NOWN):
            rb = 2 * i + h
            sl = slice(rb * 512, (rb + 1) * 512)
            y_p[b, sl] = r["y_o"][i * 512:(i + 1) * 512]
            k_p[0, b, sl] = r["k_o"][i * 512:(i + 1) * 512].reshape(512, 4, 128)
            v_p[0, b, sl] = r["v_o"][i * 512:(i + 1) * 512].reshape(512, 4, 128)
            i_p[0, b, sl] = r["ik_o"][i * 512:(i + 1) * 512]
        if h == 0:
            s_p[0, b] = r["st_o"].transpose(1, 0, 2)
        y_s[c] = r["ys_o"][0:32]
        k_s[0, c] = r["ks_o"][0:32].reshape(32, 4, 128)
        v_s[0, c] = r["vs_o"][0:32].reshape(32, 4, 128)
        i_s[0, c] = r["iks_o"][0:32]
        s_s[0, c] = r["sts_o"].transpose(1, 0, 2)
    return (y_p, y_s, k_p, v_p, i_p, s_p, k_s, v_s, i_s, s_s)
```

```python
import contextlib
import numpy as np
import concourse.bass as bass
import concourse.mybir as mybir
from concourse.bass_utils import run_bass_kernel_spmd

F32 = mybir.dt.float32
BF16 = mybir.dt.bfloat16
ALU = mybir.AluOpType
AF = mybir.ActivationFunctionType
AX = mybir.AxisListType

D = 2048
FF = 5632
C_QA, C_KA, C_VA, C_QI, C_KI, C_WI, C_QR, C_KR, C_VR, C_GR, C_GA, C_GB = (
    0, 2048, 2560, 3072, 4096, 4160, 4176, 5200, 6224, 8272, 10320, 12368)
PTOT = 14416
NBIS = 16
NEG = -1.0e30
_STAGE = 99
_STOPVB = None
_STAGEVB = 0
_DEBUG = None


class Sched:
    ENG = ('pe', 'act', 'dve', 'pool', 'sp')

    def __init__(self):
        self.ins = {e: [] for e in self.ENG}
        self.lastw = {}
        self.readers = {}
        self.dma_cnt = {}
        self.seen = {e: {} for e in self.ENG}
        self.fence_toks = []
        self.fence_id = 0
        self.eng_fence = {e: 0 for e in self.ENG}

    def fence(self):
        toks = []
        for e in self.ENG:
            for i in range(len(self.ins[e]) - 1, -1, -1):
                if self.ins[e][i]['dma'] is None:
                    toks.append(('eng', e, i))
                    break
        for k, v in self.dma_cnt.items():
            toks.append(('dma', k, v))
        self.fence_toks = toks
        self.fence_id += 1

    def op(self, eng, fn, reads=(), writes=(), dma=None, nofence=False):
        idx = len(self.ins[eng])
        deps = []
        for r in reads:
            t = self.lastw.get(r)
            if t is not None:
                deps.append((t, 'raw'))
            if r.startswith(('rot', 'acc', 'trp')):
                for t in self.readers.get(r, ()):
                    if t[1] != eng:
                        deps.append((t, 'raw'))
        for w in writes:
            t = self.lastw.get(w)
            if t is not None:
                deps.append((t, 'waw'))
            for t in self.readers.get(w, ()):
                deps.append((t, 'war'))
        if not nofence and self.eng_fence[eng] < self.fence_id:
            self.eng_fence[eng] = self.fence_id
            for t in self.fence_toks:
                if not (t[0] == 'eng' and t[1] == eng):
                    deps.append((t, 'raw'))
        if dma is not None:
            self.dma_cnt[dma] = self.dma_cnt.get(dma, 0) + 16
            tok = ('dma', dma, self.dma_cnt[dma])
        else:
            tok = ('eng', eng, idx)
        waits = {}
        for t, kind in deps:
            if t[0] == 'eng':
                if t[1] == eng and dma is None:
                    if eng == 'pe':
                        continue
                    if kind == 'war' or idx - t[2] > 8:
                        continue
                key = ('eng', t[1])
            else:
                key = ('dma', t[1])
            val = t[2]
            if self.seen[eng].get(key, -1) >= val:
                continue
            if waits.get(key, -1) < val:
                waits[key] = val
        for k, v in waits.items():
            self.seen[eng][k] = v
        self.ins[eng].append(dict(fn=fn, waits=waits, dma=dma))
        for r in reads:
            self.readers.setdefault(r, []).append(tok)
        for w in writes:
            self.lastw[w] = tok
            self.readers[w] = []
        return tok

    def emit(self, nc, finals):
        miles = {e: set() for e in self.ENG}
        for e in self.ENG:
            for ins in self.ins[e]:
                for k, v in ins['waits'].items():
                    if k[0] == 'eng':
                        miles[k[1]].add(v)
        rank = {e: {s: i + 1 for i, s in enumerate(sorted(miles[e]))} for e in self.ENG}
        dkeys = sorted(self.dma_cnt.keys(), key=str)
        with contextlib.ExitStack() as st:
            psem = {e: st.enter_context(nc.semaphore("p_" + e)) for e in self.ENG}
            dsem = {k: st.enter_context(nc.semaphore("d_%d" % i)) for i, k in enumerate(dkeys)}
            block = st.enter_context(nc.Block())

            def run(e, eng):
                for i, ins in enumerate(self.ins[e]):
                    for k, v in ins['waits'].items():
                        if k[0] == 'eng':
                            eng.wait_ge(psem[k[1]], rank[k[1]][v])
                        else:
                            eng.wait_ge(dsem[k[1]], v)
                    bi = ins['fn'](eng)
                    if ins['dma'] is not None:
                        bi.then_inc(dsem[ins['dma']], 16)
                    elif i in rank[e]:
                        bi.then_inc(psem[e], 1)
                if e == 'sp':
                    fin = {}
                    for t in finals:
                        fin[t[1]] = max(fin.get(t[1], 0), t[2])
                    for k, v in fin.items():
                        eng.wait_ge(dsem[k], v)

            @block.tensor
            def _(eng):
                run('pe', eng)

            @block.scalar
            def _(eng):
                run('act', eng)

            @block.vector
            def _(eng):
                run('dve', eng)

            @block.gpsimd
            def _(eng):
                run('pool', eng)

            @block.sync
            def _(eng):
                run('sp', eng)


def build(NBR):
    NV = NBR + 1
    NOWN = NBR // 2
    NTIL = NV * 4 + 1
    SOFF = NV * 512
    NTOK = SOFF + 1152
    nc = bass.Bass("TRN2", target_bir_lowering=False)
    S = Sched()
    finals = []

    def din(name, shape):
        return nc.dram_tensor(name, shape, F32, kind="ExternalInput").ap()

    def dout(name, shape):
        return nc.dram_tensor(name, shape, F32, kind="ExternalOutput").ap()

    def dscr(name, shape, dt=BF16):
        return nc.dram_tensor(name, shape, dt, kind="Internal").ap()

    xv = din("xv", [NV * 512, D]); xs = din("xs", [128, D])
    ck = din("ck", [1024, 512]); cv = din("cv", [1024, 512]); ci = din("ci", [1024, 64])
    st0 = din("st0", [128, 8, 256])
    w_in = din("w_in", [D, PTOT]); w_pa = din("w_pa", [D, D]); w_pb = din("w_pb", [D, D]); w_o = din("w_o", [D, D])
    w_g = din("w_g", [D, FF]); w_u = din("w_u", [D, FF]); w_d = din("w_d", [FF, D])
    n1t = din("n1t", [128, 16]); n2t = din("n2t", [128, 16]); nfr = din("nfr", [128, D])
    rt = din("rt", [NV * 512 + 128, 320])
    kdec_d = din("kdec", [128, NTIL * 8]); sdec_d = din("sdec", [128, NTIL * 8])
    kq_d = din("kq", [128, 16])
    idf_d = din("idf", [128, 128]); tri_d = din("tri", [128, 128])
    maskD_d = din("maskD", [128, 256]); mask0_d = din("mask0", [128, 512]); pow2_d = din("pow2", [128, NBIS])

    y_o = dout("y_o", [NOWN * 512, D]); k_o = dout("k_o", [NOWN * 512, 512]); v_o = dout("v_o", [NOWN * 512, 512])
    ik_o = dout("ik_o", [NOWN * 512, 64]); st_o = dout("st_o", [128, 8, 256])
    ys_o = dout("ys_o", [128, D]); ks_o = dout("ks_o", [128, 512]); vs_o = dout("vs_o", [128, 512])
    iks_o = dout("iks_o", [128, 64]); sts_o = dout("sts_o", [128, 8, 256])

    if _DEBUG is not None:
        dbg_s = dout("dbg_s", [128, 1024]); dbg_m = dout("dbg_m", [128, 1024]); dbg_oa = dout("dbg_oa", [128, 2048]); dbg_ob = dout("dbg_ob", [128, 2048])
    kTs = dscr("kTs", [4, 128, NTOK]); Vs = dscr("Vs", [NTOK, 4, 136]); kiTs = dscr("kiTs", [2, 128, NTOK])
    hTs = dscr("hTs", [128, 16, 512]); obTs = dscr("obTs", [128, 16, 512])

    ucount = [0]

    def mkgroups(wap, K, c0, ncols_total, gw=512):
        groups = []
        kch = K // 128
        for g0 in range(0, ncols_total, gw):
            ncols = min(gw, ncols_total - g0)
            units = []
            for k0 in range(0, kch, 16):
                kc = min(16, kch - k0)
                uid = ucount[0]; ucount[0] += 1
                scr = dscr("wu%d" % uid, [128, kc, ncols])
                src = wap[k0 * 128:(k0 + kc) * 128, c0 + g0:c0 + g0 + ncols].rearrange("(c p) n -> p c n", p=128)
                units.append(dict(uid=uid, scr=scr, src=src, kc=kc, k0=k0, ncols=ncols, res="wu%d" % uid))
            groups.append(units)
        return groups

    G_ka = mkgroups(w_in, D, C_KA, 512); G_va = mkgroups(w_in, D, C_VA, 512); G_ki = mkgroups(w_in, D, C_KI, 80)
    G_kr = mkgroups(w_in, D, C_KR, 1024); G_vr = mkgroups(w_in, D, C_VR, 2048)
    G_qr = mkgroups(w_in, D, C_QR, 1024); G_gr = mkgroups(w_in, D, C_GR, 2048)
    G_qa = mkgroups(w_in, D, C_QA, 2048); G_qi = mkgroups(w_in, D, C_QI, 1024)
    G_ga = mkgroups(w_in, D, C_GA, 2048); G_pa = mkgroups(w_pa, D, 0, 2048)
    G_gb = mkgroups(w_in, D, C_GB, 2048); G_pb = mkgroups(w_pb, D, 0, 2048)
    G_o = mkgroups(w_o, D, 0, 2048)
    G_g = mkgroups(w_g, D, 0, FF); G_u = mkgroups(w_u, D, 0, FF); G_d = mkgroups(w_d, FF, 0, 2048)
    allgroups = [G_ka, G_va, G_ki, G_kr, G_vr, G_qr, G_gr, G_qa, G_qi, G_ga, G_pa, G_gb, G_pb, G_o]
    ffn_order = []
    for i in range(len(G_g)):
        ffn_order += [G_g[i], G_u[i]]

    with contextlib.ExitStack() as st:
        def sb(name, shape, dt):
            return st.enter_context(nc.sbuf_tensor("s_" + name, shape, dt))

        def pst(name, shape, dt):
            return st.enter_context(nc.psum_tensor("p_" + name, shape, dt))

        wslot = [sb("ws0", [128, 16, 512], BF16), sb("ws1", [128, 16, 512], BF16)]
        Sst = sb("Sst", [128, 8, 256], F32); Sb = sb("Sb", [128, 8, 256], BF16)
        oaT = sb("oaT", [128, 16, 512], BF16)
        identf = sb("identf", [128, 128], F32); identb = sb("identb", [128, 128], BF16)
        trif = sb("trif", [128, 128], F32); trib = sb("trib", [128, 128], BF16)
        i4big = sb("i4big", [128, 4, 128], BF16)
        kdec = sb("kdec", [128, NTIL * 8], F32); sdec = sb("sdec", [128, NTIL * 8], F32)
        kq = sb("kq", [128, 16], F32)
        n1s = sb("n1s", [128, 16], F32); n2s = sb("n2s", [128, 16], F32)
        pow2 = sb("pow2", [128, NBIS], F32)
        maskD = sb("maskD", [128, 256], F32); mask0 = sb("mask0", [128, 512], F32)
        rts = [sb("rt0", [128, 320], F32), sb("rt1", [128, 320], F32)]
        rt1 = sb("rtmp1", [128, 256], F32); rt2 = sb("rtmp2", [128, 256], F32)
        stat = sb("stat", [128, 64], F32)
        wab = sb("wab", [128, 4, 16], F32); wsg = sb("wsg", [128, 4, 16], F32)
        UB = 114688
        U = sb("U", [128, UB // 4], F32)

        def uv(off, nbytes, dt, pat=None, **kw):
            a = U[:, off // 4:(off + nbytes) // 4]
            if dt == BF16:
                a = a.bitcast(BF16)
            if pat:
                a = a.rearrange(pat, **kw)
            return a

        K1 = 1024
        qaT = uv(0, 16 * K1, BF16, "p (c t) -> p c t", c=16)
        qiT = uv(16 * K1, 8 * K1, BF16, "p (c t) -> p c t", c=8)
        hT = uv(24 * K1, 16 * K1, BF16, "p (c t) -> p c t", c=16)
        xst = uv(40 * K1, 8 * K1, F32)
        kd = uv(48 * K1, 8 * K1, BF16, "p (t c) -> p t c", t=4)
        kpT = uv(56 * K1, 8 * K1, BF16, "p (c t) -> p c t", c=8)
        qrT = uv(64 * K1, 8 * K1, BF16, "p (c t) -> p c t", c=8)
        vr = uv(72 * K1, 16 * K1, BF16, "p (t c) -> p t c", t=4)
        grs = uv(88 * K1, 16 * K1, BF16, "p (t c) -> p t c", t=4)
        obb = uv(104 * K1, 4 * K1, BF16)
        obst = uv(40 * K1, 4 * K1, BF16, "p (c t) -> p c t", c=16)
        innT = uv(108 * K1, 2 * K1, BF16, "p (h t) -> p h t", h=8)
        score = uv(24 * K1, 32 * K1, F32)
        mmask = uv(56 * K1, 16 * K1, BF16)
        kvk = [uv(72 * K1 + i * 8704, 4096, BF16, "p (g t) -> p g t", g=4) for i in range(2)]
        kvv = [uv(72 * K1 + i * 8704 + 4096, 4352, BF16, "p (c g e) -> p c g e", c=4, g=4) for i in range(2)]
        o3 = 72 * K1 + 2 * 8704
        kis = [uv(o3 + i * 2 * K1, 2 * K1, BF16, "p (v t) -> p v t", v=2) for i in range(2)]
        rbuf = [uv(o3 + 4 * K1 + i * 2 * K1, 2 * K1, F32) for i in range(2)]
        PTb = [uv(o3 + 8 * K1 + i * K1, K1, BF16) for i in range(2)]
        oacc = uv(o3 + 10 * K1, 8256, F32, "p (h e) -> p h e", h=16)
        oab = uv(o3 + 10 * K1 + 8256, 4 * K1, BF16)
        hTb = uv(0, 16 * K1, BF16, "p (c t) -> p c t", c=16)
        obT = uv(16 * K1, 16 * K1, BF16, "p (c t) -> p c t", c=16)
        sA = uv(32 * K1, 16 * K1, BF16, "p (c t) -> p c t", c=16)
        sB = uv(48 * K1, 16 * K1, BF16, "p (c t) -> p c t", c=16)
        mgT = uv(64 * K1, 16 * K1, BF16, "p (c t) -> p c t", c=16)
        x2 = uv(80 * K1, 32 * K1, F32, "p (t c) -> p t c", t=4)
        xsb = uv(32 * K1, 4 * K1, BF16)
        h2T = uv(0, 16 * K1, BF16, "p (c t) -> p c t", c=16)
        aT = uv(16 * K1, 44 * K1, BF16, "p (c t) -> p c t", c=44)
        sgt = uv(60 * K1, 4 * K1, BF16, "p (c t) -> p c t", c=4)
        nfrep = uv(64 * K1, 8 * K1, F32)
        sgj = uv(60 * K1, 4 * K1, BF16)
        kfst = sb("kfst", [128, 512], F32); vfst = sb("vfst", [128, 512], F32); kifst = sb("kifst", [128, 64], F32)
        kbst = sb("kbst", [128, 512], BF16); kTst = sb("kTst", [128, 4, 128], BF16)
        vbst = sb("vbst", [128, 4, 136], BF16); kibst = sb("kibst", [128, 64], BF16); kiTst = sb("kiTst", [128, 2, 128], BF16)
        qst = sb("qst", [128, 512], BF16)

        rot = [pst("rot%d" % i, [128, 512], F32) for i in range(4)]
        trp = [pst("trp%d" % i, [128, 1024], BF16) for i in range(2)]
        acc = [pst("acc%d" % i, [128, 512], F32) for i in range(2)]
        trc = [0]
        accl = [acc[0], acc[1], rot[2], rot[3]]
        accr = ['acc0', 'acc1', 'rot2', 'rot3']
        agc = [0]

        def MM(out, lhsT, rhs, start, stop, R, W, sgc=False):
            if sgc:
                S.op('pe', lambda e: e.matmul(out, lhsT, rhs, start=start, stop=stop, skip_group_check=True), R, W)
            else:
                S.op('pe', lambda e: e.matmul(out, lhsT, rhs, start=start, stop=stop), R, W)

        def TR(out, in_, R, W):
            S.op('pe', lambda e: e.transpose(out=out, in_=in_, identity=identb[:]), list(R) + ['identb'], W)

        def ACT(out, in_, func, R, W, scale=None, bias=None, accum=None):
            kw = {}
            if scale is not None:
                kw['scale'] = scale
            if bias is not None:
                kw['bias'] = bias
            if accum is not None:
                kw['accum_out'] = accum
            S.op('act', lambda e: e.activation(out=out, in_=in_, func=func, **kw), R, W)

        def TS(eng, out, in0, s1, s2, op0, op1, R, W, accum=None):
            kw = {}
            if op1 is not None:
                kw['op1'] = op1
            if accum is not None:
                kw['accum_out'] = accum
            S.op(eng, lambda e: e.tensor_scalar(out=out, in0=in0, scalar1=s1, scalar2=s2, op0=op0, **kw), R, W)

        def TT(eng, out, in0, in1, op, R, W):
            S.op(eng, lambda e: e.tensor_tensor(out=out, in0=in0, in1=in1, op=op), R, W)

        def STT(eng, out, in0, scalar, in1, op0, op1, R, W):
            S.op(eng, lambda e: e.scalar_tensor_tensor(out=out, in0=in0, scalar=scalar, in1=in1, op0=op0, op1=op1), R, W)

        def DMA(eng, out, in_, R, W, key, nofence=False):
            return S.op(eng, lambda e: e.dma_start(out=out, in_=in_), R, W, dma=key, nofence=nofence)

        for dst, src, nm in ((identf[:], idf_d, 'identf'), (trif[:], tri_d, 'trif'), (kdec[:], kdec_d, 'kdec'),
                             (sdec[:], sdec_d, 'sdec'), (kq[:], kq_d, 'kq'), (n1s[:], n1t, 'n1s'), (n2s[:], n2t, 'n2s'),
                             (pow2[:], pow2_d, 'pow2'), (maskD[:], maskD_d, 'maskD'), (mask0[:], mask0_d, 'mask0')):
            DMA('sp', dst, src[:, :], [], [nm], 'const', nofence=True)
        ACT(identb[:], identf[:], AF.Copy, ['identf'], ['identb'])
        ACT(trib[:], trif[:], AF.Copy, ['trif'], ['trib'])
        for j in range(4):
            ACT(i4big[:, j, :], identf[:], AF.Copy, ['identf'], ['i4big'], scale=30000.0)
        S.op('dve', lambda e: e.memset(stat[:, 60:61], 1e-6), [], ['eps'])
        S.op('dve', lambda e: e.memset(vbst[:], 1.0), [], ['vbst'])
        S.op('dve', lambda e: e.memset(kiTst[:], 0.0), [], ['kiTst'])

        cast_order = []
        for G in allgroups:
            for units in G:
                cast_order += units
        for units in ffn_order:
            cast_order += units
        for units in G_d:
            cast_order += units
        for i, u in enumerate(cast_order):
            DMA('pool', u['scr'][:, :, :], u['src'], [], [u['res']], ('wc', i % 8), nofence=True)

        if _STAGE <= 1:
            S.emit(nc, finals); return nc
        wcnt = [0]

        def wload(u):
            s = wcnt[0] % 2
            wcnt[0] += 1
            DMA('sp', wslot[s][:, 0:u['kc'], 0:u['ncols']], u['scr'][:, :, :], [u['res']], ['ws%d' % s], ('wl', s), nofence=True)
            return s

        def stream(ulist, body):
            slots = {}
            if ulist:
                slots[0] = wload(ulist[0])
            for i, u in enumerate(ulist):
                if i + 1 < len(ulist):
                    slots[i + 1] = wload(ulist[i + 1])
                s = slots[i]
                body(i, u, wslot[s], 'ws%d' % s)

        defq = []

        def run_deferred():
            while defq:
                defq.pop(0)()

        def gemm_tok(groups, actT, actres, ntile, evac):
            flat = []
            for gi, units in enumerate(groups):
                for ui, u in enumerate(units):
                    flat.append((gi, ui, len(units), u))

            def body(i, u, slot, sres):
                gi, ui, nu, _ = flat[i]
                for t in range(ntile):
                    b = rot[t]
                    for c in range(u['kc']):
                        MM(b[:, 0:u['ncols']], actT[:, u['k0'] + c, t * 128:(t + 1) * 128], slot[:, c, 0:u['ncols']],
                           (ui == 0 and c == 0), (ui == nu - 1 and c == u['kc'] - 1), [actres, sres], ['rot%d' % t])
                    run_deferred()
                    if ui == nu - 1:
                        evac(gi, t, b, 'rot%d' % t, u['ncols'])
            stream([f[3] for f in flat], body)
            run_deferred()

        def gemm_feat(groups, actT, actres, NT, evac):
            flat = [units[0] for units in groups]

            def body(i, u, slot, sres):
                for nn in range(u['ncols'] // 128):
                    b = rot[nn]
                    for c in range(16):
                        MM(b[:, 0:NT], slot[:, c, nn * 128:(nn + 1) * 128], actT[:, c, 0:NT], c == 0, c == 15,
                           [actres, sres], ['rot%d' % nn])
                    evac(i, nn, b, 'rot%d' % nn)
            stream(flat, body)

        def transposes(src, dstfn, nchunk, R, W, evac_eng='act', scale_ap=None):
            for c0 in range(0, nchunk, 8):
                n = min(8, nchunk - c0)
                tb = trp[trc[0] % 2]; tres = 'trp%d' % (trc[0] % 2); trc[0] += 1
                for j in range(n):
                    TR(tb[:, j * 128:(j + 1) * 128], src[:, (c0 + j) * 128:(c0 + j + 1) * 128], R, [tres])
                dst = dstfn(c0, n)
                srcv = tb[:, 0:n * 128].rearrange("p (c t) -> p c t", c=n)
                if scale_ap is not None:
                    TT('dve', dst, srcv, scale_ap[:, c0:c0 + n].unsqueeze(2).to_broadcast([128, n, 128]), ALU.mult,
                       [tres] + list(R), W)
                elif evac_eng == 'act':
                    ACT(dst, srcv, AF.Copy, [tres], W)
                else:
                    S.op('dve', lambda e: e.tensor_copy(out=dst, in_=srcv), [tres], W)

        def rope(ps, psres, nh, hd, cos, sin, d, W, rsl):
            half = hd // 2
            x = ps.rearrange("p (h t d) -> p h t d", h=nh, t=2)
            dv = d.rearrange("p (h t d) -> p h t d", h=nh, t=2)
            t1 = rt1[:, 0:nh * half].rearrange("p (h d) -> p h d", h=nh)
            t2 = rt2[:, 0:nh * half].rearrange("p (h d) -> p h d", h=nh)
            cb = cos.unsqueeze(1).to_broadcast([128, nh, half])
            sbb = sin.unsqueeze(1).to_broadcast([128, nh, half])
            TT('dve', t1, x[:, :, 0, :], cb, ALU.mult, [psres, rsl], ['rt1'])
            TT('dve', t2, x[:, :, 1, :], sbb, ALU.mult, [psres, rsl], ['rt2'])
            TT('dve', dv[:, :, 0, :], t1, t2, ALU.subtract, ['rt1', 'rt2'], W)
            TT('dve', t1, x[:, :, 1, :], cb, ALU.mult, [psres, rsl], ['rt1'])
            TT('dve', t2, x[:, :, 0, :], sbb, ALU.mult, [psres, rsl], ['rt2'])
            TT('dve', dv[:, :, 1, :], t1, t2, ALU.add, ['rt1', 'rt2'], W)

        def rstd_of(ss_col, out_col, n, R, W):
            ACT(stat[:, 62:63], ss_col, AF.Sqrt, list(R) + ['eps'], ['sq'], scale=1.0 / n, bias=stat[:, 60:61])
            S.op('dve', lambda e: e.reciprocal(out=out_col, in_=stat[:, 62:63]), ['sq'], W)

        def norm_to_T(xt, xres, gts, dstT, dres, t, tmpb, tmpres):
            ACT(tmpb, xt, AF.Square, [xres], [tmpres, 'ss'], accum=stat[:, 0:1])
            rstd_of(stat[:, 0:1], stat[:, 1:2], D, ['ss'], ['rstd'])
            ACT(tmpb, xt, AF.Copy, [xres, 'rstd'], [tmpres], scale=stat[:, 1:2])
            transposes(tmpb, lambda c0, n: dstT[:, c0:c0 + n, t * 128:(t + 1) * 128], 16, [tmpres], [dres], scale_ap=gts)

        rtc = [0]

        def ropeload_g(gt):
            sl = rtc[0] % 2
            rtc[0] += 1
            rr = gt * 128
            DMA('sp', rts[sl][:], rt[rr:rr + 128, :], [], ['rts%d' % sl], ('rt', sl))
            return rts[sl], 'rts%d' % sl

        def kside(ntile, xsrc_fn, tok0, tile0, rrow0, own_out, spar):
            NT = ntile * 128
            scr_key = ('scw', spar)
            sres = 'scr%d' % spar
            for t in range(ntile):
                DMA('sp', xst, xsrc_fn(t), [], ['xst'], 'xst')
                norm_to_T(xst, 'xst', n1s, hT, 'hT', t, obb, 'obb')

            def ropeload(t):
                return ropeload_g(tile0 + t)

            def ev_ka(gi, t, b, bres, ncols):
                r, rres = ropeload(t)
                rope(b[:, 0:512], bres, 4, 128, r[:, 0:64], r[:, 64:128], kfst[:], ['kfst'], rres)
                if own_out:
                    finals.append(DMA('pool', own_out['k'](t), kfst[:], ['kfst'], [], 'ko'))
                ACT(kbst[:], kfst[:], AF.Copy, ['kfst'], ['kbst'])

                def later():
                    transposes(kbst, lambda c0, n: kTst[:, c0:c0 + n, :], 4, ['kbst'], ['kTst'])
                    tk = tok0 + t * 128
                    DMA('pool', kTs[:, :, tk:tk + 128].rearrange("g p t -> p g t"), kTst[:], ['kTst'], [sres], scr_key)
                defq.append(later)
            gemm_tok(G_ka, hT, 'hT', ntile, ev_ka)

            def ev_va(gi, t, b, bres, ncols):
                ACT(vfst[:], b[:, 0:512], AF.Copy, [bres], ['vfst'])
                if own_out:
                    finals.append(DMA('pool', own_out['v'](t), vfst[:], ['vfst'], [], 'vo'))
                ACT(vbst[:, :, 0:128], b[:, 0:512].rearrange("p (g e) -> p g e", g=4), AF.Copy, [bres], ['vbst'])
                tk = tok0 + t * 128
                DMA('pool', Vs[tk:tk + 128, :, :], vbst[:], ['vbst'], [sres], scr_key)
            gemm_tok(G_va, hT, 'hT', ntile, ev_va)

            def ev_ki(gi, t, b, bres, ncols):
                r, rres = ropeload(t)
                rope(b[:, 0:64], bres, 1, 64, r[:, 256:288], r[:, 288:320], kifst[:], ['kifst'], rres)
                if own_out:
                    finals.append(DMA('pool', own_out['ik'](t), kifst[:], ['kifst'], [], 'iko'))
                    ACT(wab[:, t, :], b[:, 64:80], AF.Abs, [bres], ['wab'], scale=1.0 / 32.0)
                    ACT(wsg[:, t, :], b[:, 64:80], AF.Sign, [bres], ['wsg'])
                ACT(kibst[:], kifst[:], AF.Copy, ['kifst'], ['kibst'])

                def later():
                    tb = trp[trc[0] % 2]; tres = 'trp%d' % (trc[0] % 2); trc[0] += 1
                    TR(tb[0:64, 0:128], kibst[:], ['kibst'], [tres])
                    ACT(kiTst[0:64, 0, :], tb[0:64, 0:128], AF.Copy, [tres], ['kiTst'])
                    ACT(kiTst[64:128, 1, :], tb[0:64, 0:128], AF.Copy, [tres], ['kiTst'])
                    tk = tok0 + t * 128
                    DMA('pool', kiTs[:, :, tk:tk + 128].rearrange("v p t -> p v t"), kiTst[:], ['kiTst'], [sres], scr_key)
                defq.append(later)
            gemm_tok(G_ki, hT, 'hT', ntile, ev_ki)

            def ev_kr(gi, t, b, bres, ncols):
                r, rres = ropeload(t)
                rope(b[:, 0:512], bres, 4, 128, r[:, 128:192], r[:, 192:256], kfst[:], ['kfst'], rres)
                krf = kfst[:].rearrange("p (h d) -> p h d", h=4)
                kdv = kdec[:, (tile0 + t) * 8 + gi * 4:(tile0 + t) * 8 + gi * 4 + 4].unsqueeze(2).to_broadcast([128, 4, 128])
                TT('dve', kd[:, t, gi * 512:(gi + 1) * 512].rearrange("p (h d) -> p h d", h=4), krf, kdv, ALU.mult,
                   ['kfst', 'kdec'], ['kd'])
                if own_out:
                    kiv = kq[:, gi * 4:gi * 4 + 4].unsqueeze(2).to_broadcast([128, 4, 128])
                    TT('dve', qst[:].rearrange("p (h d) -> p h d", h=4), krf, kiv, ALU.mult, ['kfst', 'kq'], ['qst'])
                    defq.append(lambda: transposes(qst, lambda c0, n: kpT[:, gi * 4 + c0:gi * 4 + c0 + n, t * 128:(t + 1) * 128], 4,
                                                   ['qst'], ['kpT']))
            gemm_tok(G_kr, hT, 'hT', ntile, ev_kr)

            def ev_vr(gi, t, b, bres, ncols):
                ACT(vr[:, t, gi * 512:(gi + 1) * 512], b[:, 0:512], AF.Copy, [bres], ['vr'])
            gemm_tok(G_vr, hT, 'hT', ntile, ev_vr)

        def state_update(t, gt):
            for hp in range(4):
                b = rot[hp]; bres = 'rot%d' % hp
                for hh in range(2):
                    h = hp * 2 + hh
                    MM(b[:, hh * 256:(hh + 1) * 256], kd[:, t, h * 128:(h + 1) * 128], vr[:, t, h * 256:(h + 1) * 256],
                       True, True, ['kd', 'vr'], [bres])
                for hh in range(2):
                    h = hp * 2 + hh
                    STT('dve', Sst[:, h, :], Sst[:, h, :], sdec[:, gt * 8 + h:gt * 8 + h + 1], b[:, hh * 256:(hh + 1) * 256],
                        ALU.mult, ALU.add, ['Sst', bres, 'sdec'], ['Sst'])
            ACT(Sb[:], Sst[:], AF.Copy, ['Sst'], ['Sb'])

        def qside_ret(ntile, tile0):
            def ev_qr(gi, t, b, bres, ncols):
                r, rres = ropeload_g(tile0 + t)
                rope(b[:, 0:512], bres, 4, 128, r[:, 0:64], r[:, 64:128], qst[:], ['qst'], rres)
                defq.append(lambda: transposes(qst, lambda c0, n: qrT[:, gi * 4 + c0:gi * 4 + c0 + n, t * 128:(t + 1) * 128], 4, ['qst'], ['qrT']))

            gemm_tok(G_qr, hT, 'hT', ntile, ev_qr)

            def ev_gr(gi, t, b, bres, ncols):
                ACT(grs[:, t, gi * 512:(gi + 1) * 512], b[:, 0:512], AF.Silu, [bres], ['grs'])
            gemm_tok(G_gr, hT, 'hT', ntile, ev_gr)
            for t in range(ntile):
                tsl = slice(t * 128, (t + 1) * 128)
                for hq in range(2):
                    b = rot[hq]; bres = 'rot%d' % hq
                    for hh in range(4):
                        h = hq * 4 + hh
                        MM(b[:, hh * 128:(hh + 1) * 128], kpT[:, h, tsl], qrT[:, h, tsl], True, True, ['kpT', 'qrT'], [bres])
                    TT('dve', innT[:, hq * 4:hq * 4 + 4, :], b[:, 0:512].rearrange("p (h t) -> p h t", h=4),
                       trib[:].unsqueeze(1).to_broadcast([128, 4, 128]), ALU.mult, [bres, 'trib'], ['innT'])
                for hp in range(4):
                    b = rot[hp]; bres = 'rot%d' % hp
                    for hh in range(2):
                        h = hp * 2 + hh
                        MM(b[:, hh * 256:(hh + 1) * 256], innT[:, h, :], vr[:, t, h * 256:(h + 1) * 256], True, False,
                           ['innT', 'vr'], [bres])
                        MM(b[:, hh * 256:(hh + 1) * 256], qrT[:, h, tsl], Sb[:, h, :], False, True, ['qrT', 'Sb'], [bres])
                    for hh in range(2):
                        h = hp * 2 + hh
                        ACT(rt1[:, 0:256], b[:, hh * 256:(hh + 1) * 256], AF.Square, [bres, 'kq'], ['rt1', 'gss'],
                            scale=kq[:, 8 + h:9 + h], accum=stat[:, 8 + h:9 + h])
                ACT(stat[:, 16:24], stat[:, 8:16], AF.Sqrt, ['gss', 'eps'], ['gsq'], scale=1.0 / 256, bias=stat[:, 60:61])
                S.op('dve', lambda e: e.reciprocal(out=stat[:, 24:32], in_=stat[:, 16:24]), ['gsq'], ['grc'])
                TT('dve', stat[:, 32:40], stat[:, 24:32], kq[:, 8:16], ALU.mult, ['grc', 'kq'], ['gc'])
                for hp in range(4):
                    b = rot[hp]; bres = 'rot%d' % hp
                    for hh in range(2):
                        h = hp * 2 + hh
                        STT('dve', obb[:, h * 256:(h + 1) * 256], b[:, hh * 256:(hh + 1) * 256], stat[:, 32 + h:33 + h],
                            grs[:, t, h * 256:(h + 1) * 256], ALU.mult, ALU.mult, [bres, 'gc', 'grs'], ['obb'])
                if _DEBUG is not None and _DEBUG == (tile0 // 4, t) and ntile == 4:
                    finals.append(DMA('pool', dbg_ob[:, :], obb, ['obb'], [], 'dbg'))
                transposes(obb, lambda c0, n: obst[:, c0:c0 + n, :], 16, ['obb'], ['obst'])
                DMA('pool', obTs[:, :, t * 128:(t + 1) * 128], obst[:], ['obst'], ['obTs'], 'obw')
                state_update(t, tile0 + t)

        def qside_proj(ntile, tile0):
            def ev_qa(gi, t, b, bres, ncols):
                r, rres = ropeload_g(tile0 + t)
                rope(b[:, 0:512], bres, 4, 128, r[:, 0:64], r[:, 64:128], qst[:], ['qst'], rres)
                defq.append(lambda: transposes(qst, lambda c0, n: qaT[:, gi * 4 + c0:gi * 4 + c0 + n, t * 128:(t + 1) * 128], 4, ['qst'], ['qaT']))
            gemm_tok(G_qa, hT, 'hT', ntile, ev_qa)

            def ev_qi(gi, t, b, bres, ncols):
                r, rres = ropeload_g(tile0 + t)
                rope(b[:, 0:512], bres, 8, 64, r[:, 256:288], r[:, 288:320], qst[:], ['qst'], rres)
                defq.append(lambda: transposes(qst, lambda c0, n: qiT[:, gi * 4 + c0:gi * 4 + c0 + n, t * 128:(t + 1) * 128], 4, ['qst'], ['qiT']))
            gemm_tok(G_qi, hT, 'hT', ntile, ev_qi)
            NT = ntile * 128
            DMA('pool', hTs[:, :, 0:NT], hT[:, :, 0:NT], ['hT'], ['hTs'], 'hTw')

        def attention_tile(t, key_segs, mtype, use_mask0, dbg=False):
            sgs = []
            pos = 0
            for (o, n) in key_segs:
                for a in range(0, n, 512):
                    w = min(512, n - a)
                    sgs.append((o + a, w, pos)); pos += w
            L = pos
            tsl = slice(t * 128, (t + 1) * 128)
            for si, (so, w, p0) in enumerate(sgs):
                sl = si % 2
                DMA('sp', kis[sl][:, :, 0:w], kiTs[:, :, so:so + w].rearrange("v p t -> p v t"), ['scr0', 'scr1'], ['kis%d' % sl], ('kis', sl))
                for j in range(16):
                    hf = j % 2
                    b = rot[j % 4]; bres = 'rot%d' % (j % 4)
                    MM(b[:, 0:w], qiT[:, j // 2, tsl], kis[sl][:, hf, 0:w], True, True, ['qiT', 'kis%d' % sl], [bres])
                    rb = rbuf[j % 2]; rres = 'rb%d' % (j % 2)
                    ACT(rb[:, 0:w], b[:, 0:w], AF.Relu, [bres, 'wab'], [rres], scale=wab[:, t, j:j + 1])
                    if j == 0:
                        TS('dve', score[:, p0:p0 + w], rb[:, 0:w], wsg[:, t, 0:1], None, ALU.mult, None, [rres, 'wsg'], ['score'])
                    else:
                        STT('dve', score[:, p0:p0 + w], rb[:, 0:w], wsg[:, t, j:j + 1], score[:, p0:p0 + w], ALU.mult, ALU.add,
                            [rres, 'wsg', 'score'], ['score'])
            if _STAGE == 5.1:
                return
            lo, hi, rng, mid, cnt, tmp = (stat[:, 40:41], stat[:, 41:42], stat[:, 42:43], stat[:, 43:44], stat[:, 44:45], stat[:, 45:46])
            S.op('dve', lambda e: e.tensor_reduce(out=lo, in_=score[:, 0:L], axis=AX.X, op=ALU.min), ['score'], ['lo'])
            S.op('dve', lambda e: e.tensor_reduce(out=hi, in_=score[:, 0:L], axis=AX.X, op=ALU.max), ['score'], ['hi'])
            TT('dve', rng, hi, lo, ALU.subtract, ['lo', 'hi'], ['rng'])
            TS('dve', stp[:], pow2[:], rng, None, ALU.mult, None, ['rng', 'pow2'], ['stp'])
            TS('dve', stp2[:], pow2[:], rng, 2.0, ALU.mult, ALU.mult, ['rng', 'pow2'], ['stp'])
            TS('dve', stp2[:, 0:1], stp[:, NBIS - 1:NBIS], 1.125, None, ALU.mult, None, ['stp'], ['stp'])
            TT('dve', score[:, L - 128:L], score[:, L - 128:L], maskD[:, mtype * 128:(mtype + 1) * 128], ALU.add,
               ['score', 'maskD'], ['score'])
            if use_mask0:
                TT('dve', score[:, 0:512], score[:, 0:512], mask0[:], ALU.add, ['score', 'mask0'], ['score'])
            TT('dve', mid, lo, stp[:, 0:1], ALU.add, ['lo', 'stp'], ['mid'])
            Ld = max(128, int(round(L * 0.47 / 128.0)) * 128)
            if L - Ld < 256:
                Ld = L
            nact = L - Ld
            thr = 256.0 - 0.5 * nact
            sacc, t2 = stat[:, 46:47], stat[:, 47:48]
            for k in range(NBIS):
                TS('dve', mmask[:, 0:Ld], score[:, 0:Ld], mid, None, ALU.is_ge, ALU.add, ['score', 'mid'], ['mmask', 'cnt'], accum=cnt)
                if nact:
                    ACT(mmask[:, Ld:L], score[:, Ld:L], AF.Sign, ['score', 'mid'], ['mmaskA', 'sacc'], scale=-1.0, bias=mid, accum=sacc)
                    STT('dve', t2, sacc, -0.5, cnt, ALU.mult, ALU.add, ['sacc', 'cnt'], ['t2'])
                    csrc, cres = t2, 't2'
                else:
                    csrc, cres = cnt, 'cnt'
                if k + 1 < NBIS:
                    STT('dve', tmp, csrc, thr, stp2[:, k + 1:k + 2], ALU.is_ge, ALU.mult, [cres, 'stp'], ['tmp'])
                    STT('dve', mid, tmp, stp[:, k + 1:k + 2], mid, ALU.subtract, ALU.add, ['tmp', 'stp', 'mid'], ['mid'])
                else:
                    STT('dve', tmp, csrc, thr, stp2[:, 0:1], ALU.is_ge, ALU.mult, [cres, 'stp'], ['tmp'])
                    STT('dve', lo, tmp, stp2[:, 0:1], mid, ALU.subtract, ALU.add, ['tmp', 'stp', 'mid'], ['lo'])
            TS('dve', mmask[:, 0:L], score[:, 0:L], lo, -1.0, ALU.is_ge, ALU.add, ['score', 'lo'], ['mmask', 'mmaskA'])
            if _STAGE == 5.2:
                return
            for si, (so, w, p0) in enumerate(sgs):
                sl = si % 2
                nch = w // 128
                DMA('sp', kvk[sl][:, :, 0:w], kTs[:, :, so:so + w].rearrange("g p t -> p g t"), ['scr0', 'scr1'],
                    ['kvk%d' % sl], ('kvk', sl))
                DMA('sp', kvv[sl][:, 0:nch, :, :], Vs[so:so + w, :, :].rearrange("(c p) g e -> p c g e", p=128),
                    ['scr0', 'scr1'], ['kvv%d' % sl], ('kvv', sl))
                items = [(g, c) for g in range(4) for c in range(nch)]

                def emit_S(ix):
                    g, c = items[ix]
                    b = rot[ix % 2]; bres = 'rot%d' % (ix % 2)
                    MM(b[:, 0:512], kvk[sl][:, g, c * 128:(c + 1) * 128], qaT[:, 4 * g:4 * g + 4, tsl], True, False,
                       ['kvk%d' % sl, 'qaT'], [bres])
                    MM(b[:, 0:512], mmask[:, p0 + c * 128:p0 + (c + 1) * 128], i4big[:], False, True, ['mmask', 'i4big'], [bres])
                emit_S(0)
                for ix, (g, c) in enumerate(items):
                    b = rot[ix % 2]; bres = 'rot%d' % (ix % 2)
                    pb = PTb[ix % 2]; pres = 'PT%d' % (ix % 2)
                    ACT(pb, b[:, 0:512], AF.Exp, [bres], [pres], scale=float(128 ** -0.5))
                    if ix + 1 < len(items):
                        emit_S(ix + 1)
                    aset = (agc[0] % 2) * 2
                    for hh in range(4):
                        a = accl[aset + hh // 2]; ares = accr[aset + hh // 2]
                        MM(a[:, (hh % 2) * 256:(hh % 2) * 256 + 129], pb[:, hh * 128:(hh + 1) * 128], kvv[sl][:, c, g, 0:129],
                           (c == 0 and hh % 2 == 0), c == nch - 1, [pres, 'kvv%d' % sl], [ares], sgc=True)
                    if c == nch - 1:
                        agc[0] += 1
                        for hp in range(2):
                            a = accl[aset + hp]; ares = accr[aset + hp]
                            dst = oacc[:, 4 * g + 2 * hp:4 * g + 2 * hp + 2, :]
                            srcv = a[:, 0:512].rearrange("p (h e) -> p h e", h=2)[:, :, 0:129]
                            if si == 0:
                                S.op('dve', lambda e, dst=dst, srcv=srcv: e.tensor_copy(out=dst, in_=srcv), [ares], ['oacc'])
                            else:
                                TT('dve', dst, srcv, dst, ALU.add, [ares, 'oacc'], ['oacc'])
            S.op('dve', lambda e: e.reciprocal(out=rcp[:], in_=oacc[:, :, 128]), ['oacc'], ['rcp'])
            TT('dve', oab.rearrange("p (h d) -> p h d", h=16), oacc[:, :, 0:128], rcp[:].unsqueeze(2).to_broadcast([128, 16, 128]),
               ALU.mult, ['oacc', 'rcp'], ['oab'])
            if dbg:
                finals.append(DMA('pool', dbg_s[:, 0:min(L, 1024)], score[:, 0:min(L, 1024)], ['score'], [], 'dbg'))
                finals.append(DMA('pool', dbg_m[:, 0:min(L, 1024)], mmask[:, 0:min(L, 1024)], ['mmask'], [], 'dbg'))
                finals.append(DMA('pool', dbg_oa[:, :], oab, ['oab'], [], 'dbg'))
            transposes(oab, lambda c0, n: oaT[:, c0:c0 + n, tsl], 16, ['oab'], ['oaT'])

        def merge_ffn(ntile, xsrc_fn, yout_fn):
            NT = ntile * 128
            S.fence()
            DMA('sp', hTb[:, :, 0:NT], hTs[:, :, 0:NT], ['hTs'], ['hTb'], 'hTr')
            DMA('sp', obT[:, :, 0:NT], obTs[:, :, 0:NT], ['obTs'], ['obT'], 'obr')

            def ev_ga(gi, nn, b, bres):
                ACT(sA[:, gi * 4 + nn, 0:NT], b[:, 0:NT], AF.Sigmoid, [bres], ['sA'])
            gemm_feat(G_ga, hTb, 'hTb', NT, ev_ga)

            def ev_pa(gi, nn, b, bres):
                TT('dve', sA[:, gi * 4 + nn, 0:NT], b[:, 0:NT], sA[:, gi * 4 + nn, 0:NT], ALU.mult, [bres, 'sA'], ['sA'])
            gemm_feat(G_pa, oaT, 'oaT', NT, ev_pa)

            def ev_gb(gi, nn, b, bres):
                ACT(sB[:, gi * 4 + nn, 0:NT], b[:, 0:NT], AF.Sigmoid, [bres], ['sB'])
            gemm_feat(G_gb, hTb, 'hTb', NT, ev_gb)

            def ev_pb(gi, nn, b, bres):
                TT('dve', sB[:, gi * 4 + nn, 0:NT], b[:, 0:NT], sB[:, gi * 4 + nn, 0:NT], ALU.mult, [bres, 'sB'], ['sB'])
                TT('dve', mgT[:, gi * 4 + nn, 0:NT], sA[:, gi * 4 + nn, 0:NT], sB[:, gi * 4 + nn, 0:NT], ALU.add, ['sA', 'sB'], ['mgT'])
            gemm_feat(G_pb, obT, 'obT', NT, ev_pb)
            for t in range(ntile):
                DMA('sp', x2[:, t, :], xsrc_fn(t), [], ['x2_%d' % t], ('x2l', t))

            def ev_o(gi, t, b, bres, ncols):
                TT('dve', x2[:, t, gi * 512:(gi + 1) * 512], b[:, 0:512], x2[:, t, gi * 512:(gi + 1) * 512], ALU.add,
                   [bres, 'x2_%d' % t], ['x2_%d' % t])
            gemm_tok(G_o, mgT, 'mgT', ntile, ev_o)
            if _STAGE == 6.5:
                return
            S.fence()
            for t in range(ntile):
                norm_to_T(x2[:, t, :], 'x2_%d' % t, n2s, h2T, 'h2T', t, xsb, 'xsb')
            S.fence()
            DMA('sp', nfrep, nfr[:, :], [], ['nfrep'], 'nfl')

            def ev_gu(i, nn, b, bres):
                G = i // 2
                if i % 2 == 0:
                    ACT(sgt[:, nn, 0:NT], b[:, 0:NT], AF.Silu, [bres], ['sgt%d' % nn])
                else:
                    TT('dve', aT[:, G * 4 + nn, 0:NT], b[:, 0:NT], sgt[:, nn, 0:NT], ALU.mult, [bres, 'sgt%d' % nn], ['aT'])
            if _STAGE == 6.6:
                return
            gemm_feat(ffn_order, h2T, 'h2T', NT, ev_gu)
            if _STAGE == 6.7:
                return

            def ev_d(gi, t, b, bres, ncols):
                TT('dve', x2[:, t, gi * 512:(gi + 1) * 512], b[:, 0:512], x2[:, t, gi * 512:(gi + 1) * 512], ALU.add,
                   [bres, 'x2_%d' % t], ['x2_%d' % t])
                if gi == 3 and _STAGE != 6.8:
                    ACT(sgj, x2[:, t, :], AF.Square, ['x2_%d' % t], ['sgj', 'ss'], accum=stat[:, 0:1])
                    rstd_of(stat[:, 0:1], stat[:, 1:2], D, ['ss'], ['rstd'])
                    STT('dve', x2[:, t, :], x2[:, t, :], stat[:, 1:2], nfrep, ALU.mult, ALU.mult, ['x2_%d' % t, 'rstd', 'nfrep'], ['x2_%d' % t])
                    finals.append(DMA('pool', yout_fn(t), x2[:, t, :], ['x2_%d' % t], [], ('yo', t)))
            gemm_tok(G_d, aT, 'aT', ntile, ev_d)
            S.fence()

        stp = sb("stp", [128, NBIS], F32)
        stp2 = sb("stp2", [128, NBIS], F32)
        rcp = sb("rcp", [128, 16], F32)

        S.op('dve', lambda e: e.memset(Sst[:], 0.0), [], ['Sst'])
        S.op('dve', lambda e: e.memset(Sb[:], 0.0), [], ['Sb'])
        for vb in range(NV):
            own = (vb % 2 == 1)
            oi = vb // 2
            oo = None
            if own:
                oo = dict(k=lambda t, oi=oi: k_o[oi * 512 + t * 128:oi * 512 + (t + 1) * 128, :],
                          v=lambda t, oi=oi: v_o[oi * 512 + t * 128:oi * 512 + (t + 1) * 128, :],
                          ik=lambda t, oi=oi: ik_o[oi * 512 + t * 128:oi * 512 + (t + 1) * 128, :])
            xf = lambda t, vb=vb: xv[vb * 512 + t * 128:vb * 512 + (t + 1) * 128, :]
            kside(4, xf, vb * 512, vb * 4, vb * 512, oo, vb % 2)
            if (_STAGE <= 2 and vb >= _STAGEVB):
                S.emit(nc, finals); return nc
            if not own:
                for t in range(4):
                    state_update(t, vb * 4 + t)
                if (_STAGE <= 3 and vb >= _STAGEVB) or _STOPVB == vb:
                    S.emit(nc, finals); return nc
                continue
            qside_ret(4, vb * 4)
            if (_STAGE <= 4 and vb >= _STAGEVB):
                S.emit(nc, finals); return nc
            qside_proj(4, vb * 4)
            if (_STAGE <= 5 and vb >= _STAGEVB):
                S.emit(nc, finals); return nc
            S.fence()
            for t in range(4):
                attention_tile(t, [(0, vb * 512 + (t + 1) * 128)], 0, True, dbg=(_DEBUG == (vb, t)))
            if (_STAGE <= 6 and vb >= _STAGEVB):
                S.emit(nc, finals); return nc
            merge_ffn(4, xf, lambda t, oi=oi: y_o[oi * 512 + t * 128:oi * 512 + (t + 1) * 128, :])
            if (_STAGE <= 7 and vb >= _STAGEVB) or _STOPVB == vb:
                S.emit(nc, finals); return nc
        finals.append(DMA('pool', st_o[:, :, :], Sst[:], ['Sst'], [], 'sto'))
        S.fence()
        DMA('sp', Sst[:], st0[:, :, :], [], ['Sst'], 'stl')
        ACT(Sb[:], Sst[:], AF.Copy, ['Sst'], ['Sb'])
        for ct in range(8):
            tk = SOFF + ct * 128
            DMA('sp', kfst[:], ck[ct * 128:(ct + 1) * 128, :], [], ['kfst'], 'ckl')
            ACT(kbst[:], kfst[:], AF.Copy, ['kfst'], ['kbst'])
            transposes(kbst, lambda c0, n: kTst[:, c0:c0 + n, :], 4, ['kbst'], ['kTst'])
            DMA('pool', kTs[:, :, tk:tk + 128].rearrange("g p t -> p g t"), kTst[:], ['kTst'], ['scr0'], ('scw', 0))
            DMA('sp', vfst[:], cv[ct * 128:(ct + 1) * 128, :], [], ['vfst'], 'cvl')
            ACT(vbst[:, :, 0:128], vfst[:].rearrange("p (g e) -> p g e", g=4), AF.Copy, ['vfst'], ['vbst'])
            DMA('pool', Vs[tk:tk + 128, :, :], vbst[:], ['vbst'], ['scr0'], ('scw', 0))
            DMA('sp', kifst[:], ci[ct * 128:(ct + 1) * 128, :], [], ['kifst'], 'cil')
            ACT(kibst[:], kifst[:], AF.Copy, ['kifst'], ['kibst'])
            tb = trp[trc[0] % 2]; tres = 'trp%d' % (trc[0] % 2); trc[0] += 1
            TR(tb[0:64, 0:128], kibst[:], ['kibst'], [tres])
            ACT(kiTst[0:64, 0, :], tb[0:64, 0:128], AF.Copy, [tres], ['kiTst'])
            ACT(kiTst[64:128, 1, :], tb[0:64, 0:128], AF.Copy, [tres], ['kiTst'])
            DMA('pool', kiTs[:, :, tk:tk + 128].rearrange("v p t -> p v t"), kiTst[:], ['kiTst'], ['scr0'], ('scw', 0))
        soo = dict(k=lambda t: ks_o[:, :], v=lambda t: vs_o[:, :], ik=lambda t: iks_o[:, :])
        xsf = lambda t: xs[:, :]
        kside(1, xsf, SOFF + 1024, NV * 4, NV * 512, soo, 0)
        qside_ret(1, NV * 4)
        qside_proj(1, NV * 4)
        S.fence()
        attention_tile(0, [(SOFF, 1152)], 1, False)
        merge_ffn(1, xsf, lambda t: ys_o[:, :])
        finals.append(DMA('pool', sts_o[:, :, :], Sst[:], ['Sst'], [], 'sto'))
        S.emit(nc, finals)
    return nc


def _tables(NBR, h):
    NV = NBR + 1
    NTIL = NV * 4 + 1
    pos = np.zeros(NV * 512 + 128, np.float64)
    dummy = NBR if h == 1 else 0
    for v in range(NV):
        r = v if h == 1 else v - 1
        if v == dummy:
            r = 0
        pos[v * 512:(v + 1) * 512] = r * 512 + np.arange(512)
    pos[NV * 512:] = 1024 + np.arange(128)
    i64 = np.arange(64, dtype=np.float32) / 64
    i32 = np.arange(32, dtype=np.float32) / 32
    fA = (np.float32(10000.0) ** (-i64)).astype(np.float32)
    fI = (np.float32(10000.0) ** (-i32)).astype(np.float32)
    angA = pos.astype(np.float32)[:, None] * fA[None, :]
    angI = pos.astype(np.float32)[:, None] * fI[None, :]
    sc = np.float32(128 ** -0.5)
    rt = np.concatenate([np.cos(angA), np.sin(angA), np.cos(angA) * sc, np.sin(angA) * sc, np.cos(angI), np.sin(angI)], axis=1)
    lg = np.log1p(-(2.0 ** (-5.0 - np.arange(8, dtype=np.float64))))
    j = np.arange(128, dtype=np.float64)[:, None]
    kdec = np.zeros((128, NTIL, 8), np.float64)
    sdec = np.zeros((128, NTIL, 8), np.float64)
    for tl in range(NTIL):
        if tl == NTIL - 1:
            kdec[:, tl, :] = np.exp((31.0 - j) * lg[None, :]); sdec[:, tl, :] = np.exp(32.0 * lg)[None, :]
        elif tl // 4 == dummy:
            kdec[:, tl, :] = 1.0; sdec[:, tl, :] = 1.0
        else:
            kdec[:, tl, :] = np.exp((127.0 - j) * lg[None, :]); sdec[:, tl, :] = np.exp(128.0 * lg)[None, :]
    kq = np.concatenate([np.exp(-(j + 1.0) * lg[None, :]), np.exp((j + 1.0) * lg[None, :])], axis=1)
    tri = (np.arange(128)[:, None] <= np.arange(128)[None, :]).astype(np.float32)
    q = np.arange(128)[:, None]; s = np.arange(128)[None, :]
    mp = np.where((s // 64) <= (q // 64), 0.0, NEG)
    ms = np.where(s < 32, 0.0, NEG) + 0.0 * q
    maskD = np.concatenate([mp, ms], axis=1)
    mask0 = np.full((128, 512), NEG if h == 0 else 0.0)
    pow2 = np.tile((2.0 ** -(np.arange(NBIS) + 1.0))[None, :], (128, 1))
    f = np.float32
    return dict(rt=rt.astype(f), kdec=kdec.reshape(128, -1).astype(f), sdec=sdec.reshape(128, -1).astype(f), kq=kq.astype(f),
                idf=np.eye(128, dtype=f), tri=tri, maskD=maskD.astype(f), mask0=mask0.astype(f), pow2=pow2.astype(f))


_NC_CACHE = {}


def kernel(x_prompt, x_sample, cache_k, cache_v, cache_idx_k, state_ret, norm1_g, w_in, w_pa, w_pb, w_o, norm2_g,
           w_ffn_gate, w_ffn_up, w_ffn_down, norm_f_g):
    f = np.float32
    x_prompt = np.asarray(x_prompt, f); x_sample = np.asarray(x_sample, f)
    B, T, _ = x_prompt.shape
    DB, DS, _ = x_sample.shape
    NBR = T // 512
    NV = NBR + 1
    NOWN = NBR // 2
    ncore = 2 * B
    assert DB == ncore and DS == 32
    if NBR not in _NC_CACHE:
        _NC_CACHE[NBR] = build(NBR)
    nc = _NC_CACHE[NBR]
    shared = dict(
        w_in=np.ascontiguousarray(np.asarray(w_in, f)[0]), w_pa=np.ascontiguousarray(np.asarray(w_pa, f)[0]),
        w_pb=np.ascontiguousarray(np.asarray(w_pb, f)[0]), w_o=np.ascontiguousarray(np.asarray(w_o, f)[0]),
        w_g=np.ascontiguousarray(np.asarray(w_ffn_gate, f)[0]), w_u=np.ascontiguousarray(np.asarray(w_ffn_up, f)[0]),
        w_d=np.ascontiguousarray(np.asarray(w_ffn_down, f)[0]),
        n1t=np.ascontiguousarray(np.asarray(norm1_g, f)[0].reshape(16, 128).T),
        n2t=np.ascontiguousarray(np.asarray(norm2_g, f)[0].reshape(16, 128).T),
        nfr=np.ascontiguousarray(np.broadcast_to(np.asarray(norm_f_g, f)[None, :], (128, D))),
    )
    tabs = [_tables(NBR, 0), _tables(NBR, 1)]
    in_maps = []
    for c in range(ncore):
        b, h = c // 2, c % 2
        xvv = np.zeros((NV * 512, D), f)
        if h == 1:
            xvv[0:NBR * 512] = x_prompt[b]
        else:
            xvv[512:] = x_prompt[b]
        xsv = np.zeros((128, D), f); xsv[0:32] = x_sample[c]
        m = dict(shared)
        m.update(tabs[h])
        m.update(xv=xvv, xs=xsv,
                 ck=np.ascontiguousarray(np.asarray(cache_k, f)[0, c].reshape(1024, 512)),
                 cv=np.ascontiguousarray(np.asarray(cache_v, f)[0, c].reshape(1024, 512)),
                 ci=np.ascontiguousarray(np.asarray(cache_idx_k, f)[0, c]),
                 st0=np.ascontiguousarray(np.asarray(state_ret, f)[0, c].transpose(1, 0, 2)))
        in_maps.append(m)
    res = run_bass_kernel_spmd(nc, in_maps, core_ids=list(range(ncore)))
    R = res.results
    if _DEBUG is not None:
        kernel.dbg = [dict(s=r['dbg_s'], m=r['dbg_m'], oa=r['dbg_oa'], ob=r['dbg_ob']) for r in R]
    y_p = np.zeros((B, T, D), f); k_p = np.zeros((1, B, T, 4, 128), f); v_p = np.zeros((1, B, T, 4, 128), f)
    i_p = np.zeros((1, B, T, 64), f); s_p = np.zeros((1, B, 8, 128, 256), f)
    y_s = np.zeros((DB, DS, D), f); k_s = np.zeros((1, DB, DS, 4, 128), f); v_s = np.zeros((1, DB, DS, 4, 128), f)
    i_s = np.zeros((1, DB, DS, 64), f); s_s = np.zeros((1, DB, 8, 128, 256), f)
    for c in range(ncore):
        b, h = c // 2, c % 2
        r = R[c]
        for i in range(NOWN):
            rb = 2 * i + h
            sl = slice(rb * 512, (rb + 1) * 512)
            y_p[b, sl] = r["y_o"][i * 512:(i + 1) * 512]
            k_p[0, b, sl] = r["k_o"][i * 512:(i + 1) * 512].reshape(512, 4, 128)
            v_p[0, b, sl] = r["v_o"][i * 512:(i + 1) * 512].reshape(512, 4, 128)
            i_p[0, b, sl] = r["ik_o"][i * 512:(i + 1) * 512]
        if h == 0:
            s_p[0, b] = r["st_o"].transpose(1, 0, 2)
        y_s[c] = r["ys_o"][0:32]
        k_s[0, c] = r["ks_o"][0:32].reshape(32, 4, 128)
        v_s[0, c] = r["vs_o"][0:32].reshape(32, 4, 128)
        i_s[0, c] = r["iks_o"][0:32]
        s_s[0, c] = r["sts_o"].transpose(1, 0, 2)
    return (y_p, y_s, k_p, v_p, i_p, s_p, k_s, v_s, i_s, s_s)
```

```python
import contextlib
import numpy as np
import concourse.bass as bass
import concourse.mybir as mybir
from concourse.bass_utils import run_bass_kernel_spmd

F32 = mybir.dt.float32
BF16 = mybir.dt.bfloat16
ALU = mybir.AluOpType
AF = mybir.ActivationFunctionType
AX = mybir.AxisListType

D = 2048
FF = 5632
C_QA, C_KA, C_VA, C_QI, C_KI, C_WI, C_QR, C_KR, C_VR, C_GR, C_GA, C_GB = (
    0, 2048, 2560, 3072, 4096, 4160, 4176, 5200, 6224, 8272, 10320, 12368)
PTOT = 14416
NBIS = 14
NEG = -1.0e30
_STAGE = 99
_STOPVB = None
_STAGEVB = 0
_DEBUG = None


class Sched:
    ENG = ('pe', 'act', 'dve', 'pool', 'sp')

    def __init__(self):
        self.ins = {e: [] for e in self.ENG}
        self.lastw = {}
        self.readers = {}
        self.dma_cnt = {}
        self.seen = {e: {} for e in self.ENG}
        self.fence_toks = []
        self.fence_id = 0
        self.eng_fence = {e: 0 for e in self.ENG}

    def fence(self):
        toks = []
        for e in self.ENG:
            for i in range(len(self.ins[e]) - 1, -1, -1):
                if self.ins[e][i]['dma'] is None:
                    toks.append(('eng', e, i))
                    break
        for k, v in self.dma_cnt.items():
            toks.append(('dma', k, v))
        self.fence_toks = toks
        self.fence_id += 1

    def op(self, eng, fn, reads=(), writes=(), dma=None, nofence=False):
        idx = len(self.ins[eng])
        deps = []
        for r in reads:
            t = self.lastw.get(r)
            if t is not None:
                deps.append((t, 'raw'))
            if r.startswith(('rot', 'acc', 'trp')):
                for t in self.readers.get(r, ()):
                    if t[1] != eng:
                        deps.append((t, 'raw'))
        for w in writes:
            t = self.lastw.get(w)
            if t is not None:
                deps.append((t, 'waw'))
            for t in self.readers.get(w, ()):
                deps.append((t, 'war'))
        if not nofence and self.eng_fence[eng] < self.fence_id:
            self.eng_fence[eng] = self.fence_id
            for t in self.fence_toks:
                if not (t[0] == 'eng' and t[1] == eng):
                    deps.append((t, 'raw'))
        if dma is not None:
            self.dma_cnt[dma] = self.dma_cnt.get(dma, 0) + 16
            tok = ('dma', dma, self.dma_cnt[dma])
        else:
            tok = ('eng', eng, idx)
        waits = {}
        for t, kind in deps:
            if t[0] == 'eng':
                if t[1] == eng and dma is None:
                    if eng == 'pe':
                        continue
                    if kind == 'war' or idx - t[2] > 8:
                        continue
                key = ('eng', t[1])
            else:
                key = ('dma', t[1])
            val = t[2]
            if self.seen[eng].get(key, -1) >= val:
                continue
            if waits.get(key, -1) < val:
                waits[key] = val
        for k, v in waits.items():
            self.seen[eng][k] = v
        self.ins[eng].append(dict(fn=fn, waits=waits, dma=dma))
        for r in reads:
            self.readers.setdefault(r, []).append(tok)
        for w in writes:
            self.lastw[w] = tok
            self.readers[w] = []
        return tok

    def emit(self, nc, finals):
        miles = {e: set() for e in self.ENG}
        for e in self.ENG:
            for ins in self.ins[e]:
                for k, v in ins['waits'].items():
                    if k[0] == 'eng':
                        miles[k[1]].add(v)
        rank = {e: {s: i + 1 for i, s in enumerate(sorted(miles[e]))} for e in self.ENG}
        dkeys = sorted(self.dma_cnt.keys(), key=str)
        with contextlib.ExitStack() as st:
            psem = {e: st.enter_context(nc.semaphore("p_" + e)) for e in self.ENG}
            dsem = {k: st.enter_context(nc.semaphore("d_%d" % i)) for i, k in enumerate(dkeys)}
            block = st.enter_context(nc.Block())

            def run(e, eng):
                for i, ins in enumerate(self.ins[e]):
                    for k, v in ins['waits'].items():
                        if k[0] == 'eng':
                            eng.wait_ge(psem[k[1]], rank[k[1]][v])
                        else:
                            eng.wait_ge(dsem[k[1]], v)
                    bi = ins['fn'](eng)
                    if ins['dma'] is not None:
                        bi.then_inc(dsem[ins['dma']], 16)
                    elif i in rank[e]:
                        bi.then_inc(psem[e], 1)
                if e == 'sp':
                    fin = {}
                    for t in finals:
                        fin[t[1]] = max(fin.get(t[1], 0), t[2])
                    for k, v in fin.items():
                        eng.wait_ge(dsem[k], v)

            @block.tensor
            def _(eng):
                run('pe', eng)

            @block.scalar
            def _(eng):
                run('act', eng)

            @block.vector
            def _(eng):
                run('dve', eng)

            @block.gpsimd
            def _(eng):
                run('pool', eng)

            @block.sync
            def _(eng):
                run('sp', eng)


def build(NBR):
    NV = NBR + 1
    NOWN = NBR // 2
    NTIL = NV * 4 + 1
    SOFF = NV * 512
    NTOK = SOFF + 1152
    nc = bass.Bass("TRN2", target_bir_lowering=False)
    S = Sched()
    finals = []

    def din(name, shape):
        return nc.dram_tensor(name, shape, F32, kind="ExternalInput").ap()

    def dout(name, shape):
        return nc.dram_tensor(name, shape, F32, kind="ExternalOutput").ap()

    def dscr(name, shape, dt=BF16):
        return nc.dram_tensor(name, shape, dt, kind="Internal").ap()

    xv = din("xv", [NV * 512, D]); xs = din("xs", [128, D])
    ck = din("ck", [1024, 512]); cv = din("cv", [1024, 512]); ci = din("ci", [1024, 64])
    st0 = din("st0", [128, 8, 256])
    w_in = din("w_in", [D, PTOT]); w_pa = din("w_pa", [D, D]); w_pb = din("w_pb", [D, D]); w_o = din("w_o", [D, D])
    w_g = din("w_g", [D, FF]); w_u = din("w_u", [D, FF]); w_d = din("w_d", [FF, D])
    n1t = din("n1t", [128, 16]); n2t = din("n2t", [128, 16]); nfr = din("nfr", [128, D])
    rt = din("rt", [NV * 512 + 128, 320])
    kdec_d = din("kdec", [128, NTIL * 8]); sdec_d = din("sdec", [128, NTIL * 8])
    kq_d = din("kq", [128, 16])
    idf_d = din("idf", [128, 128]); tri_d = din("tri", [128, 128])
    maskD_d = din("maskD", [128, 256]); mask0_d = din("mask0", [128, 512]); pow2_d = din("pow2", [128, NBIS])

    y_o = dout("y_o", [NOWN * 512, D]); k_o = dout("k_o", [NOWN * 512, 512]); v_o = dout("v_o", [NOWN * 512, 512])
    ik_o = dout("ik_o", [NOWN * 512, 64]); st_o = dout("st_o", [128, 8, 256])
    ys_o = dout("ys_o", [128, D]); ks_o = dout("ks_o", [128, 512]); vs_o = dout("vs_o", [128, 512])
    iks_o = dout("iks_o", [128, 64]); sts_o = dout("sts_o", [128, 8, 256])

    if _DEBUG is not None:
        dbg_s = dout("dbg_s", [128, 1024]); dbg_m = dout("dbg_m", [128, 1024]); dbg_oa = dout("dbg_oa", [128, 2048]); dbg_ob = dout("dbg_ob", [128, 2048])
    kTs = dscr("kTs", [4, 128, NTOK]); Vs = dscr("Vs", [NTOK, 4, 136]); kiTs = dscr("kiTs", [2, 128, NTOK])
    hTs = dscr("hTs", [128, 16, 512]); obTs = dscr("obTs", [128, 16, 512])

    ucount = [0]

    def mkgroups(wap, K, c0, ncols_total, gw=512):
        groups = []
        kch = K // 128
        for g0 in range(0, ncols_total, gw):
            ncols = min(gw, ncols_total - g0)
            units = []
            for k0 in range(0, kch, 16):
                kc = min(16, kch - k0)
                uid = ucount[0]; ucount[0] += 1
                scr = dscr("wu%d" % uid, [128, kc, ncols])
                src = wap[k0 * 128:(k0 + kc) * 128, c0 + g0:c0 + g0 + ncols].rearrange("(c p) n -> p c n", p=128)
                units.append(dict(uid=uid, scr=scr, src=src, kc=kc, k0=k0, ncols=ncols, res="wu%d" % uid))
            groups.append(units)
        return groups

    G_ka = mkgroups(w_in, D, C_KA, 512); G_va = mkgroups(w_in, D, C_VA, 512); G_ki = mkgroups(w_in, D, C_KI, 80)
    G_kr = mkgroups(w_in, D, C_KR, 1024); G_vr = mkgroups(w_in, D, C_VR, 2048)
    G_qr = mkgroups(w_in, D, C_QR, 1024); G_gr = mkgroups(w_in, D, C_GR, 2048)
    G_qa = mkgroups(w_in, D, C_QA, 2048); G_qi = mkgroups(w_in, D, C_QI, 1024)
    G_ga = mkgroups(w_in, D, C_GA, 2048); G_pa = mkgroups(w_pa, D, 0, 2048)
    G_gb = mkgroups(w_in, D, C_GB, 2048); G_pb = mkgroups(w_pb, D, 0, 2048)
    G_o = mkgroups(w_o, D, 0, 2048)
    G_g = mkgroups(w_g, D, 0, FF); G_u = mkgroups(w_u, D, 0, FF); G_d = mkgroups(w_d, FF, 0, 2048)
    allgroups = [G_ka, G_va, G_ki, G_kr, G_vr, G_qr, G_gr, G_qa, G_qi, G_ga, G_pa, G_gb, G_pb, G_o]
    ffn_order = []
    for i in range(len(G_g)):
        ffn_order += [G_g[i], G_u[i]]

    with contextlib.ExitStack() as st:
        def sb(name, shape, dt):
            return st.enter_context(nc.sbuf_tensor("s_" + name, shape, dt))

        def pst(name, shape, dt):
            return st.enter_context(nc.psum_tensor("p_" + name, shape, dt))

        wslot = [sb("ws0", [128, 16, 512], BF16), sb("ws1", [128, 16, 512], BF16)]
        Sst = sb("Sst", [128, 8, 256], F32); Sb = sb("Sb", [128, 8, 256], BF16)
        oaT = sb("oaT", [128, 16, 512], BF16)
        identf = sb("identf", [128, 128], F32); identb = sb("identb", [128, 128], BF16)
        trif = sb("trif", [128, 128], F32); trib = sb("trib", [128, 128], BF16)
        i4big = sb("i4big", [128, 4, 128], BF16)
        kdec = sb("kdec", [128, NTIL * 8], F32); sdec = sb("sdec", [128, NTIL * 8], F32)
        kq = sb("kq", [128, 16], F32)
        n1s = sb("n1s", [128, 16], F32); n2s = sb("n2s", [128, 16], F32)
        pow2 = sb("pow2", [128, NBIS], F32)
        maskD = sb("maskD", [128, 256], F32); mask0 = sb("mask0", [128, 512], F32)
        rts = [sb("rt0", [128, 320], F32), sb("rt1", [128, 320], F32)]
        rt1 = sb("rtmp1", [128, 256], F32); rt2 = sb("rtmp2", [128, 256], F32)
        stat = sb("stat", [128, 64], F32)
        wab = sb("wab", [128, 4, 16], F32); wsg = sb("wsg", [128, 4, 16], F32)
        UB = 114688
        U = sb("U", [128, UB // 4], F32)

        def uv(off, nbytes, dt, pat=None, **kw):
            a = U[:, off // 4:(off + nbytes) // 4]
            if dt == BF16:
                a = a.bitcast(BF16)
            if pat:
                a = a.rearrange(pat, **kw)
            return a

        K1 = 1024
        qaT = uv(0, 16 * K1, BF16, "p (c t) -> p c t", c=16)
        qiT = uv(16 * K1, 8 * K1, BF16, "p (c t) -> p c t", c=8)
        hT = uv(24 * K1, 16 * K1, BF16, "p (c t) -> p c t", c=16)
        xst = uv(40 * K1, 8 * K1, F32)
        kd = uv(48 * K1, 8 * K1, BF16, "p (t c) -> p t c", t=4)
        kpT = uv(56 * K1, 8 * K1, BF16, "p (c t) -> p c t", c=8)
        qrT = uv(64 * K1, 8 * K1, BF16, "p (c t) -> p c t", c=8)
        vr = uv(72 * K1, 16 * K1, BF16, "p (t c) -> p t c", t=4)
        grs = uv(88 * K1, 16 * K1, BF16, "p (t c) -> p t c", t=4)
        obb = uv(104 * K1, 4 * K1, BF16)
        obst = uv(40 * K1, 4 * K1, BF16, "p (c t) -> p c t", c=16)
        innT = uv(108 * K1, 2 * K1, BF16, "p (h t) -> p h t", h=8)
        score = uv(24 * K1, 32 * K1, F32)
        mmask = uv(56 * K1, 16 * K1, BF16)
        kvk = [uv(72 * K1 + i * 8704, 4096, BF16, "p (g t) -> p g t", g=4) for i in range(2)]
        kvv = [uv(72 * K1 + i * 8704 + 4096, 4352, BF16, "p (c g e) -> p c g e", c=4, g=4) for i in range(2)]
        o3 = 72 * K1 + 2 * 8704
        kis = [uv(o3 + i * 2 * K1, 2 * K1, BF16, "p (v t) -> p v t", v=2) for i in range(2)]
        rbuf = [uv(o3 + 4 * K1 + i * 2 * K1, 2 * K1, F32) for i in range(2)]
        PTb = [uv(o3 + 8 * K1 + i * K1, K1, BF16) for i in range(2)]
        oacc = uv(o3 + 10 * K1, 8256, F32, "p (h e) -> p h e", h=16)
        oab = uv(o3 + 10 * K1 + 8256, 4 * K1, BF16)
        hTb = uv(0, 16 * K1, BF16, "p (c t) -> p c t", c=16)
        obT = uv(16 * K1, 16 * K1, BF16, "p (c t) -> p c t", c=16)
        sA = uv(32 * K1, 16 * K1, BF16, "p (c t) -> p c t", c=16)
        sB = uv(48 * K1, 16 * K1, BF16, "p (c t) -> p c t", c=16)
        mgT = uv(64 * K1, 16 * K1, BF16, "p (c t) -> p c t", c=16)
        x2 = uv(80 * K1, 32 * K1, F32, "p (t c) -> p t c", t=4)
        xsb = uv(32 * K1, 4 * K1, BF16)
        h2T = uv(0, 16 * K1, BF16, "p (c t) -> p c t", c=16)
        aT = uv(16 * K1, 44 * K1, BF16, "p (c t) -> p c t", c=44)
        sgt = uv(60 * K1, 4 * K1, BF16, "p (c t) -> p c t", c=4)
        nfrep = uv(64 * K1, 8 * K1, F32)
        sgj = uv(60 * K1, 4 * K1, BF16)
        kfst = sb("kfst", [128, 512], F32); vfst = sb("vfst", [128, 512], F32); kifst = sb("kifst", [128, 64], F32)
        kbst = sb("kbst", [128, 512], BF16); kTst = sb("kTst", [128, 4, 128], BF16)
        vbst = sb("vbst", [128, 4, 136], BF16); kibst = sb("kibst", [128, 64], BF16); kiTst = sb("kiTst", [128, 2, 128], BF16)
        qst = sb("qst", [128, 512], BF16)

        rot = [pst("rot%d" % i, [128, 512], F32) for i in range(4)]
        trp = [pst("trp%d" % i, [128, 1024], BF16) for i in range(2)]
        acc = [pst("acc%d" % i, [128, 512], F32) for i in range(2)]
        trc = [0]
        accl = [acc[0], acc[1], rot[2], rot[3]]
        accr = ['acc0', 'acc1', 'rot2', 'rot3']
        agc = [0]

        def MM(out, lhsT, rhs, start, stop, R, W, sgc=False):
            if sgc:
                S.op('pe', lambda e: e.matmul(out, lhsT, rhs, start=start, stop=stop, skip_group_check=True), R, W)
            else:
                S.op('pe', lambda e: e.matmul(out, lhsT, rhs, start=start, stop=stop), R, W)

        def TR(out, in_, R, W):
            S.op('pe', lambda e: e.transpose(out=out, in_=in_, identity=identb[:]), list(R) + ['identb'], W)

        def ACT(out, in_, func, R, W, scale=None, bias=None, accum=None):
            kw = {}
            if scale is not None:
                kw['scale'] = scale
            if bias is not None:
                kw['bias'] = bias
            if accum is not None:
                kw['accum_out'] = accum
            S.op('act', lambda e: e.activation(out=out, in_=in_, func=func, **kw), R, W)

        def TS(eng, out, in0, s1, s2, op0, op1, R, W, accum=None):
            kw = {}
            if op1 is not None:
                kw['op1'] = op1
            if accum is not None:
                kw['accum_out'] = accum
            S.op(eng, lambda e: e.tensor_scalar(out=out, in0=in0, scalar1=s1, scalar2=s2, op0=op0, **kw), R, W)

        def TT(eng, out, in0, in1, op, R, W):
            S.op(eng, lambda e: e.tensor_tensor(out=out, in0=in0, in1=in1, op=op), R, W)

        def STT(eng, out, in0, scalar, in1, op0, op1, R, W):
            S.op(eng, lambda e: e.scalar_tensor_tensor(out=out, in0=in0, scalar=scalar, in1=in1, op0=op0, op1=op1), R, W)

        def DMA(eng, out, in_, R, W, key, nofence=False):
            return S.op(eng, lambda e: e.dma_start(out=out, in_=in_), R, W, dma=key, nofence=nofence)

        for dst, src, nm in ((identf[:], idf_d, 'identf'), (trif[:], tri_d, 'trif'), (kdec[:], kdec_d, 'kdec'),
                             (sdec[:], sdec_d, 'sdec'), (kq[:], kq_d, 'kq'), (n1s[:], n1t, 'n1s'), (n2s[:], n2t, 'n2s'),
                             (pow2[:], pow2_d, 'pow2'), (maskD[:], maskD_d, 'maskD'), (mask0[:], mask0_d, 'mask0')):
            DMA('sp', dst, src[:, :], [], [nm], 'const', nofence=True)
        ACT(identb[:], identf[:], AF.Copy, ['identf'], ['identb'])
        ACT(trib[:], trif[:], AF.Copy, ['trif'], ['trib'])
        for j in range(4):
            ACT(i4big[:, j, :], identf[:], AF.Copy, ['identf'], ['i4big'], scale=30000.0)
        S.op('dve', lambda e: e.memset(stat[:, 60:61], 1e-6), [], ['eps'])
        S.op('dve', lambda e: e.memset(vbst[:], 1.0), [], ['vbst'])
        S.op('dve', lambda e: e.memset(kiTst[:], 0.0), [], ['kiTst'])

        cast_order = []
        for G in allgroups:
            for units in G:
                cast_order += units
        for units in ffn_order:
            cast_order += units
        for units in G_d:
            cast_order += units
        for i, u in enumerate(cast_order):
            DMA('pool', u['scr'][:, :, :], u['src'], [], [u['res']], ('wc', i % 8), nofence=True)

        if _STAGE <= 1:
            S.emit(nc, finals); return nc
        wcnt = [0]

        def wload(u):
            s = wcnt[0] % 2
            wcnt[0] += 1
            DMA('sp', wslot[s][:, 0:u['kc'], 0:u['ncols']], u['scr'][:, :, :], [u['res']], ['ws%d' % s], ('wl', s), nofence=True)
            return s

        def stream(ulist, body):
            slots = {}
            if ulist:
                slots[0] = wload(ulist[0])
            for i, u in enumerate(ulist):
                if i + 1 < len(ulist):
                    slots[i + 1] = wload(ulist[i + 1])
                s = slots[i]
                body(i, u, wslot[s], 'ws%d' % s)

        defq = []

        def run_deferred():
            while defq:
                defq.pop(0)()

        def gemm_tok(groups, actT, actres, ntile, evac):
            flat = []
            for gi, units in enumerate(groups):
                for ui, u in enumerate(units):
                    flat.append((gi, ui, len(units), u))

            def body(i, u, slot, sres):
                gi, ui, nu, _ = flat[i]
                for t in range(ntile):
                    b = rot[t]
                    for c in range(u['kc']):
                        MM(b[:, 0:u['ncols']], actT[:, u['k0'] + c, t * 128:(t + 1) * 128], slot[:, c, 0:u['ncols']],
                           (ui == 0 and c == 0), (ui == nu - 1 and c == u['kc'] - 1), [actres, sres], ['rot%d' % t])
                    run_deferred()
                    if ui == nu - 1:
                        evac(gi, t, b, 'rot%d' % t, u['ncols'])
            stream([f[3] for f in flat], body)
            run_deferred()

        def gemm_feat(groups, actT, actres, NT, evac):
            flat = [units[0] for units in groups]

            def body(i, u, slot, sres):
                for nn in range(u['ncols'] // 128):
                    b = rot[nn]
                    for c in range(16):
                        MM(b[:, 0:NT], slot[:, c, nn * 128:(nn + 1) * 128], actT[:, c, 0:NT], c == 0, c == 15,
                           [actres, sres], ['rot%d' % nn])
                    evac(i, nn, b, 'rot%d' % nn)
            stream(flat, body)

        def transposes(src, dstfn, nchunk, R, W, evac_eng='act', scale_ap=None):
            for c0 in range(0, nchunk, 8):
                n = min(8, nchunk - c0)
                tb = trp[trc[0] % 2]; tres = 'trp%d' % (trc[0] % 2); trc[0] += 1
                for j in range(n):
                    TR(tb[:, j * 128:(j + 1) * 128], src[:, (c0 + j) * 128:(c0 + j + 1) * 128], R, [tres])
                dst = dstfn(c0, n)
                srcv = tb[:, 0:n * 128].rearrange("p (c t) -> p c t", c=n)
                if scale_ap is not None:
                    TT('dve', dst, srcv, scale_ap[:, c0:c0 + n].unsqueeze(2).to_broadcast([128, n, 128]), ALU.mult,
                       [tres] + list(R), W)
                elif evac_eng == 'act':
                    ACT(dst, srcv, AF.Copy, [tres], W)
                else:
                    S.op('dve', lambda e: e.tensor_copy(out=dst, in_=srcv), [tres], W)

        def rope(ps, psres, nh, hd, cos, sin, d, W, rsl):
            half = hd // 2
            x = ps.rearrange("p (h t d) -> p h t d", h=nh, t=2)
            dv = d.rearrange("p (h t d) -> p h t d", h=nh, t=2)
            t1 = rt1[:, 0:nh * half].rearrange("p (h d) -> p h d", h=nh)
            t2 = rt2[:, 0:nh * half].rearrange("p (h d) -> p h d", h=nh)
            cb = cos.unsqueeze(1).to_broadcast([128, nh, half])
            sbb = sin.unsqueeze(1).to_broadcast([128, nh, half])
            TT('dve', t1, x[:, :, 0, :], cb, ALU.mult, [psres, rsl], ['rt1'])
            TT('dve', t2, x[:, :, 1, :], sbb, ALU.mult, [psres, rsl], ['rt2'])
            TT('dve', dv[:, :, 0, :], t1, t2, ALU.subtract, ['rt1', 'rt2'], W)
            TT('dve', t1, x[:, :, 1, :], cb, ALU.mult, [psres, rsl], ['rt1'])
            TT('dve', t2, x[:, :, 0, :], sbb, ALU.mult, [psres, rsl], ['rt2'])
            TT('dve', dv[:, :, 1, :], t1, t2, ALU.add, ['rt1', 'rt2'], W)

        def rstd_of(ss_col, out_col, n, R, W):
            ACT(stat[:, 62:63], ss_col, AF.Sqrt, list(R) + ['eps'], ['sq'], scale=1.0 / n, bias=stat[:, 60:61])
            S.op('dve', lambda e: e.reciprocal(out=out_col, in_=stat[:, 62:63]), ['sq'], W)

        def norm_to_T(xt, xres, gts, dstT, dres, t, tmpb, tmpres):
            ACT(tmpb, xt, AF.Square, [xres], [tmpres, 'ss'], accum=stat[:, 0:1])
            rstd_of(stat[:, 0:1], stat[:, 1:2], D, ['ss'], ['rstd'])
            ACT(tmpb, xt, AF.Copy, [xres, 'rstd'], [tmpres], scale=stat[:, 1:2])
            transposes(tmpb, lambda c0, n: dstT[:, c0:c0 + n, t * 128:(t + 1) * 128], 16, [tmpres], [dres], scale_ap=gts)

        rtc = [0]

        def ropeload_g(gt):
            sl = rtc[0] % 2
            rtc[0] += 1
            rr = gt * 128
            DMA('sp', rts[sl][:], rt[rr:rr + 128, :], [], ['rts%d' % sl], ('rt', sl))
            return rts[sl], 'rts%d' % sl

        def kside(ntile, xsrc_fn, tok0, tile0, rrow0, own_out, spar):
            NT = ntile * 128
            scr_key = ('scw', spar)
            sres = 'scr%d' % spar
            for t in range(ntile):
                DMA('sp', xst, xsrc_fn(t), [], ['xst'], 'xst')
                norm_to_T(xst, 'xst', n1s, hT, 'hT', t, obb, 'obb')

            def ropeload(t):
                return ropeload_g(tile0 + t)

            def ev_ka(gi, t, b, bres, ncols):
                r, rres = ropeload(t)
                rope(b[:, 0:512], bres, 4, 128, r[:, 0:64], r[:, 64:128], kfst[:], ['kfst'], rres)
                if own_out:
                    finals.append(DMA('pool', own_out['k'](t), kfst[:], ['kfst'], [], 'ko'))
                ACT(kbst[:], kfst[:], AF.Copy, ['kfst'], ['kbst'])

                def later():
                    transposes(kbst, lambda c0, n: kTst[:, c0:c0 + n, :], 4, ['kbst'], ['kTst'])
                    tk = tok0 + t * 128
                    DMA('pool', kTs[:, :, tk:tk + 128].rearrange("g p t -> p g t"), kTst[:], ['kTst'], [sres], scr_key)
                defq.append(later)
            gemm_tok(G_ka, hT, 'hT', ntile, ev_ka)

            def ev_va(gi, t, b, bres, ncols):
                ACT(vfst[:], b[:, 0:512], AF.Copy, [bres], ['vfst'])
                if own_out:
                    finals.append(DMA('pool', own_out['v'](t), vfst[:], ['vfst'], [], 'vo'))
                ACT(vbst[:, :, 0:128], b[:, 0:512].rearrange("p (g e) -> p g e", g=4), AF.Copy, [bres], ['vbst'])
                tk = tok0 + t * 128
                DMA('pool', Vs[tk:tk + 128, :, :], vbst[:], ['vbst'], [sres], scr_key)
            gemm_tok(G_va, hT, 'hT', ntile, ev_va)

            def ev_ki(gi, t, b, bres, ncols):
                r, rres = ropeload(t)
                rope(b[:, 0:64], bres, 1, 64, r[:, 256:288], r[:, 288:320], kifst[:], ['kifst'], rres)
                if own_out:
                    finals.append(DMA('pool', own_out['ik'](t), kifst[:], ['kifst'], [], 'iko'))
                    ACT(wab[:, t, :], b[:, 64:80], AF.Abs, [bres], ['wab'], scale=1.0 / 32.0)
                    ACT(wsg[:, t, :], b[:, 64:80], AF.Sign, [bres], ['wsg'])
                ACT(kibst[:], kifst[:], AF.Copy, ['kifst'], ['kibst'])

                def later():
                    tb = trp[trc[0] % 2]; tres = 'trp%d' % (trc[0] % 2); trc[0] += 1
                    TR(tb[0:64, 0:128], kibst[:], ['kibst'], [tres])
                    ACT(kiTst[0:64, 0, :], tb[0:64, 0:128], AF.Copy, [tres], ['kiTst'])
                    ACT(kiTst[64:128, 1, :], tb[0:64, 0:128], AF.Copy, [tres], ['kiTst'])
                    tk = tok0 + t * 128
                    DMA('pool', kiTs[:, :, tk:tk + 128].rearrange("v p t -> p v t"), kiTst[:], ['kiTst'], [sres], scr_key)
                defq.append(later)
            gemm_tok(G_ki, hT, 'hT', ntile, ev_ki)

            def ev_kr(gi, t, b, bres, ncols):
                r, rres = ropeload(t)
                rope(b[:, 0:512], bres, 4, 128, r[:, 128:192], r[:, 192:256], kfst[:], ['kfst'], rres)
                krf = kfst[:].rearrange("p (h d) -> p h d", h=4)
                kdv = kdec[:, (tile0 + t) * 8 + gi * 4:(tile0 + t) * 8 + gi * 4 + 4].unsqueeze(2).to_broadcast([128, 4, 128])
                TT('dve', kd[:, t, gi * 512:(gi + 1) * 512].rearrange("p (h d) -> p h d", h=4), krf, kdv, ALU.mult,
                   ['kfst', 'kdec'], ['kd'])
                if own_out:
                    kiv = kq[:, gi * 4:gi * 4 + 4].unsqueeze(2).to_broadcast([128, 4, 128])
                    TT('dve', qst[:].rearrange("p (h d) -> p h d", h=4), krf, kiv, ALU.mult, ['kfst', 'kq'], ['qst'])
                    defq.append(lambda: transposes(qst, lambda c0, n: kpT[:, gi * 4 + c0:gi * 4 + c0 + n, t * 128:(t + 1) * 128], 4,
                                                   ['qst'], ['kpT']))
            gemm_tok(G_kr, hT, 'hT', ntile, ev_kr)

            def ev_vr(gi, t, b, bres, ncols):
                ACT(vr[:, t, gi * 512:(gi + 1) * 512], b[:, 0:512], AF.Copy, [bres], ['vr'])
            gemm_tok(G_vr, hT, 'hT', ntile, ev_vr)

        def state_update(t, gt):
            for hp in range(4):
                b = rot[hp]; bres = 'rot%d' % hp
                for hh in range(2):
                    h = hp * 2 + hh
                    MM(b[:, hh * 256:(hh + 1) * 256], kd[:, t, h * 128:(h + 1) * 128], vr[:, t, h * 256:(h + 1) * 256],
                       True, True, ['kd', 'vr'], [bres])
                for hh in range(2):
                    h = hp * 2 + hh
                    STT('dve', Sst[:, h, :], Sst[:, h, :], sdec[:, gt * 8 + h:gt * 8 + h + 1], b[:, hh * 256:(hh + 1) * 256],
                        ALU.mult, ALU.add, ['Sst', bres, 'sdec'], ['Sst'])
            ACT(Sb[:], Sst[:], AF.Copy, ['Sst'], ['Sb'])

        def qside_ret(ntile, tile0):
            def ev_qr(gi, t, b, bres, ncols):
                r, rres = ropeload_g(tile0 + t)
                rope(b[:, 0:512], bres, 4, 128, r[:, 0:64], r[:, 64:128], qst[:], ['qst'], rres)
                defq.append(lambda: transposes(qst, lambda c0, n: qrT[:, gi * 4 + c0:gi * 4 + c0 + n, t * 128:(t + 1) * 128], 4, ['qst'], ['qrT']))

            gemm_tok(G_qr, hT, 'hT', ntile, ev_qr)

            def ev_gr(gi, t, b, bres, ncols):
                ACT(grs[:, t, gi * 512:(gi + 1) * 512], b[:, 0:512], AF.Silu, [bres], ['grs'])
            gemm_tok(G_gr, hT, 'hT', ntile, ev_gr)
            for t in range(ntile):
                tsl = slice(t * 128, (t + 1) * 128)
                for hq in range(2):
                    b = rot[hq]; bres = 'rot%d' % hq
                    for hh in range(4):
                        h = hq * 4 + hh
                        MM(b[:, hh * 128:(hh + 1) * 128], kpT[:, h, tsl], qrT[:, h, tsl], True, True, ['kpT', 'qrT'], [bres])
                    TT('dve', innT[:, hq * 4:hq * 4 + 4, :], b[:, 0:512].rearrange("p (h t) -> p h t", h=4),
                       trib[:].unsqueeze(1).to_broadcast([128, 4, 128]), ALU.mult, [bres, 'trib'], ['innT'])
                for hp in range(4):
                    b = rot[hp]; bres = 'rot%d' % hp
                    for hh in range(2):
                        h = hp * 2 + hh
                        MM(b[:, hh * 256:(hh + 1) * 256], innT[:, h, :], vr[:, t, h * 256:(h + 1) * 256], True, False,
                           ['innT', 'vr'], [bres])
                        MM(b[:, hh * 256:(hh + 1) * 256], qrT[:, h, tsl], Sb[:, h, :], False, True, ['qrT', 'Sb'], [bres])
                    for hh in range(2):
                        h = hp * 2 + hh
                        ACT(rt1[:, 0:256], b[:, hh * 256:(hh + 1) * 256], AF.Square, [bres, 'kq'], ['rt1', 'gss'],
                            scale=kq[:, 8 + h:9 + h], accum=stat[:, 8 + h:9 + h])
                ACT(stat[:, 16:24], stat[:, 8:16], AF.Sqrt, ['gss', 'eps'], ['gsq'], scale=1.0 / 256, bias=stat[:, 60:61])
                S.op('dve', lambda e: e.reciprocal(out=stat[:, 24:32], in_=stat[:, 16:24]), ['gsq'], ['grc'])
                TT('dve', stat[:, 32:40], stat[:, 24:32], kq[:, 8:16], ALU.mult, ['grc', 'kq'], ['gc'])
                for hp in range(4):
                    b = rot[hp]; bres = 'rot%d' % hp
                    for hh in range(2):
                        h = hp * 2 + hh
                        STT('dve', obb[:, h * 256:(h + 1) * 256], b[:, hh * 256:(hh + 1) * 256], stat[:, 32 + h:33 + h],
                            grs[:, t, h * 256:(h + 1) * 256], ALU.mult, ALU.mult, [bres, 'gc', 'grs'], ['obb'])
                if _DEBUG is not None and _DEBUG == (tile0 // 4, t) and ntile == 4:
                    finals.append(DMA('pool', dbg_ob[:, :], obb, ['obb'], [], 'dbg'))
                transposes(obb, lambda c0, n: obst[:, c0:c0 + n, :], 16, ['obb'], ['obst'])
                DMA('pool', obTs[:, :, t * 128:(t + 1) * 128], obst[:], ['obst'], ['obTs'], 'obw')
                state_update(t, tile0 + t)

        def qside_proj(ntile, tile0):
            def ev_qa(gi, t, b, bres, ncols):
                r, rres = ropeload_g(tile0 + t)
                rope(b[:, 0:512], bres, 4, 128, r[:, 0:64], r[:, 64:128], qst[:], ['qst'], rres)
                defq.append(lambda: transposes(qst, lambda c0, n: qaT[:, gi * 4 + c0:gi * 4 + c0 + n, t * 128:(t + 1) * 128], 4, ['qst'], ['qaT']))
            gemm_tok(G_qa, hT, 'hT', ntile, ev_qa)

            def ev_qi(gi, t, b, bres, ncols):
                r, rres = ropeload_g(tile0 + t)
                rope(b[:, 0:512], bres, 8, 64, r[:, 256:288], r[:, 288:320], qst[:], ['qst'], rres)
                defq.append(lambda: transposes(qst, lambda c0, n: qiT[:, gi * 4 + c0:gi * 4 + c0 + n, t * 128:(t + 1) * 128], 4, ['qst'], ['qiT']))
            gemm_tok(G_qi, hT, 'hT', ntile, ev_qi)
            NT = ntile * 128
            DMA('pool', hTs[:, :, 0:NT], hT[:, :, 0:NT], ['hT'], ['hTs'], 'hTw')

        def attention_tile(t, key_segs, mtype, use_mask0, dbg=False):
            sgs = []
            pos = 0
            for (o, n) in key_segs:
                for a in range(0, n, 512):
                    w = min(512, n - a)
                    sgs.append((o + a, w, pos)); pos += w
            L = pos
            tsl = slice(t * 128, (t + 1) * 128)
            for si, (so, w, p0) in enumerate(sgs):
                sl = si % 2
                DMA('sp', kis[sl][:, :, 0:w], kiTs[:, :, so:so + w].rearrange("v p t -> p v t"), ['scr0', 'scr1'], ['kis%d' % sl], ('kis', sl))
                for j in range(16):
                    hf = j % 2
                    b = rot[j % 4]; bres = 'rot%d' % (j % 4)
                    MM(b[:, 0:w], qiT[:, j // 2, tsl], kis[sl][:, hf, 0:w], True, True, ['qiT', 'kis%d' % sl], [bres])
                    rb = rbuf[j % 2]; rres = 'rb%d' % (j % 2)
                    ACT(rb[:, 0:w], b[:, 0:w], AF.Relu, [bres, 'wab'], [rres], scale=wab[:, t, j:j + 1])
                    if j == 0:
                        TS('dve', score[:, p0:p0 + w], rb[:, 0:w], wsg[:, t, 0:1], None, ALU.mult, None, [rres, 'wsg'], ['score'])
                    else:
                        STT('dve', score[:, p0:p0 + w], rb[:, 0:w], wsg[:, t, j:j + 1], score[:, p0:p0 + w], ALU.mult, ALU.add,
                            [rres, 'wsg', 'score'], ['score'])
            if _STAGE == 5.1:
                return
            lo, hi, rng, mid, cnt, tmp = (stat[:, 40:41], stat[:, 41:42], stat[:, 42:43], stat[:, 43:44], stat[:, 44:45], stat[:, 45:46])
            S.op('dve', lambda e: e.tensor_reduce(out=lo, in_=score[:, 0:L], axis=AX.X, op=ALU.min), ['score'], ['lo'])
            S.op('dve', lambda e: e.tensor_reduce(out=hi, in_=score[:, 0:L], axis=AX.X, op=ALU.max), ['score'], ['hi'])
            TT('dve', rng, hi, lo, ALU.subtract, ['lo', 'hi'], ['rng'])
            TS('dve', stp[:], pow2[:], rng, None, ALU.mult, None, ['rng', 'pow2'], ['stp'])
            TS('dve', stp2[:], pow2[:], rng, 2.0, ALU.mult, ALU.mult, ['rng', 'pow2'], ['stp'])
            TS('dve', stp2[:, 0:1], stp[:, NBIS - 1:NBIS], 1.125, None, ALU.mult, None, ['stp'], ['stp'])
            TT('dve', score[:, L - 128:L], score[:, L - 128:L], maskD[:, mtype * 128:(mtype + 1) * 128], ALU.add,
               ['score', 'maskD'], ['score'])
            if use_mask0:
                TT('dve', score[:, 0:512], score[:, 0:512], mask0[:], ALU.add, ['score', 'mask0'], ['score'])
            TT('dve', mid, lo, stp[:, 0:1], ALU.add, ['lo', 'stp'], ['mid'])
            Ld = max(128, int(round(L * 0.47 / 128.0)) * 128)
            if L - Ld < 256:
                Ld = L
            nact = L - Ld
            thr = 256.0 - 0.5 * nact
            sacc, t2 = stat[:, 46:47], stat[:, 47:48]
            for k in range(NBIS):
                TS('dve', mmask[:, 0:Ld], score[:, 0:Ld], mid, None, ALU.is_ge, ALU.add, ['score', 'mid'], ['mmask', 'cnt'], accum=cnt)
                if nact:
                    ACT(mmask[:, Ld:L], score[:, Ld:L], AF.Sign, ['score', 'mid'], ['mmaskA', 'sacc'], scale=-1.0, bias=mid, accum=sacc)
                    STT('dve', t2, sacc, -0.5, cnt, ALU.mult, ALU.add, ['sacc', 'cnt'], ['t2'])
                    csrc, cres = t2, 't2'
                else:
                    csrc, cres = cnt, 'cnt'
                if k + 1 < NBIS:
                    STT('dve', tmp, csrc, thr, stp2[:, k + 1:k + 2], ALU.is_ge, ALU.mult, [cres, 'stp'], ['tmp'])
                    STT('dve', mid, tmp, stp[:, k + 1:k + 2], mid, ALU.subtract, ALU.add, ['tmp', 'stp', 'mid'], ['mid'])
                else:
                    STT('dve', tmp, csrc, thr, stp2[:, 0:1], ALU.is_ge, ALU.mult, [cres, 'stp'], ['tmp'])
                    STT('dve', lo, tmp, stp2[:, 0:1], mid, ALU.subtract, ALU.add, ['tmp', 'stp', 'mid'], ['lo'])
            TS('dve', mmask[:, 0:L], score[:, 0:L], lo, -1.0, ALU.is_ge, ALU.add, ['score', 'lo'], ['mmask', 'mmaskA'])
            if _STAGE == 5.2:
                return
            for si, (so, w, p0) in enumerate(sgs):
                sl = si % 2
                nch = w // 128
                DMA('sp', kvk[sl][:, :, 0:w], kTs[:, :, so:so + w].rearrange("g p t -> p g t"), ['scr0', 'scr1'],
                    ['kvk%d' % sl], ('kvk', sl))
                DMA('sp', kvv[sl][:, 0:nch, :, :], Vs[so:so + w, :, :].rearrange("(c p) g e -> p c g e", p=128),
                    ['scr0', 'scr1'], ['kvv%d' % sl], ('kvv', sl))
                items = [(g, c) for g in range(4) for c in range(nch)]

                def emit_S(ix):
                    g, c = items[ix]
                    b = rot[ix % 2]; bres = 'rot%d' % (ix % 2)
                    MM(b[:, 0:512], kvk[sl][:, g, c * 128:(c + 1) * 128], qaT[:, 4 * g:4 * g + 4, tsl], True, False,
                       ['kvk%d' % sl, 'qaT'], [bres])
                    MM(b[:, 0:512], mmask[:, p0 + c * 128:p0 + (c + 1) * 128], i4big[:], False, True, ['mmask', 'i4big'], [bres])
                emit_S(0)
                for ix, (g, c) in enumerate(items):
                    b = rot[ix % 2]; bres = 'rot%d' % (ix % 2)
                    pb = PTb[ix % 2]; pres = 'PT%d' % (ix % 2)
                    ACT(pb, b[:, 0:512], AF.Exp, [bres], [pres], scale=float(128 ** -0.5))
                    if ix + 1 < len(items):
                        emit_S(ix + 1)
                    aset = (agc[0] % 2) * 2
                    for hh in range(4):
                        a = accl[aset + hh // 2]; ares = accr[aset + hh // 2]
                        MM(a[:, (hh % 2) * 256:(hh % 2) * 256 + 129], pb[:, hh * 128:(hh + 1) * 128], kvv[sl][:, c, g, 0:129],
                           (c == 0 and hh % 2 == 0), c == nch - 1, [pres, 'kvv%d' % sl], [ares], sgc=True)
                    if c == nch - 1:
                        agc[0] += 1
                        for hp in range(2):
                            a = accl[aset + hp]; ares = accr[aset + hp]
                            dst = oacc[:, 4 * g + 2 * hp:4 * g + 2 * hp + 2, :]
                            srcv = a[:, 0:512].rearrange("p (h e) -> p h e", h=2)[:, :, 0:129]
                            if si == 0:
                                S.op('dve', lambda e, dst=dst, srcv=srcv: e.tensor_copy(out=dst, in_=srcv), [ares], ['oacc'])
                            else:
                                TT('dve', dst, srcv, dst, ALU.add, [ares, 'oacc'], ['oacc'])
            S.op('dve', lambda e: e.reciprocal(out=rcp[:], in_=oacc[:, :, 128]), ['oacc'], ['rcp'])
            TT('dve', oab.rearrange("p (h d) -> p h d", h=16), oacc[:, :, 0:128], rcp[:].unsqueeze(2).to_broadcast([128, 16, 128]),
               ALU.mult, ['oacc', 'rcp'], ['oab'])
            if dbg:
                finals.append(DMA('pool', dbg_s[:, 0:min(L, 1024)], score[:, 0:min(L, 1024)], ['score'], [], 'dbg'))
                finals.append(DMA('pool', dbg_m[:, 0:min(L, 1024)], mmask[:, 0:min(L, 1024)], ['mmask'], [], 'dbg'))
                finals.append(DMA('pool', dbg_oa[:, :], oab, ['oab'], [], 'dbg'))
            transposes(oab, lambda c0, n: oaT[:, c0:c0 + n, tsl], 16, ['oab'], ['oaT'])

        def merge_ffn(ntile, xsrc_fn, yout_fn):
            NT = ntile * 128
            S.fence()
            DMA('sp', hTb[:, :, 0:NT], hTs[:, :, 0:NT], ['hTs'], ['hTb'], 'hTr')
            DMA('sp', obT[:, :, 0:NT], obTs[:, :, 0:NT], ['obTs'], ['obT'], 'obr')

            def ev_ga(gi, nn, b, bres):
                ACT(sA[:, gi * 4 + nn, 0:NT], b[:, 0:NT], AF.Sigmoid, [bres], ['sA'])
            gemm_feat(G_ga, hTb, 'hTb', NT, ev_ga)

            def ev_pa(gi, nn, b, bres):
                TT('dve', sA[:, gi * 4 + nn, 0:NT], b[:, 0:NT], sA[:, gi * 4 + nn, 0:NT], ALU.mult, [bres, 'sA'], ['sA'])
            gemm_feat(G_pa, oaT, 'oaT', NT, ev_pa)

            def ev_gb(gi, nn, b, bres):
                ACT(sB[:, gi * 4 + nn, 0:NT], b[:, 0:NT], AF.Sigmoid, [bres], ['sB'])
            gemm_feat(G_gb, hTb, 'hTb', NT, ev_gb)

            def ev_pb(gi, nn, b, bres):
                TT('dve', sB[:, gi * 4 + nn, 0:NT], b[:, 0:NT], sB[:, gi * 4 + nn, 0:NT], ALU.mult, [bres, 'sB'], ['sB'])
                TT('dve', mgT[:, gi * 4 + nn, 0:NT], sA[:, gi * 4 + nn, 0:NT], sB[:, gi * 4 + nn, 0:NT], ALU.add, ['sA', 'sB'], ['mgT'])
            gemm_feat(G_pb, obT, 'obT', NT, ev_pb)
            for t in range(ntile):
                DMA('sp', x2[:, t, :], xsrc_fn(t), [], ['x2_%d' % t], ('x2l', t))

            def ev_o(gi, t, b, bres, ncols):
                TT('dve', x2[:, t, gi * 512:(gi + 1) * 512], b[:, 0:512], x2[:, t, gi * 512:(gi + 1) * 512], ALU.add,
                   [bres, 'x2_%d' % t], ['x2_%d' % t])
            gemm_tok(G_o, mgT, 'mgT', ntile, ev_o)
            if _STAGE == 6.5:
                return
            S.fence()
            for t in range(ntile):
                norm_to_T(x2[:, t, :], 'x2_%d' % t, n2s, h2T, 'h2T', t, xsb, 'xsb')
            S.fence()
            DMA('sp', nfrep, nfr[:, :], [], ['nfrep'], 'nfl')

            def ev_gu(i, nn, b, bres):
                G = i // 2
                if i % 2 == 0:
                    ACT(sgt[:, nn, 0:NT], b[:, 0:NT], AF.Silu, [bres], ['sgt%d' % nn])
                else:
                    TT('dve', aT[:, G * 4 + nn, 0:NT], b[:, 0:NT], sgt[:, nn, 0:NT], ALU.mult, [bres, 'sgt%d' % nn], ['aT'])
            if _STAGE == 6.6:
                return
            gemm_feat(ffn_order, h2T, 'h2T', NT, ev_gu)
            if _STAGE == 6.7:
                return

            def ev_d(gi, t, b, bres, ncols):
                TT('dve', x2[:, t, gi * 512:(gi + 1) * 512], b[:, 0:512], x2[:, t, gi * 512:(gi + 1) * 512], ALU.add,
                   [bres, 'x2_%d' % t], ['x2_%d' % t])
                if gi == 3 and _STAGE != 6.8:
                    ACT(sgj, x2[:, t, :], AF.Square, ['x2_%d' % t], ['sgj', 'ss'], accum=stat[:, 0:1])
                    rstd_of(stat[:, 0:1], stat[:, 1:2], D, ['ss'], ['rstd'])
                    STT('dve', x2[:, t, :], x2[:, t, :], stat[:, 1:2], nfrep, ALU.mult, ALU.mult, ['x2_%d' % t, 'rstd', 'nfrep'], ['x2_%d' % t])
                    finals.append(DMA('pool', yout_fn(t), x2[:, t, :], ['x2_%d' % t], [], ('yo', t)))
            gemm_tok(G_d, aT, 'aT', ntile, ev_d)
            S.fence()

        stp = sb("stp", [128, NBIS], F32)
        stp2 = sb("stp2", [128, NBIS], F32)
        rcp = sb("rcp", [128, 16], F32)

        S.op('dve', lambda e: e.memset(Sst[:], 0.0), [], ['Sst'])
        S.op('dve', lambda e: e.memset(Sb[:], 0.0), [], ['Sb'])
        for vb in range(NV):
            own = (vb % 2 == 1)
            oi = vb // 2
            oo = None
            if own:
                oo = dict(k=lambda t, oi=oi: k_o[oi * 512 + t * 128:oi * 512 + (t + 1) * 128, :],
                          v=lambda t, oi=oi: v_o[oi * 512 + t * 128:oi * 512 + (t + 1) * 128, :],
                          ik=lambda t, oi=oi: ik_o[oi * 512 + t * 128:oi * 512 + (t + 1) * 128, :])
            xf = lambda t, vb=vb: xv[vb * 512 + t * 128:vb * 512 + (t + 1) * 128, :]
            kside(4, xf, vb * 512, vb * 4, vb * 512, oo, vb % 2)
            if (_STAGE <= 2 and vb >= _STAGEVB):
                S.emit(nc, finals); return nc
            if not own:
                for t in range(4):
                    state_update(t, vb * 4 + t)
                if (_STAGE <= 3 and vb >= _STAGEVB) or _STOPVB == vb:
                    S.emit(nc, finals); return nc
                continue
            qside_ret(4, vb * 4)
            if (_STAGE <= 4 and vb >= _STAGEVB):
                S.emit(nc, finals); return nc
            qside_proj(4, vb * 4)
            if (_STAGE <= 5 and vb >= _STAGEVB):
                S.emit(nc, finals); return nc
            S.fence()
            for t in range(4):
                attention_tile(t, [(0, vb * 512 + (t + 1) * 128)], 0, True, dbg=(_DEBUG == (vb, t)))
            if (_STAGE <= 6 and vb >= _STAGEVB):
                S.emit(nc, finals); return nc
            merge_ffn(4, xf, lambda t, oi=oi: y_o[oi * 512 + t * 128:oi * 512 + (t + 1) * 128, :])
            if (_STAGE <= 7 and vb >= _STAGEVB) or _STOPVB == vb:
                S.emit(nc, finals); return nc
        finals.append(DMA('pool', st_o[:, :, :], Sst[:], ['Sst'], [], 'sto'))
        S.fence()
        DMA('sp', Sst[:], st0[:, :, :], [], ['Sst'], 'stl')
        ACT(Sb[:], Sst[:], AF.Copy, ['Sst'], ['Sb'])
        for ct in range(8):
            tk = SOFF + ct * 128
            DMA('sp', kfst[:], ck[ct * 128:(ct + 1) * 128, :], [], ['kfst'], 'ckl')
            ACT(kbst[:], kfst[:], AF.Copy, ['kfst'], ['kbst'])
            transposes(kbst, lambda c0, n: kTst[:, c0:c0 + n, :], 4, ['kbst'], ['kTst'])
            DMA('pool', kTs[:, :, tk:tk + 128].rearrange("g p t -> p g t"), kTst[:], ['kTst'], ['scr0'], ('scw', 0))
            DMA('sp', vfst[:], cv[ct * 128:(ct + 1) * 128, :], [], ['vfst'], 'cvl')
            ACT(vbst[:, :, 0:128], vfst[:].rearrange("p (g e) -> p g e", g=4), AF.Copy, ['vfst'], ['vbst'])
            DMA('pool', Vs[tk:tk + 128, :, :], vbst[:], ['vbst'], ['scr0'], ('scw', 0))
            DMA('sp', kifst[:], ci[ct * 128:(ct + 1) * 128, :], [], ['kifst'], 'cil')
            ACT(kibst[:], kifst[:], AF.Copy, ['kifst'], ['kibst'])
            tb = trp[trc[0] % 2]; tres = 'trp%d' % (trc[0] % 2); trc[0] += 1
            TR(tb[0:64, 0:128], kibst[:], ['kibst'], [tres])
            ACT(kiTst[0:64, 0, :], tb[0:64, 0:128], AF.Copy, [tres], ['kiTst'])
            ACT(kiTst[64:128, 1, :], tb[0:64, 0:128], AF.Copy, [tres], ['kiTst'])
            DMA('pool', kiTs[:, :, tk:tk + 128].rearrange("v p t -> p v t"), kiTst[:], ['kiTst'], ['scr0'], ('scw', 0))
        soo = dict(k=lambda t: ks_o[:, :], v=lambda t: vs_o[:, :], ik=lambda t: iks_o[:, :])
        xsf = lambda t: xs[:, :]
        kside(1, xsf, SOFF + 1024, NV * 4, NV * 512, soo, 0)
        qside_ret(1, NV * 4)
        qside_proj(1, NV * 4)
        S.fence()
        attention_tile(0, [(SOFF, 1152)], 1, False)
        merge_ffn(1, xsf, lambda t: ys_o[:, :])
        finals.append(DMA('pool', sts_o[:, :, :], Sst[:], ['Sst'], [], 'sto'))
        S.emit(nc, finals)
    return nc


def _tables(NBR, h):
    NV = NBR + 1
    NTIL = NV * 4 + 1
    pos = np.zeros(NV * 512 + 128, np.float64)
    dummy = NBR if h == 1 else 0
    for v in range(NV):
        r = v if h == 1 else v - 1
        if v == dummy:
            r = 0
        pos[v * 512:(v + 1) * 512] = r * 512 + np.arange(512)
    pos[NV * 512:] = 1024 + np.arange(128)
    i64 = np.arange(64, dtype=np.float32) / 64
    i32 = np.arange(32, dtype=np.float32) / 32
    fA = (np.float32(10000.0) ** (-i64)).astype(np.float32)
    fI = (np.float32(10000.0) ** (-i32)).astype(np.float32)
    angA = pos.astype(np.float32)[:, None] * fA[None, :]
    angI = pos.astype(np.float32)[:, None] * fI[None, :]
    sc = np.float32(128 ** -0.5)
    rt = np.concatenate([np.cos(angA), np.sin(angA), np.cos(angA) * sc, np.sin(angA) * sc, np.cos(angI), np.sin(angI)], axis=1)
    lg = np.log1p(-(2.0 ** (-5.0 - np.arange(8, dtype=np.float64))))
    j = np.arange(128, dtype=np.float64)[:, None]
    kdec = np.zeros((128, NTIL, 8), np.float64)
    sdec = np.zeros((128, NTIL, 8), np.float64)
    for tl in range(NTIL):
        if tl == NTIL - 1:
            kdec[:, tl, :] = np.exp((31.0 - j) * lg[None, :]); sdec[:, tl, :] = np.exp(32.0 * lg)[None, :]
        elif tl // 4 == dummy:
            kdec[:, tl, :] = 1.0; sdec[:, tl, :] = 1.0
        else:
            kdec[:, tl, :] = np.exp((127.0 - j) * lg[None, :]); sdec[:, tl, :] = np.exp(128.0 * lg)[None, :]
    kq = np.concatenate([np.exp(-(j + 1.0) * lg[None, :]), np.exp((j + 1.0) * lg[None, :])], axis=1)
    tri = (np.arange(128)[:, None] <= np.arange(128)[None, :]).astype(np.float32)
    q = np.arange(128)[:, None]; s = np.arange(128)[None, :]
    mp = np.where((s // 64) <= (q // 64), 0.0, NEG)
    ms = np.where(s < 32, 0.0, NEG) + 0.0 * q
    maskD = np.concatenate([mp, ms], axis=1)
    mask0 = np.full((128, 512), NEG if h == 0 else 0.0)
    pow2 = np.tile((2.0 ** -(np.arange(NBIS) + 1.0))[None, :], (128, 1))
    f = np.float32
    return dict(rt=rt.astype(f), kdec=kdec.reshape(128, -1).astype(f), sdec=sdec.reshape(128, -1).astype(f), kq=kq.astype(f),
                idf=np.eye(128, dtype=f), tri=tri, maskD=maskD.astype(f), mask0=mask0.astype(f), pow2=pow2.astype(f))


_NC_CACHE = {}


def kernel(x_prompt, x_sample, cache_k, cache_v, cache_idx_k, state_ret, norm1_g, w_in, w_pa, w_pb, w_o, norm2_g,
           w_ffn_gate, w_ffn_up, w_ffn_down, norm_f_g):
    f = np.float32
    x_prompt = np.asarray(x_prompt, f); x_sample = np.asarray(x_sample, f)
    B, T, _ = x_prompt.shape
    DB, DS, _ = x_sample.shape
    NBR = T // 512
    NV = NBR + 1
    NOWN = NBR // 2
    ncore = 2 * B
    assert DB == ncore and DS == 32
    if NBR not in _NC_CACHE:
        _NC_CACHE[NBR] = build(NBR)
    nc = _NC_CACHE[NBR]
    shared = dict(
        w_in=np.ascontiguousarray(np.asarray(w_in, f)[0]), w_pa=np.ascontiguousarray(np.asarray(w_pa, f)[0]),
        w_pb=np.ascontiguousarray(np.asarray(w_pb, f)[0]), w_o=np.ascontiguousarray(np.asarray(w_o, f)[0]),
        w_g=np.ascontiguousarray(np.asarray(w_ffn_gate, f)[0]), w_u=np.ascontiguousarray(np.asarray(w_ffn_up, f)[0]),
        w_d=np.ascontiguousarray(np.asarray(w_ffn_down, f)[0]),
        n1t=np.ascontiguousarray(np.asarray(norm1_g, f)[0].reshape(16, 128).T),
        n2t=np.ascontiguousarray(np.asarray(norm2_g, f)[0].reshape(16, 128).T),
        nfr=np.ascontiguousarray(np.broadcast_to(np.asarray(norm_f_g, f)[None, :], (128, D))),
    )
    tabs = [_tables(NBR, 0), _tables(NBR, 1)]
    in_maps = []
    for c in range(ncore):
        b, h = c // 2, c % 2
        xvv = np.zeros((NV * 512, D), f)
        if h == 1:
            xvv[0:NBR * 512] = x_prompt[b]
        else:
            xvv[512:] = x_prompt[b]
        xsv = np.zeros((128, D), f); xsv[0:32] = x_sample[c]
        m = dict(shared)
        m.update(tabs[h])
        m.update(xv=xvv, xs=xsv,
                 ck=np.ascontiguousarray(np.asarray(cache_k, f)[0, c].reshape(1024, 512)),
                 cv=np.ascontiguousarray(np.asarray(cache_v, f)[0, c].reshape(1024, 512)),
                 ci=np.ascontiguousarray(np.asarray(cache_idx_k, f)[0, c]),
                 st0=np.ascontiguousarray(np.asarray(state_ret, f)[0, c].transpose(1, 0, 2)))
        in_maps.append(m)
    res = run_bass_kernel_spmd(nc, in_maps, core_ids=list(range(ncore)))
    R = res.results
    if _DEBUG is not None:
        kernel.dbg = [dict(s=r['dbg_s'], m=r['dbg_m'], oa=r['dbg_oa'], ob=r['dbg_ob']) for r in R]
    y_p = np.zeros((B, T, D), f); k_p = np.zeros((1, B, T, 4, 128), f); v_p = np.zeros((1, B, T, 4, 128), f)
    i_p = np.zeros((1, B, T, 64), f); s_p = np.zeros((1, B, 8, 128, 256), f)
    y_s = np.zeros((DB, DS, D), f); k_s = np.zeros((1, DB, DS, 4, 128), f); v_s = np.zeros((1, DB, DS, 4, 128), f)
    i_s = np.zeros((1, DB, DS, 64), f); s_s = np.zeros((1, DB, 8, 128, 256), f)
    for c in range(ncore):
        b, h = c // 2, c % 2
        r = R[c]
        for i in range(NOWN):
            rb = 2 * i + h
            sl = slice(rb * 512, (rb + 1) * 512)
            y_p[b, sl] = r["y_o"][i * 512:(i + 1) * 512]
            k_p[0, b, sl] = r["k_o"][i * 512:(i + 1) * 512].reshape(512, 4, 128)
            v_p[0, b, sl] = r["v_o"][i * 512:(i + 1) * 512].reshape(512, 4, 128)
            i_p[0, b, sl] = r["ik_o"][i * 512:(i + 1) * 512]
        if h == 0:
            s_p[0, b] = r["st_o"].transpose(1, 0, 2)
        y_s[c] = r["ys_o"][0:32]
        k_s[0, c] = r["ks_o"][0:32].reshape(32, 4, 128)
        v_s[0, c] = r["vs_o"][0:32].reshape(32, 4, 128)
        i_s[0, c] = r["iks_o"][0:32]
        s_s[0, c] = r["sts_o"].transpose(1, 0, 2)
    return (y_p, y_s, k_p, v_p, i_p, s_p, k_s, v_s, i_s, s_s)
```

```python
import contextlib
import numpy as np
import concourse.bass as bass
import concourse.mybir as mybir
from concourse.bass_utils import run_bass_kernel_spmd

F32 = mybir.dt.float32
BF16 = mybir.dt.bfloat16
ALU = mybir.AluOpType
AF = mybir.ActivationFunctionType
AX = mybir.AxisListType

D = 2048
FF = 5632
C_QA, C_KA, C_VA, C_QI, C_KI, C_WI, C_QR, C_KR, C_VR, C_GR, C_GA, C_GB = (
    0, 2048, 2560, 3072, 4096, 4160, 4176, 5200, 6224, 8272, 10320, 12368)
PTOT = 14416
NBIS = 14
NEG = -1.0e30
_STAGE = 99
_STOPVB = None
_STAGEVB = 0
_DEBUG = None


class Sched:
    ENG = ('pe', 'act', 'dve', 'pool', 'sp')

    def __init__(self):
        self.ins = {e: [] for e in self.ENG}
        self.lastw = {}
        self.readers = {}
        self.dma_cnt = {}
        self.seen = {e: {} for e in self.ENG}
        self.fence_toks = []
        self.fence_id = 0
        self.eng_fence = {e: 0 for e in self.ENG}

    def fence(self):
        toks = []
        for e in self.ENG:
            for i in range(len(self.ins[e]) - 1, -1, -1):
                if self.ins[e][i]['dma'] is None:
                    toks.append(('eng', e, i))
                    break
        for k, v in self.dma_cnt.items():
            toks.append(('dma', k, v))
        self.fence_toks = toks
        self.fence_id += 1

    def op(self, eng, fn, reads=(), writes=(), dma=None, nofence=False):
        idx = len(self.ins[eng])
        deps = []
        for r in reads:
            t = self.lastw.get(r)
            if t is not None:
                deps.append((t, 'raw'))
            if r.startswith(('rot', 'acc', 'trp')):
                for t in self.readers.get(r, ()):
                    if t[1] != eng:
                        deps.append((t, 'raw'))
        for w in writes:
            t = self.lastw.get(w)
            if t is not None:
                deps.append((t, 'waw'))
            for t in self.readers.get(w, ()):
                deps.append((t, 'war'))
        if not nofence and self.eng_fence[eng] < self.fence_id:
            self.eng_fence[eng] = self.fence_id
            for t in self.fence_toks:
                if not (t[0] == 'eng' and t[1] == eng):
                    deps.append((t, 'raw'))
        if dma is not None:
            self.dma_cnt[dma] = self.dma_cnt.get(dma, 0) + 16
            tok = ('dma', dma, self.dma_cnt[dma])
        else:
            tok = ('eng', eng, idx)
        waits = {}
        for t, kind in deps:
            if t[0] == 'eng':
                if t[1] == eng and dma is None:
                    if eng == 'pe':
                        continue
                    if kind == 'war' or idx - t[2] > 8:
                        continue
                key = ('eng', t[1])
            else:
                key = ('dma', t[1])
            val = t[2]
            if self.seen[eng].get(key, -1) >= val:
                continue
            if waits.get(key, -1) < val:
                waits[key] = val
        for k, v in waits.items():
            self.seen[eng][k] = v
        self.ins[eng].append(dict(fn=fn, waits=waits, dma=dma))
        for r in reads:
            self.readers.setdefault(r, []).append(tok)
        for w in writes:
            self.lastw[w] = tok
            self.readers[w] = []
        return tok

    def emit(self, nc, finals):
        miles = {e: set() for e in self.ENG}
        for e in self.ENG:
            for ins in self.ins[e]:
                for k, v in ins['waits'].items():
                    if k[0] == 'eng':
                        miles[k[1]].add(v)
        rank = {e: {s: i + 1 for i, s in enumerate(sorted(miles[e]))} for e in self.ENG}
        dkeys = sorted(self.dma_cnt.keys(), key=str)
        with contextlib.ExitStack() as st:
            psem = {e: st.enter_context(nc.semaphore("p_" + e)) for e in self.ENG}
            dsem = {k: st.enter_context(nc.semaphore("d_%d" % i)) for i, k in enumerate(dkeys)}
            block = st.enter_context(nc.Block())

            def run(e, eng):
                for i, ins in enumerate(self.ins[e]):
                    for k, v in ins['waits'].items():
                        if k[0] == 'eng':
                            eng.wait_ge(psem[k[1]], rank[k[1]][v])
                        else:
                            eng.wait_ge(dsem[k[1]], v)
                    bi = ins['fn'](eng)
                    if ins['dma'] is not None:
                        bi.then_inc(dsem[ins['dma']], 16)
                    elif i in rank[e]:
                        bi.then_inc(psem[e], 1)
                if e == 'sp':
                    fin = {}
                    for t in finals:
                        fin[t[1]] = max(fin.get(t[1], 0), t[2])
                    for k, v in fin.items():
                        eng.wait_ge(dsem[k], v)

            @block.tensor
            def _(eng):
                run('pe', eng)

            @block.scalar
            def _(eng):
                run('act', eng)

            @block.vector
            def _(eng):
                run('dve', eng)

            @block.gpsimd
            def _(eng):
                run('pool', eng)

            @block.sync
            def _(eng):
                run('sp', eng)


def build(NBR):
    NV = NBR + 1
    NOWN = NBR // 2
    NTIL = NV * 4 + 1
    SOFF = NV * 512
    NTOK = SOFF + 1152
    nc = bass.Bass("TRN2", target_bir_lowering=False)
    S = Sched()
    finals = []

    def din(name, shape):
        return nc.dram_tensor(name, shape, F32, kind="ExternalInput").ap()

    def dout(name, shape):
        return nc.dram_tensor(name, shape, F32, kind="ExternalOutput").ap()

    def dscr(name, shape, dt=BF16):
        return nc.dram_tensor(name, shape, dt, kind="Internal").ap()

    xv = din("xv", [NV * 512, D]); xs = din("xs", [128, D])
    ck = din("ck", [1024, 512]); cv = din("cv", [1024, 512]); ci = din("ci", [1024, 64])
    st0 = din("st0", [128, 8, 256])
    w_in = din("w_in", [D, PTOT]); w_pa = din("w_pa", [D, D]); w_pb = din("w_pb", [D, D]); w_o = din("w_o", [D, D])
    w_g = din("w_g", [D, FF]); w_u = din("w_u", [D, FF]); w_d = din("w_d", [FF, D])
    n1t = din("n1t", [128, 16]); n2t = din("n2t", [128, 16]); nfr = din("nfr", [128, D])
    rt = din("rt", [NV * 512 + 128, 320])
    kdec_d = din("kdec", [128, NTIL * 8]); sdec_d = din("sdec", [128, NTIL * 8])
    kq_d = din("kq", [128, 16])
    idf_d = din("idf", [128, 128]); tri_d = din("tri", [128, 128])
    maskD_d = din("maskD", [128, 256]); mask0_d = din("mask0", [128, 512]); pow2_d = din("pow2", [128, NBIS])

    y_o = dout("y_o", [NOWN * 512, D]); k_o = dout("k_o", [NOWN * 512, 512]); v_o = dout("v_o", [NOWN * 512, 512])
    ik_o = dout("ik_o", [NOWN * 512, 64]); st_o = dout("st_o", [128, 8, 256])
    ys_o = dout("ys_o", [128, D]); ks_o = dout("ks_o", [128, 512]); vs_o = dout("vs_o", [128, 512])
    iks_o = dout("iks_o", [128, 64]); sts_o = dout("sts_o", [128, 8, 256])

    if _DEBUG is not None:
        dbg_s = dout("dbg_s", [128, 1024]); dbg_m = dout("dbg_m", [128, 1024]); dbg_oa = dout("dbg_oa", [128, 2048]); dbg_ob = dout("dbg_ob", [128, 2048])
    kTs = dscr("kTs", [4, 128, NTOK]); Vs = dscr("Vs", [NTOK, 4, 136]); kiTs = dscr("kiTs", [2, 128, NTOK])
    hTs = dscr("hTs", [128, 16, 512]); obTs = dscr("obTs", [128, 16, 512])

    ucount = [0]

    def mkgroups(wap, K, c0, ncols_total, gw=512):
        groups = []
        kch = K // 128
        for g0 in range(0, ncols_total, gw):
            ncols = min(gw, ncols_total - g0)
            units = []
            for k0 in range(0, kch, 16):
                kc = min(16, kch - k0)
                uid = ucount[0]; ucount[0] += 1
                scr = dscr("wu%d" % uid, [128, kc, ncols])
                src = wap[k0 * 128:(k0 + kc) * 128, c0 + g0:c0 + g0 + ncols].rearrange("(c p) n -> p c n", p=128)
                units.append(dict(uid=uid, scr=scr, src=src, kc=kc, k0=k0, ncols=ncols, res="wu%d" % uid))
            groups.append(units)
        return groups

    G_ka = mkgroups(w_in, D, C_KA, 512); G_va = mkgroups(w_in, D, C_VA, 512); G_ki = mkgroups(w_in, D, C_KI, 80)
    G_kr = mkgroups(w_in, D, C_KR, 1024); G_vr = mkgroups(w_in, D, C_VR, 2048)
    G_qr = mkgroups(w_in, D, C_QR, 1024); G_gr = mkgroups(w_in, D, C_GR, 2048)
    G_qa = mkgroups(w_in, D, C_QA, 2048); G_qi = mkgroups(w_in, D, C_QI, 1024)
    G_ga = mkgroups(w_in, D, C_GA, 2048); G_pa = mkgroups(w_pa, D, 0, 2048)
    G_gb = mkgroups(w_in, D, C_GB, 2048); G_pb = mkgroups(w_pb, D, 0, 2048)
    G_o = mkgroups(w_o, D, 0, 2048)
    G_g = mkgroups(w_g, D, 0, FF); G_u = mkgroups(w_u, D, 0, FF); G_d = mkgroups(w_d, FF, 0, 2048)
    allgroups = [G_ka, G_va, G_ki, G_kr, G_vr, G_qr, G_gr, G_qa, G_qi, G_ga, G_pa, G_gb, G_pb, G_o]
    ffn_order = []
    for i in range(len(G_g)):
        ffn_order += [G_g[i], G_u[i]]

    with contextlib.ExitStack() as st:
        def sb(name, shape, dt):
            return st.enter_context(nc.sbuf_tensor("s_" + name, shape, dt))

        def pst(name, shape, dt):
            return st.enter_context(nc.psum_tensor("p_" + name, shape, dt))

        wslot = [sb("ws0", [128, 16, 512], BF16), sb("ws1", [128, 16, 512], BF16)]
        Sst = sb("Sst", [128, 8, 256], F32); Sb = sb("Sb", [128, 8, 256], BF16)
        oaT = sb("oaT", [128, 16, 512], BF16)
        identf = sb("identf", [128, 128], F32); identb = sb("identb", [128, 128], BF16)
        trif = sb("trif", [128, 128], F32); trib = sb("trib", [128, 128], BF16)
        i4big = sb("i4big", [128, 4, 128], BF16)
        kdec = sb("kdec", [128, NTIL * 8], F32); sdec = sb("sdec", [128, NTIL * 8], F32)
        kq = sb("kq", [128, 16], F32)
        n1s = sb("n1s", [128, 16], F32); n2s = sb("n2s", [128, 16], F32)
        pow2 = sb("pow2", [128, NBIS], F32)
        maskD = sb("maskD", [128, 256], F32); mask0 = sb("mask0", [128, 512], F32)
        rts = [sb("rt0", [128, 320], F32), sb("rt1", [128, 320], F32)]
        rt1 = sb("rtmp1", [128, 256], F32); rt2 = sb("rtmp2", [128, 256], F32)
        stat = sb("stat", [128, 64], F32)
        wab = sb("wab", [128, 4, 16], F32); wsg = sb("wsg", [128, 4, 16], F32)
        UB = 116736
        U = sb("U", [128, UB // 4], F32)

        def uv(off, nbytes, dt, pat=None, **kw):
            a = U[:, off // 4:(off + nbytes) // 4]
            if dt == BF16:
                a = a.bitcast(BF16)
            if pat:
                a = a.rearrange(pat, **kw)
            return a

        K1 = 1024
        qaT = uv(0, 16 * K1, BF16, "p (c t) -> p c t", c=16)
        qiT = uv(16 * K1, 8 * K1, BF16, "p (c t) -> p c t", c=8)
        hT = uv(24 * K1, 16 * K1, BF16, "p (c t) -> p c t", c=16)
        xst = uv(40 * K1, 8 * K1, F32)
        kd = uv(48 * K1, 8 * K1, BF16, "p (t c) -> p t c", t=4)
        kpT = uv(56 * K1, 8 * K1, BF16, "p (c t) -> p c t", c=8)
        qrT = uv(64 * K1, 8 * K1, BF16, "p (c t) -> p c t", c=8)
        vr = uv(72 * K1, 16 * K1, BF16, "p (t c) -> p t c", t=4)
        grs = uv(88 * K1, 16 * K1, BF16, "p (t c) -> p t c", t=4)
        obb = uv(104 * K1, 4 * K1, BF16)
        obst = uv(40 * K1, 4 * K1, BF16, "p (c t) -> p c t", c=16)
        innT = uv(108 * K1, 2 * K1, BF16, "p (h t) -> p h t", h=8)
        score = uv(24 * K1, 32 * K1, F32)
        mmask = uv(56 * K1, 16 * K1, BF16)
        kvk = [uv(72 * K1 + i * 8704, 4096, BF16, "p (g t) -> p g t", g=4) for i in range(2)]
        kvv = [uv(72 * K1 + i * 8704 + 4096, 4352, BF16, "p (c g e) -> p c g e", c=4, g=4) for i in range(2)]
        o3 = 72 * K1 + 2 * 8704
        kis = [uv(o3 + i * 2 * K1, 2 * K1, BF16, "p (v t) -> p v t", v=2) for i in range(2)]
        rbuf = [uv(o3 + 4 * K1 + i * K1, K1, BF16) for i in range(4)]
        PTb = [uv(o3 + 8 * K1 + i * K1, K1, BF16) for i in range(4)]
        oacc = uv(o3 + 12 * K1, 8256, F32, "p (h e) -> p h e", h=16)
        oab = uv(o3 + 12 * K1 + 8256, 4 * K1, BF16)
        hTb = uv(0, 16 * K1, BF16, "p (c t) -> p c t", c=16)
        obT = uv(16 * K1, 16 * K1, BF16, "p (c t) -> p c t", c=16)
        sA = uv(32 * K1, 16 * K1, BF16, "p (c t) -> p c t", c=16)
        sB = uv(48 * K1, 16 * K1, BF16, "p (c t) -> p c t", c=16)
        mgT = uv(64 * K1, 16 * K1, BF16, "p (c t) -> p c t", c=16)
        x2 = uv(80 * K1, 32 * K1, F32, "p (t c) -> p t c", t=4)
        xsb = uv(32 * K1, 4 * K1, BF16)
        h2T = uv(0, 16 * K1, BF16, "p (c t) -> p c t", c=16)
        aT = uv(16 * K1, 44 * K1, BF16, "p (c t) -> p c t", c=44)
        sgt = uv(60 * K1, 4 * K1, BF16, "p (c t) -> p c t", c=4)
        nfrep = uv(64 * K1, 8 * K1, F32)
        sgj = uv(60 * K1, 4 * K1, BF16)
        kfst = sb("kfst", [128, 512], F32); vfst = sb("vfst", [128, 512], F32); kifst = sb("kifst", [128, 64], F32)
        kbst = sb("kbst", [128, 512], BF16); kTst = sb("kTst", [128, 4, 128], BF16)
        vbst = sb("vbst", [128, 4, 136], BF16); kibst = sb("kibst", [128, 64], BF16); kiTst = sb("kiTst", [128, 2, 128], BF16)
        qst = sb("qst", [128, 512], BF16)

        rot = [pst("rot%d" % i, [128, 512], F32) for i in range(4)]
        trp = [pst("trp%d" % i, [128, 1024], BF16) for i in range(2)]
        acc = [pst("acc%d" % i, [128, 512], F32) for i in range(2)]
        trc = [0]
        accl = [acc[0], acc[1], rot[2], rot[3]]
        accr = ['acc0', 'acc1', 'rot2', 'rot3']
        agc = [0]
        sbanks = [rot[0], rot[1], trp[0][:, :].bitcast(F32), trp[1][:, :].bitcast(F32)]
        sbres = ['rot0', 'rot1', 'trp0', 'trp1']

        def MM(out, lhsT, rhs, start, stop, R, W, sgc=False):
            if sgc:
                S.op('pe', lambda e: e.matmul(out, lhsT, rhs, start=start, stop=stop, skip_group_check=True), R, W)
            else:
                S.op('pe', lambda e: e.matmul(out, lhsT, rhs, start=start, stop=stop), R, W)

        def TR(out, in_, R, W):
            S.op('pe', lambda e: e.transpose(out=out, in_=in_, identity=identb[:]), list(R) + ['identb'], W)

        def ACT(out, in_, func, R, W, scale=None, bias=None, accum=None):
            kw = {}
            if scale is not None:
                kw['scale'] = scale
            if bias is not None:
                kw['bias'] = bias
            if accum is not None:
                kw['accum_out'] = accum
            S.op('act', lambda e: e.activation(out=out, in_=in_, func=func, **kw), R, W)

        def TS(eng, out, in0, s1, s2, op0, op1, R, W, accum=None):
            kw = {}
            if op1 is not None:
                kw['op1'] = op1
            if accum is not None:
                kw['accum_out'] = accum
            S.op(eng, lambda e: e.tensor_scalar(out=out, in0=in0, scalar1=s1, scalar2=s2, op0=op0, **kw), R, W)

        def TT(eng, out, in0, in1, op, R, W):
            S.op(eng, lambda e: e.tensor_tensor(out=out, in0=in0, in1=in1, op=op), R, W)

        def STT(eng, out, in0, scalar, in1, op0, op1, R, W):
            S.op(eng, lambda e: e.scalar_tensor_tensor(out=out, in0=in0, scalar=scalar, in1=in1, op0=op0, op1=op1), R, W)

        def DMA(eng, out, in_, R, W, key, nofence=False):
            return S.op(eng, lambda e: e.dma_start(out=out, in_=in_), R, W, dma=key, nofence=nofence)

        for dst, src, nm in ((identf[:], idf_d, 'identf'), (trif[:], tri_d, 'trif'), (kdec[:], kdec_d, 'kdec'),
                             (sdec[:], sdec_d, 'sdec'), (kq[:], kq_d, 'kq'), (n1s[:], n1t, 'n1s'), (n2s[:], n2t, 'n2s'),
                             (pow2[:], pow2_d, 'pow2'), (maskD[:], maskD_d, 'maskD'), (mask0[:], mask0_d, 'mask0')):
            DMA('sp', dst, src[:, :], [], [nm], 'const', nofence=True)
        ACT(identb[:], identf[:], AF.Copy, ['identf'], ['identb'])
        ACT(trib[:], trif[:], AF.Copy, ['trif'], ['trib'])
        for j in range(4):
            ACT(i4big[:, j, :], identf[:], AF.Copy, ['identf'], ['i4big'], scale=30000.0)
        S.op('dve', lambda e: e.memset(stat[:, 60:61], 1e-6), [], ['eps'])
        S.op('dve', lambda e: e.memset(vbst[:], 1.0), [], ['vbst'])
        S.op('dve', lambda e: e.memset(kiTst[:], 0.0), [], ['kiTst'])

        cast_order = []
        for G in allgroups:
            for units in G:
                cast_order += units
        for units in ffn_order:
            cast_order += units
        for units in G_d:
            cast_order += units
        for i, u in enumerate(cast_order):
            DMA('pool', u['scr'][:, :, :], u['src'], [], [u['res']], ('wc', i % 8), nofence=True)

        if _STAGE <= 1:
            S.emit(nc, finals); return nc
        wcnt = [0]

        def wload(u):
            s = wcnt[0] % 2
            wcnt[0] += 1
            DMA('sp', wslot[s][:, 0:u['kc'], 0:u['ncols']], u['scr'][:, :, :], [u['res']], ['ws%d' % s], ('wl', s), nofence=True)
            return s

        def stream(ulist, body):
            slots = {}
            if ulist:
                slots[0] = wload(ulist[0])
            for i, u in enumerate(ulist):
                if i + 1 < len(ulist):
                    slots[i + 1] = wload(ulist[i + 1])
                s = slots[i]
                body(i, u, wslot[s], 'ws%d' % s)

        defq = []

        def run_deferred():
            while defq:
                defq.pop(0)()

        def gemm_tok(groups, actT, actres, ntile, evac):
            flat = []
            for gi, units in enumerate(groups):
                for ui, u in enumerate(units):
                    flat.append((gi, ui, len(units), u))

            def body(i, u, slot, sres):
                gi, ui, nu, _ = flat[i]
                for t in range(ntile):
                    b = rot[t]
                    for c in range(u['kc']):
                        MM(b[:, 0:u['ncols']], actT[:, u['k0'] + c, t * 128:(t + 1) * 128], slot[:, c, 0:u['ncols']],
                           (ui == 0 and c == 0), (ui == nu - 1 and c == u['kc'] - 1), [actres, sres], ['rot%d' % t])
                    run_deferred()
                    if ui == nu - 1:
                        evac(gi, t, b, 'rot%d' % t, u['ncols'])
            stream([f[3] for f in flat], body)
            run_deferred()

        def gemm_feat(groups, actT, actres, NT, evac):
            flat = [units[0] for units in groups]

            def body(i, u, slot, sres):
                for nn in range(u['ncols'] // 128):
                    b = rot[nn]
                    for c in range(16):
                        MM(b[:, 0:NT], slot[:, c, nn * 128:(nn + 1) * 128], actT[:, c, 0:NT], c == 0, c == 15,
                           [actres, sres], ['rot%d' % nn])
                    evac(i, nn, b, 'rot%d' % nn)
            stream(flat, body)

        def transposes(src, dstfn, nchunk, R, W, evac_eng='act', scale_ap=None):
            for c0 in range(0, nchunk, 8):
                n = min(8, nchunk - c0)
                tb = trp[trc[0] % 2]; tres = 'trp%d' % (trc[0] % 2); trc[0] += 1
                for j in range(n):
                    TR(tb[:, j * 128:(j + 1) * 128], src[:, (c0 + j) * 128:(c0 + j + 1) * 128], R, [tres])
                dst = dstfn(c0, n)
                srcv = tb[:, 0:n * 128].rearrange("p (c t) -> p c t", c=n)
                if scale_ap is not None:
                    TT('dve', dst, srcv, scale_ap[:, c0:c0 + n].unsqueeze(2).to_broadcast([128, n, 128]), ALU.mult,
                       [tres] + list(R), W)
                elif evac_eng == 'act':
                    ACT(dst, srcv, AF.Copy, [tres], W)
                else:
                    S.op('dve', lambda e: e.tensor_copy(out=dst, in_=srcv), [tres], W)

        def rope(ps, psres, nh, hd, cos, sin, d, W, rsl):
            half = hd // 2
            x = ps.rearrange("p (h t d) -> p h t d", h=nh, t=2)
            dv = d.rearrange("p (h t d) -> p h t d", h=nh, t=2)
            t1 = rt1[:, 0:nh * half].rearrange("p (h d) -> p h d", h=nh)
            t2 = rt2[:, 0:nh * half].rearrange("p (h d) -> p h d", h=nh)
            cb = cos.unsqueeze(1).to_broadcast([128, nh, half])
            sbb = sin.unsqueeze(1).to_broadcast([128, nh, half])
            TT('dve', t1, x[:, :, 0, :], cb, ALU.mult, [psres, rsl], ['rt1'])
            TT('dve', t2, x[:, :, 1, :], sbb, ALU.mult, [psres, rsl], ['rt2'])
            TT('dve', dv[:, :, 0, :], t1, t2, ALU.subtract, ['rt1', 'rt2'], W)
            TT('dve', t1, x[:, :, 1, :], cb, ALU.mult, [psres, rsl], ['rt1'])
            TT('dve', t2, x[:, :, 0, :], sbb, ALU.mult, [psres, rsl], ['rt2'])
            TT('dve', dv[:, :, 1, :], t1, t2, ALU.add, ['rt1', 'rt2'], W)

        def rstd_of(ss_col, out_col, n, R, W):
            ACT(stat[:, 62:63], ss_col, AF.Sqrt, list(R) + ['eps'], ['sq'], scale=1.0 / n, bias=stat[:, 60:61])
            S.op('dve', lambda e: e.reciprocal(out=out_col, in_=stat[:, 62:63]), ['sq'], W)

        def norm_to_T(xt, xres, gts, dstT, dres, t, tmpb, tmpres):
            ACT(tmpb, xt, AF.Square, [xres], [tmpres, 'ss'], accum=stat[:, 0:1])
            rstd_of(stat[:, 0:1], stat[:, 1:2], D, ['ss'], ['rstd'])
            ACT(tmpb, xt, AF.Copy, [xres, 'rstd'], [tmpres], scale=stat[:, 1:2])
            transposes(tmpb, lambda c0, n: dstT[:, c0:c0 + n, t * 128:(t + 1) * 128], 16, [tmpres], [dres], scale_ap=gts)

        rtc = [0]

        def ropeload_g(gt):
            sl = rtc[0] % 2
            rtc[0] += 1
            rr = gt * 128
            DMA('sp', rts[sl][:], rt[rr:rr + 128, :], [], ['rts%d' % sl], ('rt', sl))
            return rts[sl], 'rts%d' % sl

        def kside(ntile, xsrc_fn, tok0, tile0, rrow0, own_out, spar):
            NT = ntile * 128
            scr_key = ('scw', spar)
            sres = 'scr%d' % spar
            for t in range(ntile):
                DMA('sp', xst, xsrc_fn(t), [], ['xst'], 'xst')
                norm_to_T(xst, 'xst', n1s, hT, 'hT', t, obb, 'obb')

            def ropeload(t):
                return ropeload_g(tile0 + t)

            def ev_ka(gi, t, b, bres, ncols):
                r, rres = ropeload(t)
                rope(b[:, 0:512], bres, 4, 128, r[:, 0:64], r[:, 64:128], kfst[:], ['kfst'], rres)
                if own_out:
                    finals.append(DMA('pool', own_out['k'](t), kfst[:], ['kfst'], [], 'ko'))
                ACT(kbst[:], kfst[:], AF.Copy, ['kfst'], ['kbst'])

                def later():
                    transposes(kbst, lambda c0, n: kTst[:, c0:c0 + n, :], 4, ['kbst'], ['kTst'])
                    tk = tok0 + t * 128
                    DMA('pool', kTs[:, :, tk:tk + 128].rearrange("g p t -> p g t"), kTst[:], ['kTst'], [sres], scr_key)
                defq.append(later)
            gemm_tok(G_ka, hT, 'hT', ntile, ev_ka)

            def ev_va(gi, t, b, bres, ncols):
                ACT(vfst[:], b[:, 0:512], AF.Copy, [bres], ['vfst'])
                if own_out:
                    finals.append(DMA('pool', own_out['v'](t), vfst[:], ['vfst'], [], 'vo'))
                ACT(vbst[:, :, 0:128], b[:, 0:512].rearrange("p (g e) -> p g e", g=4), AF.Copy, [bres], ['vbst'])
                tk = tok0 + t * 128
                DMA('pool', Vs[tk:tk + 128, :, :], vbst[:], ['vbst'], [sres], scr_key)
            gemm_tok(G_va, hT, 'hT', ntile, ev_va)

            def ev_ki(gi, t, b, bres, ncols):
                r, rres = ropeload(t)
                rope(b[:, 0:64], bres, 1, 64, r[:, 256:288], r[:, 288:320], kifst[:], ['kifst'], rres)
                if own_out:
                    finals.append(DMA('pool', own_out['ik'](t), kifst[:], ['kifst'], [], 'iko'))
                    ACT(wab[:, t, :], b[:, 64:80], AF.Abs, [bres], ['wab'], scale=1.0 / 32.0)
                    ACT(wsg[:, t, :], b[:, 64:80], AF.Sign, [bres], ['wsg'])
                ACT(kibst[:], kifst[:], AF.Copy, ['kifst'], ['kibst'])

                def later():
                    tb = trp[trc[0] % 2]; tres = 'trp%d' % (trc[0] % 2); trc[0] += 1
                    TR(tb[0:64, 0:128], kibst[:], ['kibst'], [tres])
                    ACT(kiTst[0:64, 0, :], tb[0:64, 0:128], AF.Copy, [tres], ['kiTst'])
                    ACT(kiTst[64:128, 1, :], tb[0:64, 0:128], AF.Copy, [tres], ['kiTst'])
                    tk = tok0 + t * 128
                    DMA('pool', kiTs[:, :, tk:tk + 128].rearrange("v p t -> p v t"), kiTst[:], ['kiTst'], [sres], scr_key)
                defq.append(later)
            gemm_tok(G_ki, hT, 'hT', ntile, ev_ki)

            def ev_kr(gi, t, b, bres, ncols):
                r, rres = ropeload(t)
                rope(b[:, 0:512], bres, 4, 128, r[:, 128:192], r[:, 192:256], kfst[:], ['kfst'], rres)
                krf = kfst[:].rearrange("p (h d) -> p h d", h=4)
                kdv = kdec[:, (tile0 + t) * 8 + gi * 4:(tile0 + t) * 8 + gi * 4 + 4].unsqueeze(2).to_broadcast([128, 4, 128])
                TT('dve', kd[:, t, gi * 512:(gi + 1) * 512].rearrange("p (h d) -> p h d", h=4), krf, kdv, ALU.mult,
                   ['kfst', 'kdec'], ['kd'])
                if own_out:
                    kiv = kq[:, gi * 4:gi * 4 + 4].unsqueeze(2).to_broadcast([128, 4, 128])
                    TT('dve', qst[:].rearrange("p (h d) -> p h d", h=4), krf, kiv, ALU.mult, ['kfst', 'kq'], ['qst'])
                    defq.append(lambda: transposes(qst, lambda c0, n: kpT[:, gi * 4 + c0:gi * 4 + c0 + n, t * 128:(t + 1) * 128], 4,
                                                   ['qst'], ['kpT']))
            gemm_tok(G_kr, hT, 'hT', ntile, ev_kr)

            def ev_vr(gi, t, b, bres, ncols):
                ACT(vr[:, t, gi * 512:(gi + 1) * 512], b[:, 0:512], AF.Copy, [bres], ['vr'])
            gemm_tok(G_vr, hT, 'hT', ntile, ev_vr)

        def state_update(t, gt):
            for hp in range(4):
                b = rot[hp]; bres = 'rot%d' % hp
                for hh in range(2):
                    h = hp * 2 + hh
                    MM(b[:, hh * 256:(hh + 1) * 256], kd[:, t, h * 128:(h + 1) * 128], vr[:, t, h * 256:(h + 1) * 256],
                       True, True, ['kd', 'vr'], [bres])
                for hh in range(2):
                    h = hp * 2 + hh
                    STT('dve', Sst[:, h, :], Sst[:, h, :], sdec[:, gt * 8 + h:gt * 8 + h + 1], b[:, hh * 256:(hh + 1) * 256],
                        ALU.mult, ALU.add, ['Sst', bres, 'sdec'], ['Sst'])
            ACT(Sb[:], Sst[:], AF.Copy, ['Sst'], ['Sb'])

        def qside_ret(ntile, tile0):
            def ev_qr(gi, t, b, bres, ncols):
                r, rres = ropeload_g(tile0 + t)
                rope(b[:, 0:512], bres, 4, 128, r[:, 0:64], r[:, 64:128], qst[:], ['qst'], rres)
                defq.append(lambda: transposes(qst, lambda c0, n: qrT[:, gi * 4 + c0:gi * 4 + c0 + n, t * 128:(t + 1) * 128], 4, ['qst'], ['qrT']))

            gemm_tok(G_qr, hT, 'hT', ntile, ev_qr)

            def ev_gr(gi, t, b, bres, ncols):
                ACT(grs[:, t, gi * 512:(gi + 1) * 512], b[:, 0:512], AF.Silu, [bres], ['grs'])
            gemm_tok(G_gr, hT, 'hT', ntile, ev_gr)
            for t in range(ntile):
                tsl = slice(t * 128, (t + 1) * 128)
                for hq in range(2):
                    b = rot[hq]; bres = 'rot%d' % hq
                    for hh in range(4):
                        h = hq * 4 + hh
                        MM(b[:, hh * 128:(hh + 1) * 128], kpT[:, h, tsl], qrT[:, h, tsl], True, True, ['kpT', 'qrT'], [bres])
                    TT('dve', innT[:, hq * 4:hq * 4 + 4, :], b[:, 0:512].rearrange("p (h t) -> p h t", h=4),
                       trib[:].unsqueeze(1).to_broadcast([128, 4, 128]), ALU.mult, [bres, 'trib'], ['innT'])
                for hp in range(4):
                    b = rot[hp]; bres = 'rot%d' % hp
                    for hh in range(2):
                        h = hp * 2 + hh
                        MM(b[:, hh * 256:(hh + 1) * 256], innT[:, h, :], vr[:, t, h * 256:(h + 1) * 256], True, False,
                           ['innT', 'vr'], [bres])
                        MM(b[:, hh * 256:(hh + 1) * 256], qrT[:, h, tsl], Sb[:, h, :], False, True, ['qrT', 'Sb'], [bres])
                    for hh in range(2):
                        h = hp * 2 + hh
                        ACT(rt1[:, 0:256], b[:, hh * 256:(hh + 1) * 256], AF.Square, [bres, 'kq'], ['rt1', 'gss'],
                            scale=kq[:, 8 + h:9 + h], accum=stat[:, 8 + h:9 + h])
                ACT(stat[:, 16:24], stat[:, 8:16], AF.Sqrt, ['gss', 'eps'], ['gsq'], scale=1.0 / 256, bias=stat[:, 60:61])
                S.op('dve', lambda e: e.reciprocal(out=stat[:, 24:32], in_=stat[:, 16:24]), ['gsq'], ['grc'])
                TT('dve', stat[:, 32:40], stat[:, 24:32], kq[:, 8:16], ALU.mult, ['grc', 'kq'], ['gc'])
                for hp in range(4):
                    b = rot[hp]; bres = 'rot%d' % hp
                    for hh in range(2):
                        h = hp * 2 + hh
                        STT('dve', obb[:, h * 256:(h + 1) * 256], b[:, hh * 256:(hh + 1) * 256], stat[:, 32 + h:33 + h],
                            grs[:, t, h * 256:(h + 1) * 256], ALU.mult, ALU.mult, [bres, 'gc', 'grs'], ['obb'])
                if _DEBUG is not None and _DEBUG == (tile0 // 4, t) and ntile == 4:
                    finals.append(DMA('pool', dbg_ob[:, :], obb, ['obb'], [], 'dbg'))
                transposes(obb, lambda c0, n: obst[:, c0:c0 + n, :], 16, ['obb'], ['obst'])
                DMA('pool', obTs[:, :, t * 128:(t + 1) * 128], obst[:], ['obst'], ['obTs'], 'obw')
                state_update(t, tile0 + t)

        def qside_proj(ntile, tile0):
            def ev_qa(gi, t, b, bres, ncols):
                r, rres = ropeload_g(tile0 + t)
                rope(b[:, 0:512], bres, 4, 128, r[:, 0:64], r[:, 64:128], qst[:], ['qst'], rres)
                defq.append(lambda: transposes(qst, lambda c0, n: qaT[:, gi * 4 + c0:gi * 4 + c0 + n, t * 128:(t + 1) * 128], 4, ['qst'], ['qaT']))
            gemm_tok(G_qa, hT, 'hT', ntile, ev_qa)

            def ev_qi(gi, t, b, bres, ncols):
                r, rres = ropeload_g(tile0 + t)
                rope(b[:, 0:512], bres, 8, 64, r[:, 256:288], r[:, 288:320], qst[:], ['qst'], rres)
                defq.append(lambda: transposes(qst, lambda c0, n: qiT[:, gi * 4 + c0:gi * 4 + c0 + n, t * 128:(t + 1) * 128], 4, ['qst'], ['qiT']))
            gemm_tok(G_qi, hT, 'hT', ntile, ev_qi)
            NT = ntile * 128
            DMA('pool', hTs[:, :, 0:NT], hT[:, :, 0:NT], ['hT'], ['hTs'], 'hTw')

        def attention_tile(t, key_segs, mtype, use_mask0, dbg=False):
            sgs = []
            pos = 0
            for (o, n) in key_segs:
                for a in range(0, n, 512):
                    w = min(512, n - a)
                    sgs.append((o + a, w, pos)); pos += w
            L = pos
            tsl = slice(t * 128, (t + 1) * 128)
            for si, (so, w, p0) in enumerate(sgs):
                sl = si % 2
                DMA('sp', kis[sl][:, :, 0:w], kiTs[:, :, so:so + w].rearrange("v p t -> p v t"), ['scr0', 'scr1'], ['kis%d' % sl], ('kis', sl))
                for j in range(16):
                    hf = j % 2
                    b = rot[j % 4]; bres = 'rot%d' % (j % 4)
                    MM(b[:, 0:w], qiT[:, j // 2, tsl], kis[sl][:, hf, 0:w], True, True, ['qiT', 'kis%d' % sl], [bres])
                    rb = rbuf[j % 4]; rres = 'rb%d' % (j % 4)
                    ACT(rb[:, 0:w], b[:, 0:w], AF.Relu, [bres, 'wab'], [rres], scale=wab[:, t, j:j + 1])
                    if j == 0:
                        TS('dve', score[:, p0:p0 + w], rb[:, 0:w], wsg[:, t, 0:1], None, ALU.mult, None, [rres, 'wsg'], ['score'])
                    else:
                        STT('dve', score[:, p0:p0 + w], rb[:, 0:w], wsg[:, t, j:j + 1], score[:, p0:p0 + w], ALU.mult, ALU.add,
                            [rres, 'wsg', 'score'], ['score'])
            if _STAGE == 5.1:
                return
            lo, hi, rng, mid, cnt, tmp = (stat[:, 40:41], stat[:, 41:42], stat[:, 42:43], stat[:, 43:44], stat[:, 44:45], stat[:, 45:46])
            S.op('dve', lambda e: e.tensor_reduce(out=lo, in_=score[:, 0:L], axis=AX.X, op=ALU.min), ['score'], ['lo'])
            S.op('dve', lambda e: e.tensor_reduce(out=hi, in_=score[:, 0:L], axis=AX.X, op=ALU.max), ['score'], ['hi'])
            TT('dve', rng, hi, lo, ALU.subtract, ['lo', 'hi'], ['rng'])
            TS('dve', stp[:], pow2[:], rng, None, ALU.mult, None, ['rng', 'pow2'], ['stp'])
            TS('dve', stp2[:], pow2[:], rng, 2.0, ALU.mult, ALU.mult, ['rng', 'pow2'], ['stp'])
            TS('dve', stp2[:, 0:1], stp[:, NBIS - 1:NBIS], 1.125, None, ALU.mult, None, ['stp'], ['stp'])
            TT('dve', score[:, L - 128:L], score[:, L - 128:L], maskD[:, mtype * 128:(mtype + 1) * 128], ALU.add,
               ['score', 'maskD'], ['score'])
            if use_mask0:
                TT('dve', score[:, 0:512], score[:, 0:512], mask0[:], ALU.add, ['score', 'mask0'], ['score'])
            TT('dve', mid, lo, stp[:, 0:1], ALU.add, ['lo', 'stp'], ['mid'])
            Ld = max(128, int(round(L * 0.47 / 128.0)) * 128)
            if L - Ld < 256:
                Ld = L
            nact = L - Ld
            thr = 256.0 - 0.5 * nact
            sacc, t2 = stat[:, 46:47], stat[:, 47:48]
            for k in range(NBIS):
                TS('dve', mmask[:, 0:Ld], score[:, 0:Ld], mid, None, ALU.is_ge, ALU.add, ['score', 'mid'], ['mmask', 'cnt'], accum=cnt)
                if nact:
                    ACT(mmask[:, Ld:L], score[:, Ld:L], AF.Sign, ['score', 'mid'], ['mmaskA', 'sacc'], scale=-1.0, bias=mid, accum=sacc)
                    STT('dve', t2, sacc, -0.5, cnt, ALU.mult, ALU.add, ['sacc', 'cnt'], ['t2'])
                    csrc, cres = t2, 't2'
                else:
                    csrc, cres = cnt, 'cnt'
                if k + 1 < NBIS:
                    STT('dve', tmp, csrc, thr, stp2[:, k + 1:k + 2], ALU.is_ge, ALU.mult, [cres, 'stp'], ['tmp'])
                    STT('dve', mid, tmp, stp[:, k + 1:k + 2], mid, ALU.subtract, ALU.add, ['tmp', 'stp', 'mid'], ['mid'])
                else:
                    STT('dve', tmp, csrc, thr, stp2[:, 0:1], ALU.is_ge, ALU.mult, [cres, 'stp'], ['tmp'])
                    STT('dve', lo, tmp, stp2[:, 0:1], mid, ALU.subtract, ALU.add, ['tmp', 'stp', 'mid'], ['lo'])
            TS('dve', mmask[:, 0:L], score[:, 0:L], lo, -1.0, ALU.is_ge, ALU.add, ['score', 'lo'], ['mmask', 'mmaskA'])
            if _STAGE == 5.2:
                return
            for si, (so, w, p0) in enumerate(sgs):
                sl = si % 2
                nch = w // 128
                DMA('sp', kvk[sl][:, :, 0:w], kTs[:, :, so:so + w].rearrange("g p t -> p g t"), ['scr0', 'scr1'],
                    ['kvk%d' % sl], ('kvk', sl))
                DMA('sp', kvv[sl][:, 0:nch, :, :], Vs[so:so + w, :, :].rearrange("(c p) g e -> p c g e", p=128),
                    ['scr0', 'scr1'], ['kvv%d' % sl], ('kvv', sl))
                items = [(g, c) for g in range(4) for c in range(nch)]

                def emit_S(ix):
                    g, c = items[ix]
                    b = sbanks[ix % 4]; bres = sbres[ix % 4]
                    MM(b[:, 0:512], kvk[sl][:, g, c * 128:(c + 1) * 128], qaT[:, 4 * g:4 * g + 4, tsl], True, False,
                       ['kvk%d' % sl, 'qaT'], [bres])
                    MM(b[:, 0:512], mmask[:, p0 + c * 128:p0 + (c + 1) * 128], i4big[:], False, True, ['mmask', 'i4big'], [bres])
                emit_S(0)
                if len(items) > 1:
                    emit_S(1)
                for ix, (g, c) in enumerate(items):
                    b = sbanks[ix % 4]; bres = sbres[ix % 4]
                    pb = PTb[ix % 4]; pres = 'PT%d' % (ix % 4)
                    ACT(pb, b[:, 0:512], AF.Exp, [bres], [pres], scale=float(128 ** -0.5))
                    if ix + 2 < len(items):
                        emit_S(ix + 2)
                    aset = (agc[0] % 2) * 2
                    for hh in range(4):
                        a = accl[aset + hh // 2]; ares = accr[aset + hh // 2]
                        MM(a[:, (hh % 2) * 256:(hh % 2) * 256 + 129], pb[:, hh * 128:(hh + 1) * 128], kvv[sl][:, c, g, 0:129],
                           (c == 0 and hh % 2 == 0), c == nch - 1, [pres, 'kvv%d' % sl], [ares], sgc=True)
                    if c == nch - 1:
                        agc[0] += 1
                        for hp in range(2):
                            a = accl[aset + hp]; ares = accr[aset + hp]
                            dst = oacc[:, 4 * g + 2 * hp:4 * g + 2 * hp + 2, :]
                            srcv = a[:, 0:512].rearrange("p (h e) -> p h e", h=2)[:, :, 0:129]
                            if si == 0:
                                S.op('dve', lambda e, dst=dst, srcv=srcv: e.tensor_copy(out=dst, in_=srcv), [ares], ['oacc'])
                            else:
                                TT('dve', dst, srcv, dst, ALU.add, [ares, 'oacc'], ['oacc'])
            S.op('dve', lambda e: e.reciprocal(out=rcp[:], in_=oacc[:, :, 128]), ['oacc'], ['rcp'])
            TT('dve', oab.rearrange("p (h d) -> p h d", h=16), oacc[:, :, 0:128], rcp[:].unsqueeze(2).to_broadcast([128, 16, 128]),
               ALU.mult, ['oacc', 'rcp'], ['oab'])
            if dbg:
                finals.append(DMA('pool', dbg_s[:, 0:min(L, 1024)], score[:, 0:min(L, 1024)], ['score'], [], 'dbg'))
                finals.append(DMA('pool', dbg_m[:, 0:min(L, 1024)], mmask[:, 0:min(L, 1024)], ['mmask'], [], 'dbg'))
                finals.append(DMA('pool', dbg_oa[:, :], oab, ['oab'], [], 'dbg'))
            transposes(oab, lambda c0, n: oaT[:, c0:c0 + n, tsl], 16, ['oab'], ['oaT'])

        def merge_ffn(ntile, xsrc_fn, yout_fn):
            NT = ntile * 128
            S.fence()
            DMA('sp', hTb[:, :, 0:NT], hTs[:, :, 0:NT], ['hTs'], ['hTb'], 'hTr')
            DMA('sp', obT[:, :, 0:NT], obTs[:, :, 0:NT], ['obTs'], ['obT'], 'obr')

            def ev_ga(gi, nn, b, bres):
                ACT(sA[:, gi * 4 + nn, 0:NT], b[:, 0:NT], AF.Sigmoid, [bres], ['sA'])
            gemm_feat(G_ga, hTb, 'hTb', NT, ev_ga)

            def ev_pa(gi, nn, b, bres):
                TT('dve', sA[:, gi * 4 + nn, 0:NT], b[:, 0:NT], sA[:, gi * 4 + nn, 0:NT], ALU.mult, [bres, 'sA'], ['sA'])
            gemm_feat(G_pa, oaT, 'oaT', NT, ev_pa)

            def ev_gb(gi, nn, b, bres):
                ACT(sB[:, gi * 4 + nn, 0:NT], b[:, 0:NT], AF.Sigmoid, [bres], ['sB'])
            gemm_feat(G_gb, hTb, 'hTb', NT, ev_gb)

            def ev_pb(gi, nn, b, bres):
                TT('dve', sB[:, gi * 4 + nn, 0:NT], b[:, 0:NT], sB[:, gi * 4 + nn, 0:NT], ALU.mult, [bres, 'sB'], ['sB'])
                TT('dve', mgT[:, gi * 4 + nn, 0:NT], sA[:, gi * 4 + nn, 0:NT], sB[:, gi * 4 + nn, 0:NT], ALU.add, ['sA', 'sB'], ['mgT'])
            gemm_feat(G_pb, obT, 'obT', NT, ev_pb)
            for t in range(ntile):
                DMA('sp', x2[:, t, :], xsrc_fn(t), [], ['x2_%d' % t], ('x2l', t))

            def ev_o(gi, t, b, bres, ncols):
                TT('dve', x2[:, t, gi * 512:(gi + 1) * 512], b[:, 0:512], x2[:, t, gi * 512:(gi + 1) * 512], ALU.add,
                   [bres, 'x2_%d' % t], ['x2_%d' % t])
            gemm_tok(G_o, mgT, 'mgT', ntile, ev_o)
            if _STAGE == 6.5:
                return
            S.fence()
            for t in range(ntile):
                norm_to_T(x2[:, t, :], 'x2_%d' % t, n2s, h2T, 'h2T', t, xsb, 'xsb')
            S.fence()
            DMA('sp', nfrep, nfr[:, :], [], ['nfrep'], 'nfl')

            def ev_gu(i, nn, b, bres):
                G = i // 2
                if i % 2 == 0:
                    ACT(sgt[:, nn, 0:NT], b[:, 0:NT], AF.Silu, [bres], ['sgt%d' % nn])
                else:
                    TT('dve', aT[:, G * 4 + nn, 0:NT], b[:, 0:NT], sgt[:, nn, 0:NT], ALU.mult, [bres, 'sgt%d' % nn], ['aT'])
            if _STAGE == 6.6:
                return
            gemm_feat(ffn_order, h2T, 'h2T', NT, ev_gu)
            if _STAGE == 6.7:
                return

            def ev_d(gi, t, b, bres, ncols):
                TT('dve', x2[:, t, gi * 512:(gi + 1) * 512], b[:, 0:512], x2[:, t, gi * 512:(gi + 1) * 512], ALU.add,
                   [bres, 'x2_%d' % t], ['x2_%d' % t])
                if gi == 3 and _STAGE != 6.8:
                    ACT(sgj, x2[:, t, :], AF.Square, ['x2_%d' % t], ['sgj', 'ss'], accum=stat[:, 0:1])
                    rstd_of(stat[:, 0:1], stat[:, 1:2], D, ['ss'], ['rstd'])
                    STT('dve', x2[:, t, :], x2[:, t, :], stat[:, 1:2], nfrep, ALU.mult, ALU.mult, ['x2_%d' % t, 'rstd', 'nfrep'], ['x2_%d' % t])
                    finals.append(DMA('pool', yout_fn(t), x2[:, t, :], ['x2_%d' % t], [], ('yo', t)))
            gemm_tok(G_d, aT, 'aT', ntile, ev_d)
            S.fence()

        stp = sb("stp", [128, NBIS], F32)
        stp2 = sb("stp2", [128, NBIS], F32)
        rcp = sb("rcp", [128, 16], F32)

        S.op('dve', lambda e: e.memset(Sst[:], 0.0), [], ['Sst'])
        S.op('dve', lambda e: e.memset(Sb[:], 0.0), [], ['Sb'])
        for vb in range(NV):
            own = (vb % 2 == 1)
            oi = vb // 2
            oo = None
            if own:
                oo = dict(k=lambda t, oi=oi: k_o[oi * 512 + t * 128:oi * 512 + (t + 1) * 128, :],
                          v=lambda t, oi=oi: v_o[oi * 512 + t * 128:oi * 512 + (t + 1) * 128, :],
                          ik=lambda t, oi=oi: ik_o[oi * 512 + t * 128:oi * 512 + (t + 1) * 128, :])
            xf = lambda t, vb=vb: xv[vb * 512 + t * 128:vb * 512 + (t + 1) * 128, :]
            kside(4, xf, vb * 512, vb * 4, vb * 512, oo, vb % 2)
            if (_STAGE <= 2 and vb >= _STAGEVB):
                S.emit(nc, finals); return nc
            if not own:
                for t in range(4):
                    state_update(t, vb * 4 + t)
                if (_STAGE <= 3 and vb >= _STAGEVB) or _STOPVB == vb:
                    S.emit(nc, finals); return nc
                continue
            qside_ret(4, vb * 4)
            if (_STAGE <= 4 and vb >= _STAGEVB):
                S.emit(nc, finals); return nc
            qside_proj(4, vb * 4)
            if (_STAGE <= 5 and vb >= _STAGEVB):
                S.emit(nc, finals); return nc
            S.fence()
            for t in range(4):
                attention_tile(t, [(0, vb * 512 + (t + 1) * 128)], 0, True, dbg=(_DEBUG == (vb, t)))
            if (_STAGE <= 6 and vb >= _STAGEVB):
                S.emit(nc, finals); return nc
            merge_ffn(4, xf, lambda t, oi=oi: y_o[oi * 512 + t * 128:oi * 512 + (t + 1) * 128, :])
            if (_STAGE <= 7 and vb >= _STAGEVB) or _STOPVB == vb:
                S.emit(nc, finals); return nc
        finals.append(DMA('pool', st_o[:, :, :], Sst[:], ['Sst'], [], 'sto'))
        S.fence()
        DMA('sp', Sst[:], st0[:, :, :], [], ['Sst'], 'stl')
        ACT(Sb[:], Sst[:], AF.Copy, ['Sst'], ['Sb'])
        for ct in range(8):
            tk = SOFF + ct * 128
            DMA('sp', kfst[:], ck[ct * 128:(ct + 1) * 128, :], [], ['kfst'], 'ckl')
            ACT(kbst[:], kfst[:], AF.Copy, ['kfst'], ['kbst'])
            transposes(kbst, lambda c0, n: kTst[:, c0:c0 + n, :], 4, ['kbst'], ['kTst'])
            DMA('pool', kTs[:, :, tk:tk + 128].rearrange("g p t -> p g t"), kTst[:], ['kTst'], ['scr0'], ('scw', 0))
            DMA('sp', vfst[:], cv[ct * 128:(ct + 1) * 128, :], [], ['vfst'], 'cvl')
            ACT(vbst[:, :, 0:128], vfst[:].rearrange("p (g e) -> p g e", g=4), AF.Copy, ['vfst'], ['vbst'])
            DMA('pool', Vs[tk:tk + 128, :, :], vbst[:], ['vbst'], ['scr0'], ('scw', 0))
            DMA('sp', kifst[:], ci[ct * 128:(ct + 1) * 128, :], [], ['kifst'], 'cil')
            ACT(kibst[:], kifst[:], AF.Copy, ['kifst'], ['kibst'])
            tb = trp[trc[0] % 2]; tres = 'trp%d' % (trc[0] % 2); trc[0] += 1
            TR(tb[0:64, 0:128], kibst[:], ['kibst'], [tres])
            ACT(kiTst[0:64, 0, :], tb[0:64, 0:128], AF.Copy, [tres], ['kiTst'])
            ACT(kiTst[64:128, 1, :], tb[0:64, 0:128], AF.Copy, [tres], ['kiTst'])
            DMA('pool', kiTs[:, :, tk:tk + 128].rearrange("v p t -> p v t"), kiTst[:], ['kiTst'], ['scr0'], ('scw', 0))
        soo = dict(k=lambda t: ks_o[:, :], v=lambda t: vs_o[:, :], ik=lambda t: iks_o[:, :])
        xsf = lambda t: xs[:, :]
        kside(1, xsf, SOFF + 1024, NV * 4, NV * 512, soo, 0)
        qside_ret(1, NV * 4)
        qside_proj(1, NV * 4)
        S.fence()
        attention_tile(0, [(SOFF, 1152)], 1, False)
        merge_ffn(1, xsf, lambda t: ys_o[:, :])
        finals.append(DMA('pool', sts_o[:, :, :], Sst[:], ['Sst'], [], 'sto'))
        S.emit(nc, finals)
    return nc


def _tables(NBR, h):
    NV = NBR + 1
    NTIL = NV * 4 + 1
    pos = np.zeros(NV * 512 + 128, np.float64)
    dummy = NBR if h == 1 else 0
    for v in range(NV):
        r = v if h == 1 else v - 1
        if v == dummy:
            r = 0
        pos[v * 512:(v + 1) * 512] = r * 512 + np.arange(512)
    pos[NV * 512:] = 1024 + np.arange(128)
    i64 = np.arange(64, dtype=np.float32) / 64
    i32 = np.arange(32, dtype=np.float32) / 32
    fA = (np.float32(10000.0) ** (-i64)).astype(np.float32)
    fI = (np.float32(10000.0) ** (-i32)).astype(np.float32)
    angA = pos.astype(np.float32)[:, None] * fA[None, :]
    angI = pos.astype(np.float32)[:, None] * fI[None, :]
    sc = np.float32(128 ** -0.5)
    rt = np.concatenate([np.cos(angA), np.sin(angA), np.cos(angA) * sc, np.sin(angA) * sc, np.cos(angI), np.sin(angI)], axis=1)
    lg = np.log1p(-(2.0 ** (-5.0 - np.arange(8, dtype=np.float64))))
    j = np.arange(128, dtype=np.float64)[:, None]
    kdec = np.zeros((128, NTIL, 8), np.float64)
    sdec = np.zeros((128, NTIL, 8), np.float64)
    for tl in range(NTIL):
        if tl == NTIL - 1:
            kdec[:, tl, :] = np.exp((31.0 - j) * lg[None, :]); sdec[:, tl, :] = np.exp(32.0 * lg)[None, :]
        elif tl // 4 == dummy:
            kdec[:, tl, :] = 1.0; sdec[:, tl, :] = 1.0
        else:
            kdec[:, tl, :] = np.exp((127.0 - j) * lg[None, :]); sdec[:, tl, :] = np.exp(128.0 * lg)[None, :]
    kq = np.concatenate([np.exp(-(j + 1.0) * lg[None, :]), np.exp((j + 1.0) * lg[None, :])], axis=1)
    tri = (np.arange(128)[:, None] <= np.arange(128)[None, :]).astype(np.float32)
    q = np.arange(128)[:, None]; s = np.arange(128)[None, :]
    mp = np.where((s // 64) <= (q // 64), 0.0, NEG)
    ms = np.where(s < 32, 0.0, NEG) + 0.0 * q
    maskD = np.concatenate([mp, ms], axis=1)
    mask0 = np.full((128, 512), NEG if h == 0 else 0.0)
    pow2 = np.tile((2.0 ** -(np.arange(NBIS) + 1.0))[None, :], (128, 1))
    f = np.float32
    return dict(rt=rt.astype(f), kdec=kdec.reshape(128, -1).astype(f), sdec=sdec.reshape(128, -1).astype(f), kq=kq.astype(f),
                idf=np.eye(128, dtype=f), tri=tri, maskD=maskD.astype(f), mask0=mask0.astype(f), pow2=pow2.astype(f))


_NC_CACHE = {}


def kernel(x_prompt, x_sample, cache_k, cache_v, cache_idx_k, state_ret, norm1_g, w_in, w_pa, w_pb, w_o, norm2_g,
           w_ffn_gate, w_ffn_up, w_ffn_down, norm_f_g):
    f = np.float32
    x_prompt = np.asarray(x_prompt, f); x_sample = np.asarray(x_sample, f)
    B, T, _ = x_prompt.shape
    DB, DS, _ = x_sample.shape
    NBR = T // 512
    NV = NBR + 1
    NOWN = NBR // 2
    ncore = 2 * B
    assert DB == ncore and DS == 32
    if NBR not in _NC_CACHE:
        _NC_CACHE[NBR] = build(NBR)
    nc = _NC_CACHE[NBR]
    shared = dict(
        w_in=np.ascontiguousarray(np.asarray(w_in, f)[0]), w_pa=np.ascontiguousarray(np.asarray(w_pa, f)[0]),
        w_pb=np.ascontiguousarray(np.asarray(w_pb, f)[0]), w_o=np.ascontiguousarray(np.asarray(w_o, f)[0]),
        w_g=np.ascontiguousarray(np.asarray(w_ffn_gate, f)[0]), w_u=np.ascontiguousarray(np.asarray(w_ffn_up, f)[0]),
        w_d=np.ascontiguousarray(np.asarray(w_ffn_down, f)[0]),
        n1t=np.ascontiguousarray(np.asarray(norm1_g, f)[0].reshape(16, 128).T),
        n2t=np.ascontiguousarray(np.asarray(norm2_g, f)[0].reshape(16, 128).T),
        nfr=np.ascontiguousarray(np.broadcast_to(np.asarray(norm_f_g, f)[None, :], (128, D))),
    )
    tabs = [_tables(NBR, 0), _tables(NBR, 1)]
    in_maps = []
    for c in range(ncore):
        b, h = c // 2, c % 2
        xvv = np.zeros((NV * 512, D), f)
        if h == 1:
            xvv[0:NBR * 512] = x_prompt[b]
        else:
            xvv[512:] = x_prompt[b]
        xsv = np.zeros((128, D), f); xsv[0:32] = x_sample[c]
        m = dict(shared)
        m.update(tabs[h])
        m.update(xv=xvv, xs=xsv,
                 ck=np.ascontiguousarray(np.asarray(cache_k, f)[0, c].reshape(1024, 512)),
                 cv=np.ascontiguousarray(np.asarray(cache_v, f)[0, c].reshape(1024, 512)),
                 ci=np.ascontiguousarray(np.asarray(cache_idx_k, f)[0, c]),
                 st0=np.ascontiguousarray(np.asarray(state_ret, f)[0, c].transpose(1, 0, 2)))
        in_maps.append(m)
    res = run_bass_kernel_spmd(nc, in_maps, core_ids=list(range(ncore)))
    R = res.results
    if _DEBUG is not None:
        kernel.dbg = [dict(s=r['dbg_s'], m=r['dbg_m'], oa=r['dbg_oa'], ob=r['dbg_ob']) for r in R]
    y_p = np.zeros((B, T, D), f); k_p = np.zeros((1, B, T, 4, 128), f); v_p = np.zeros((1, B, T, 4, 128), f)
    i_p = np.zeros((1, B, T, 64), f); s_p = np.zeros((1, B, 8, 128, 256), f)
    y_s = np.zeros((DB, DS, D), f); k_s = np.zeros((1, DB, DS, 4, 128), f); v_s = np.zeros((1, DB, DS, 4, 128), f)
    i_s = np.zeros((1, DB, DS, 64), f); s_s = np.zeros((1, DB, 8, 128, 256), f)
    for c in range(ncore):
        b, h = c // 2, c % 2
        r = R[c]
        for i in range(NOWN):
            rb = 2 * i + h
            sl = slice(rb * 512, (rb + 1) * 512)
            y_p[b, sl] = r["y_o"][i * 512:(i + 1) * 512]
            k_p[0, b, sl] = r["k_o"][i * 512:(i + 1) * 512].reshape(512, 4, 128)
            v_p[0, b, sl] = r["v_o"][i * 512:(i + 1) * 512].reshape(512, 4, 128)
            i_p[0, b, sl] = r["ik_o"][i * 512:(i + 1) * 512]
        if h == 0:
            s_p[0, b] = r["st_o"].transpose(1, 0, 2)
        y_s[c] = r["ys_o"][0:32]
        k_s[0, c] = r["ks_o"][0:32].reshape(32, 4, 128)
        v_s[0, c] = r["vs_o"][0:32].reshape(32, 4, 128)
        i_s[0, c] = r["iks_o"][0:32]
        s_s[0, c] = r["sts_o"].transpose(1, 0, 2)
    return (y_p, y_s, k_p, v_p, i_p, s_p, k_s, v_s, i_s, s_s)
```
